# Optimizing a Trainium2 kernel written in Bass

```python
import math
import jax
import jax.numpy as jnp
from jax import lax
import numpy as np

D_MODEL = 1024
BATCH = 8
SEQ = 2048
DEPTH = 2

GRID_W = 64
CTX_LEN = 256
MIX_WIDTH = D_MODEL
N_BRANCH = 3
HEAD_DIM = 128
ATT_HEADS = MIX_WIDTH // HEAD_DIM
ATT_KV_HEADS = ATT_HEADS // 4
ATT_WIDTH = ATT_HEADS * HEAD_DIM
KV_WIDTH = ATT_KV_HEADS * HEAD_DIM
Q_BLOCK = 128
ROPE_THETA = 10000.0
HG_KDIM = 128
HG_HEADS = MIX_WIDTH // HG_KDIM
HG_VDIM = MIX_WIDTH // HG_HEADS
HG_KWIDTH = HG_HEADS * HG_KDIM
HG_VWIDTH = HG_HEADS * HG_VDIM
HG_CHUNK = 64
HG_MIN_F = 1e-6
HY_WIDTH = MIX_WIDTH
HY_ORDER = 2
HY_SHORT = 3
HY_BANDS = 16
HY_EMB = 1 + 2 * HY_BANDS
HY_FILTER_HIDDEN = 64
HY_FAST_DECAY = 0.3
HY_SLOW_DECAY = 1.5
HY_TARGET = 1e-2
HY_MIN_DECAY = math.log(HY_TARGET) / HY_SLOW_DECAY
HY_MAX_DECAY = math.log(HY_TARGET) / HY_FAST_DECAY
D_FF = 4 * D_MODEL
EPS = 1e-6
F32 = jnp.float32
IN_SIZES = (ATT_WIDTH, KV_WIDTH, KV_WIDTH,
            HG_KWIDTH, HG_KWIDTH, HG_KWIDTH, HG_VWIDTH, HG_VWIDTH,
            (HY_ORDER + 1) * HY_WIDTH,
            N_BRANCH * D_MODEL)
IN_TOTAL = (ATT_WIDTH + 2 * KV_WIDTH + 3 * HG_KWIDTH + 2 * HG_VWIDTH
            + (HY_ORDER + 1) * HY_WIDTH + N_BRANCH * D_MODEL)

kernel_name = "hybrid_flow_backbone"


def rms_norm(x, g):
    xf = x.astype(F32)
    y = xf * lax.rsqrt(jnp.mean(xf * xf, axis=-1, keepdims=True) + EPS)
    return (y * g.astype(F32)).astype(x.dtype)


def adaln_modulation(cvec, w, b):
    m = jnp.einsum('...d,de->...e', jax.nn.silu(cvec), w) + b
    return jnp.split(m, 6, axis=-1)


def split_in(z):
    out, start = [], 0
    for size in IN_SIZES:
        out.append(z[..., start:start + size])
        start += size
    return out


def heads(t, n):
    b, l, _ = t.shape
    return t.reshape(b, l, n, -1).transpose(0, 2, 1, 3)


def axial_rope_tables(row, col):
    half = HEAD_DIM // 2
    inv = ROPE_THETA ** (-jnp.arange(0, half, 2, dtype=F32) / half)
    ar = row.astype(F32)[:, None] * inv
    ac = col.astype(F32)[:, None] * inv
    ang = jnp.concatenate([ar, ar, ac, ac], axis=-1)
    return jnp.cos(ang), jnp.sin(ang)


def apply_axial_rope(x, cos, sin):
    def rot_half(u):
        u1, u2 = jnp.split(u, 2, axis=-1)
        return jnp.concatenate([-u2, u1], axis=-1)
    xr, xc = jnp.split(x, 2, axis=-1)
    rx = jnp.concatenate([rot_half(xr), rot_half(xc)], axis=-1)
    return (x * cos + rx * sin).astype(x.dtype)


def attend(qb, keys, vals):
    s = jnp.einsum('bkgqd,bksd->bkgqs', qb, keys).astype(F32) * (HEAD_DIM ** -0.5)
    p = jax.nn.softmax(s, axis=-1).astype(vals.dtype)
    return jnp.einsum('bkgqs,bksd->bkgqd', p, vals)


def attention_mixer(q_l, k_l, v_l, q_c, k_c, v_c, gq, gk, cos, sin, need_ctx):
    b, n_lat, _ = q_l.shape
    grp = ATT_HEADS // ATT_KV_HEADS
    ql = apply_axial_rope(rms_norm(heads(q_l, ATT_HEADS), gq), cos, sin)
    kl = apply_axial_rope(rms_norm(heads(k_l, ATT_KV_HEADS), gk), cos, sin)
    kc = rms_norm(heads(k_c, ATT_KV_HEADS), gk)
    vl = heads(v_l, ATT_KV_HEADS)
    vc = heads(v_c, ATT_KV_HEADS)
    keys = jnp.concatenate([kl, kc], axis=2)
    vals = jnp.concatenate([vl, vc], axis=2)
    nblk = n_lat // Q_BLOCK
    qb = ql.reshape(b, ATT_KV_HEADS, grp, nblk, Q_BLOCK, HEAD_DIM).transpose(3, 0, 1, 2, 4, 5)
    ol = lax.map(lambda blk: attend(blk, keys, vals), qb)
    out_l = ol.transpose(1, 0, 4, 2, 3, 5).reshape(b, n_lat, ATT_WIDTH)
    out_c = None
    if need_ctx:
        n_ctx = q_c.shape[1]
        qc = rms_norm(heads(q_c, ATT_HEADS), gq).reshape(b, ATT_KV_HEADS, grp, n_ctx, HEAD_DIM)
        out_c = attend(qc, kc, vc).transpose(0, 3, 1, 2, 4).reshape(b, n_ctx, ATT_WIDTH)
    return out_l, out_c


def hgrn2_chunk_scan(q, k, v, log_f, s0):
    b, h, n, kd = q.shape
    vd = v.shape[-1]
    nc = n // HG_CHUNK
    resh = lambda a: a.reshape(b, h, nc, HG_CHUNK, a.shape[-1]).transpose(2, 0, 1, 3, 4)
    tri = jnp.tril(jnp.ones((HG_CHUNK, HG_CHUNK), dtype=bool))[:, :, None]

    def step(state, inp):
        qb, kb, vb, fb = inp
        cum = jnp.cumsum(fb, axis=2)
        o_inter = jnp.einsum('bhck,bhkv->bhcv', qb * jnp.exp(cum), state)
        diff = cum[:, :, :, None, :] - cum[:, :, None, :, :]
        decay = jnp.where(tri, jnp.exp(jnp.where(tri, diff, 0.0)), 0.0)
        scores = jnp.einsum('bhtk,bhsk,bhtsk->bhts', qb, kb, decay)
        o = o_inter + jnp.einsum('bhts,bhsv->bhtv', scores, vb)
        last = cum[:, :, -1:, :]
        new_state = state * jnp.exp(last[:, :, 0, :, None]) + jnp.einsum(
            'bhsk,bhsv->bhkv', kb * jnp.exp(last - cum), vb)
        return new_state, o

    s_fin, oc = lax.scan(step, s0, (resh(q), resh(k), resh(v), resh(log_f)))
    return oc.transpose(1, 2, 0, 3, 4).reshape(b, h, n, vd), s_fin


def hgrn2_direction(q, i, z, lb, s0):
    f = lb + (1.0 - lb) * jax.nn.sigmoid(z)
    log_f = jnp.log(jnp.maximum(f, HG_MIN_F))
    k = (1.0 - lb) * jax.nn.sigmoid(-z)
    return hgrn2_chunk_scan(q, k, i, log_f, s0)


def hgrn2_readout(o, g, g_norm):
    b, h, n, vd = o.shape
    o = rms_norm(o, g_norm).transpose(0, 2, 1, 3).reshape(b, n, h * vd)
    return (o * jax.nn.silu(g.astype(F32))).astype(g.dtype)


def hgrn2_mixer(q_l, zf_l, zb_l, i_l, g_l, q_c, zf_c, zb_c, i_c, g_c, lb, g_norm, need_ctx):
    to_heads = lambda t: heads(t, HG_HEADS).astype(F32)
    lb_f = lb[0].reshape(HG_HEADS, 1, HG_KDIM)
    lb_b = lb[1].reshape(HG_HEADS, 1, HG_KDIM)
    rev = lambda t: jnp.flip(t, axis=2)
    s0 = jnp.zeros((q_l.shape[0], HG_HEADS, HG_KDIM, HG_VDIM), F32)
    qC, fC, bC, iC = [to_heads(t) for t in (q_c, zf_c, zb_c, i_c)]
    qL, fL, bL, iL = [to_heads(t) for t in (q_l, zf_l, zb_l, i_l)]
    o_cf, s_cf = hgrn2_direction(qC, iC, fC, lb_f, s0)
    o_cb, s_cb = hgrn2_direction(rev(qC), rev(iC), rev(bC), lb_b, s0)
    o_lf, _ = hgrn2_direction(qL, iL, fL, lb_f, s_cf)
    o_lb, _ = hgrn2_direction(rev(qL), rev(iL), rev(bL), lb_b, s_cb)
    out_l = hgrn2_readout(o_lf + rev(o_lb), g_l, g_norm)
    out_c = hgrn2_readout(o_cf + rev(o_cb), g_c, g_norm) if need_ctx else None
    return out_l, out_c


def hyena_filters(n, w1, b1, w2, b2, w3, freq):
    t = jnp.linspace(0.0, 1.0, n, dtype=F32)[:, None]
    w = (2.0 * math.pi / n) * jnp.arange(n, dtype=F32)[:, None]
    f = jnp.linspace(1e-4, HY_BANDS - 1, HY_BANDS, dtype=F32)[None, :]
    z = jnp.concatenate([t, jnp.cos(f * w), -jnp.sin(f * w)], axis=-1)
    h = jnp.sin(freq[0] * (z @ w1 + b1))
    h = jnp.sin(freq[1] * (h @ w2 + b2))
    h = (h @ w3).astype(F32).reshape(n, HY_ORDER, 2, HY_WIDTH)
    deltas = jnp.abs(jnp.linspace(HY_MIN_DECAY, HY_MAX_DECAY, HY_WIDTH, dtype=F32))
    h = h * jnp.exp(-t * deltas)[:, None, None, :]
    fwd, bwd = h[:, :, 0], h[:, :, 1]
    filt2 = jnp.concatenate([fwd, jnp.zeros_like(fwd[:1]), bwd[:0:-1]], axis=0)
    return filt2 / (jnp.sum(jnp.abs(filt2), axis=0, keepdims=True) + EPS)


def short_conv(u, w, b):
    n = u.shape[1]
    up = jnp.pad(u, ((0, 0), (1, 1), (0, 0)))
    return up[:, :n] * w[0] + up[:, 1:n + 1] * w[1] + up[:, 2:] * w[2] + b


def hyena_mixer(u, sw, sb, filt2, bias):
    u = short_conv(u, sw, sb).astype(F32)
    x1, x2, v = jnp.split(u, 3, axis=-1)
    n = u.shape[1]
    filt_f = jnp.fft.rfft(filt2, axis=0)
    z = v
    for o, gate in enumerate((x1, x2)):
        zf = jnp.fft.rfft(z, n=2 * n, axis=1)
        zc = jnp.fft.irfft(zf * filt_f[None, :, o], n=2 * n, axis=1)[:, :n]
        z = gate * (zc + z * bias[o].astype(F32))
    return z


def merge_branches(att, hg, hy, gates, wb):
    ga, gh, gy = jnp.split(jax.nn.sigmoid(gates.astype(F32)).astype(gates.dtype), N_BRANCH, axis=-1)
    return ga * (att @ wb[0]) + gh * (hg @ wb[1]) + gy * (hy @ wb[2])


def sq_relu_mlp(h, w1, w2):
    return jnp.square(jax.nn.relu(h @ w1)) @ w2


def setup_inputs(seed: int = 0) -> dict:
    key = jax.random.key(seed)
    ks = jax.random.split(key, 32)
    nrm = lambda k, shape, s: jax.random.normal(k, shape, F32) * s
    return {
        'x': nrm(ks[0], (BATCH, SEQ, D_MODEL), 1.0),
        'c': nrm(ks[1], (BATCH, D_MODEL), 1.0),
        'ctx': nrm(ks[2], (BATCH, CTX_LEN, D_MODEL), 1.0),
        'c_ctx': nrm(ks[3], (D_MODEL,), 1.0),
        'ada_w': nrm(ks[4], (DEPTH, D_MODEL, 6 * D_MODEL), 0.5 * D_MODEL ** -0.5),
        'ada_b': nrm(ks[5], (DEPTH, 6 * D_MODEL), 0.02),
        'norm1_g': 1.0 + nrm(ks[6], (DEPTH, D_MODEL), 0.02),
        'norm2_g': 1.0 + nrm(ks[7], (DEPTH, D_MODEL), 0.02),
        'w_in': nrm(ks[8], (DEPTH, D_MODEL, IN_TOTAL), D_MODEL ** -0.5),
        'q_norm_g': 1.0 + nrm(ks[9], (DEPTH, HEAD_DIM), 0.02),
        'k_norm_g': 1.0 + nrm(ks[10], (DEPTH, HEAD_DIM), 0.02),
        'hg_lb_raw': nrm(ks[11], (DEPTH, 2, HG_KWIDTH), 0.1),
        'hg_norm_g': 1.0 + nrm(ks[12], (DEPTH, HG_VDIM), 0.02),
        'hy_short_w': nrm(ks[13], (DEPTH, HY_SHORT, (HY_ORDER + 1) * HY_WIDTH), HY_SHORT ** -0.5),
        'hy_short_b': nrm(ks[14], (DEPTH, (HY_ORDER + 1) * HY_WIDTH), 0.02),
        'hy_filt_w1': nrm(ks[15], (DEPTH, HY_EMB, HY_FILTER_HIDDEN), HY_EMB ** -0.5),
        'hy_filt_b1': nrm(ks[16], (DEPTH, HY_FILTER_HIDDEN), 0.02),
        'hy_filt_w2': nrm(ks[17], (DEPTH, HY_FILTER_HIDDEN, HY_FILTER_HIDDEN), HY_FILTER_HIDDEN ** -0.5),
        'hy_filt_b2': nrm(ks[18], (DEPTH, HY_FILTER_HIDDEN), 0.02),
        'hy_filt_w3': nrm(ks[19], (DEPTH, HY_FILTER_HIDDEN, HY_ORDER * 2 * HY_WIDTH), HY_FILTER_HIDDEN ** -0.5),
        'hy_freq': 1.0 + nrm(ks[20], (DEPTH, 2, HY_FILTER_HIDDEN), 0.02),
        'hy_bias': nrm(ks[21], (DEPTH, HY_ORDER, HY_WIDTH), 0.1),
        'w_branch': nrm(ks[22], (DEPTH, N_BRANCH, MIX_WIDTH, D_MODEL), MIX_WIDTH ** -0.5),
        'w_out': nrm(ks[23], (DEPTH, D_MODEL, D_MODEL), D_MODEL ** -0.5),
        'w_mlp1': nrm(ks[24], (DEPTH, D_MODEL, D_FF), D_MODEL ** -0.5),
        'w_mlp2': nrm(ks[25], (DEPTH, D_FF, D_MODEL), D_FF ** -0.5),
    }


def reference(x, c, ctx, c_ctx, ada_w, ada_b, norm1_g, norm2_g, w_in, q_norm_g, k_norm_g,
              hg_lb_raw, hg_norm_g, hy_short_w, hy_short_b, hy_filt_w1, hy_filt_b1,
              hy_filt_w2, hy_filt_b2, hy_filt_w3, hy_freq, hy_bias, w_branch, w_out,
              w_mlp1, w_mlp2):
    n_lat = x.shape[1]
    n_ctx = ctx.shape[1]
    rows = n_lat // GRID_W
    row = jnp.repeat(jnp.arange(rows, dtype=jnp.int32), GRID_W)
    col = jnp.tile(jnp.arange(GRID_W, dtype=jnp.int32), rows)
    cos, sin = axial_rope_tables(row, col)
    sm = jax.nn.softmax(hg_lb_raw.astype(F32), axis=0)
    lower = jnp.cumsum(sm, axis=0) - sm[0:1]

    xl, xc = x, ctx
    for l in range(DEPTH):
        need_ctx = l < DEPTH - 1
        sh1, sc1, ga1, sh2, sc2, ga2 = [m[:, None, :] for m in adaln_modulation(c, ada_w[l], ada_b[l])]
        csh1, csc1, cga1, csh2, csc2, cga2 = adaln_modulation(c_ctx, ada_w[l], ada_b[l])

        hl = rms_norm(xl, norm1_g[l]) * (1.0 + sc1) + sh1
        hc = rms_norm(xc, norm1_g[l]) * (1.0 + csc1) + csh1
        ql, kl, vl, hql, hfl, hbl, hil, hgl, hyl, gtl = split_in(hl @ w_in[l])
        qc, kc, vc, hqc, hfc, hbc, hic, hgc, hyc, gtc = split_in(hc @ w_in[l])

        att_l, att_c = attention_mixer(ql, kl, vl, qc, kc, vc, q_norm_g[l], k_norm_g[l], cos, sin, need_ctx)
        hg_l, hg_c = hgrn2_mixer(hql, hfl, hbl, hil, hgl, hqc, hfc, hbc, hic, hgc,
                                 lower[l], hg_norm_g[l], need_ctx)
        filt_l = hyena_filters(n_lat, hy_filt_w1[l], hy_filt_b1[l], hy_filt_w2[l], hy_filt_b2[l],
                               hy_filt_w3[l], hy_freq[l])
        hy_l = hyena_mixer(hyl, hy_short_w[l], hy_short_b[l], filt_l, hy_bias[l]).astype(xl.dtype)
        xl = xl + ga1 * (merge_branches(att_l, hg_l, hy_l, gtl, w_branch[l]) @ w_out[l])
        if need_ctx:
            filt_c = hyena_filters(n_ctx, hy_filt_w1[l], hy_filt_b1[l], hy_filt_w2[l], hy_filt_b2[l],
                                   hy_filt_w3[l], hy_freq[l])
            hy_c = hyena_mixer(hyc, hy_short_w[l], hy_short_b[l], filt_c, hy_bias[l]).astype(xc.dtype)
            xc = xc + cga1 * (merge_branches(att_c, hg_c, hy_c, gtc, w_branch[l]) @ w_out[l])

        xl = xl + ga2 * sq_relu_mlp(rms_norm(xl, norm2_g[l]) * (1.0 + sc2) + sh2, w_mlp1[l], w_mlp2[l])
        if need_ctx:
            xc = xc + cga2 * sq_relu_mlp(rms_norm(xc, norm2_g[l]) * (1.0 + csc2) + csh2,
                                         w_mlp1[l], w_mlp2[l])
    return xl
```

```python
import contextlib
import math
import numpy as np
import ml_dtypes
import concourse.bass as bass
import concourse.mybir as mybir
from concourse.bass_utils import run_bass_kernel_spmd

F32 = mybir.dt.float32
BF16 = mybir.dt.bfloat16
AF = mybir.ActivationFunctionType
ALU = mybir.AluOpType
AX = mybir.AxisListType

ENGS = ("tensor", "vector", "scalar", "gpsimd", "sync")
N_DMA_SEMS = 24
SAME_ENGINE_SYNC = True

D = 1024
KC = 8
NLAT = 2048
NCTX = 256
TOK = NLAT + NCTX
NTILE = TOK // 128
DEPTH = 2
DFF = 4096
IN_TOTAL = 12800
OFF_Q, OFF_K, OFF_V = 0, 1024, 1280
OFF_HQ, OFF_HF, OFF_HB, OFF_HI, OFF_HG = 1536, 2560, 3584, 4608, 5632
OFF_HY = 6656
OFF_GATE = 9728
EPS = 1e-6
CH = [(0, 256), (256, 512), (768, 512), (1280, 512), (1792, 512)]
SB_BASE = 16512
SB_LIMIT = 229376


class Op:
    __slots__ = ("eng", "fn", "deps", "dma", "sem", "val", "needed", "idx")

    def __init__(self, eng, fn, deps, dma):
        self.eng, self.fn, self.deps, self.dma = eng, fn, deps, dma
        self.sem = None
        self.val = None
        self.needed = False


class Prog:
    def __init__(self, nc):
        self.nc = nc
        self.q = {e: [] for e in ENGS}
        self.last_write = {}
        self.readers = {}
        self.dma_count = 0
        self.dma_hist = [[], []]
        self.final_ops = []

    def op(self, eng, fn, reads=(), writes=(), dma=False, final=False):
        reads = list(reads) + ["ALL"]
        deps = set()
        for r in reads:
            w = self.last_write.get(r)
            if w is not None:
                deps.add(w)
        for wkey in writes:
            w = self.last_write.get(wkey)
            if w is not None:
                deps.add(w)
            for o in self.readers.get(wkey, {}).values():
                deps.add(o)
        o = Op(eng, fn, deps, dma)
        if dma:
            pool = 1 if eng == "gpsimd" else 0
            hist = self.dma_hist[pool]
            n = len(hist)
            o.sem = pool * N_DMA_SEMS + n % N_DMA_SEMS
            o.val = 16 * (n // N_DMA_SEMS + 1)
            if n >= N_DMA_SEMS:
                deps.add(hist[n - N_DMA_SEMS])
            hist.append(o)
        o.idx = len(self.q[eng])
        self.q[eng].append(o)
        rk = ("dma", id(o)) if dma else eng
        for r in reads:
            self.readers.setdefault(r, {})[rk] = o
        for wkey in writes:
            self.last_write[wkey] = o
            self.readers[wkey] = {}
        if final:
            self.final_ops.append(o)
        return o

    def barrier(self):
        self.op("sync", lambda e: e.nop(), reads=(), writes=["ALL"])

    def dma(self, out, in_, reads=(), writes=(), eng="sync", final=False, **kw):
        return self.op(eng, lambda e: e.dma_start(out=out, in_=in_, **kw), reads, writes, dma=True, final=final)

    def mm(self, out, lhsT, rhs, start=True, stop=True, reads=(), writes=()):
        return self.op("tensor", lambda e: e.matmul(out, lhsT, rhs, start=start, stop=stop), reads, writes)

    def tr(self, out, in_, ident, reads=(), writes=()):
        return self.op("tensor", lambda e: e.transpose(out, in_, ident), reads, writes)

    def act(self, out, in_, func, reads=(), writes=(), eng="scalar", **kw):
        return self.op(eng, lambda e: e.activation(out, in_, func, **kw), reads, writes)

    def tt(self, out, in0, in1, op, reads=(), writes=(), eng="vector"):
        return self.op(eng, lambda e: e.tensor_tensor(out, in0, in1, op), reads, writes)

    def ts(self, out, in0, s1, s2, op0, op1=None, reads=(), writes=(), eng="vector", **kw):
        if op1 is None:
            return self.op(eng, lambda e: e.tensor_scalar(out, in0, s1, s2, op0, **kw), reads, writes)
        return self.op(eng, lambda e: e.tensor_scalar(out, in0, s1, s2, op0, op1, **kw), reads, writes)

    def stt(self, out, in0, scalar, in1, op0, op1, reads=(), writes=(), eng="vector"):
        return self.op(eng, lambda e: e.scalar_tensor_tensor(out, in0, scalar, in1, op0, op1), reads, writes)

    def copy(self, out, in_, reads=(), writes=(), eng="vector"):
        if eng == "scalar":
            return self.op(eng, lambda e: e.copy(out, in_), reads, writes)
        return self.op(eng, lambda e: e.tensor_copy(out, in_), reads, writes)

    def recip(self, out, in_, reads=(), writes=()):
        return self.op("vector", lambda e: e.reciprocal(out, in_), reads, writes)

    def memset(self, ap, val, writes=(), eng="vector"):
        return self.op(eng, lambda e: e.memset(ap, val), (), writes)

    def emit(self, sems_eng, sems_dma):
        nc = self.nc
        for e in ENGS:
            for o in self.q[e]:
                for d in o.deps:
                    d.needed = True
        for o in self.final_ops:
            o.needed = True
        for e in ENGS:
            cnt = 0
            for o in self.q[e]:
                if o.dma:
                    o.sem = sems_dma[o.sem]
                    continue
                o.sem = sems_eng[e]
                if o.needed:
                    cnt += 1
                    o.val = cnt
        final_ops = self.final_ops

        def run_engine(ename, eobj):
            waited = {}
            for o in self.q[ename]:
                need = {}
                for d in o.deps:
                    if d.eng == ename and not d.dma:
                        if ename == "tensor" or not SAME_ENGINE_SYNC:
                            continue
                    k = id(d.sem)
                    if k not in need or need[k][1] < d.val:
                        need[k] = (d.sem, d.val)
                for k, (s, v) in need.items():
                    if waited.get(k, 0) >= v:
                        continue
                    eobj.wait_ge(s, v)
                    waited[k] = v
                ins = o.fn(eobj)
                if o.dma:
                    ins.then_inc(o.sem, 16)
                elif o.needed:
                    ins.then_inc(o.sem, 1)
            if ename == "sync":
                for o in final_ops:
                    eobj.wait_ge(o.sem, o.val)

        with nc.Block() as block:
            @block.tensor
            def _(e):
                run_engine("tensor", e)

            @block.vector
            def _(e):
                run_engine("vector", e)

            @block.scalar
            def _(e):
                run_engine("scalar", e)

            @block.gpsimd
            def _(e):
                run_engine("gpsimd", e)

            @block.sync
            def _(e):
                run_engine("sync", e)


def _dtbytes(dt):
    return 2 if dt == BF16 else 4


class Arena:
    def __init__(self, nc):
        self.nc = nc
        self.off = SB_BASE
        self.n = 0
        self.peak = 0

    def alloc(self, name, shape, dt):
        sz = int(np.prod(shape[1:])) * _dtbytes(dt)
        off = (self.off + 63) // 64 * 64
        self.n += 1
        t = self.nc.alloc_sbuf_tensor_at("%s_%d" % (name, self.n), list(shape), dt, offset=off)
        self.off = off + sz
        self.peak = max(self.peak, self.off)
        assert self.off <= SB_LIMIT, ("SBUF overflow", name, self.off)
        return t

    def mark(self):
        return self.off

    def reset(self, m):
        self.off = m


class Rot:
    def __init__(self, tiles, name):
        self.tiles = tiles
        self.keys = ["%s#%d" % (name, i) for i in range(len(tiles))]
        self.i = -1

    def next(self):
        self.i = (self.i + 1) % len(self.tiles)
        return self.tiles[self.i], self.keys[self.i]


def build_program(dbg=None, stages=None):
    dbg = dbg or ()
    nc = bass.Bass("TRN2", target_bir_lowering=False)
    P = Prog(nc)
    ar = Arena(nc)

    def din(name, shape, dt=F32):
        return nc.dram_tensor(name, list(shape), dt, kind="ExternalInput").ap()

    scr = {}

    def dscr(name, shape, dt):
        a = nc.dram_tensor(name, list(shape), dt, kind="Internal").ap()
        scr[name] = (a, list(shape), dt)
        return a

    x_b = din("x_b", [NLAT, D])
    ctx_b = din("ctx_b", [NCTX, D])
    ccol = din("ccol", [128, KC, 2])
    ada_w = din("ada_w", [DEPTH, D, 6 * D])
    ada_bc = din("ada_bc", [128, DEPTH, 48])
    g1c = din("g1c", [128, DEPTH, KC])
    g2c = din("g2c", [128, DEPTH, KC])
    w_in = din("w_in", [DEPTH, D, IN_TOTAL])
    qkg = din("qkg", [128, DEPTH, 2])
    w_branch = din("w_branch", [DEPTH, 3, D, D])
    w_out = din("w_out", [DEPTH, D, D])
    w_mlp1 = din("w_mlp1", [DEPTH, D, DFF])
    w_mlp2 = din("w_mlp2", [DEPTH, DFF, D])
    c_idf = din("c_idf", [128, 128])
    c_idb = din("c_idb", [128, 128], BF16)
    c_onesb = din("c_onesb", [128, 128], BF16)
    c_rm = din("c_rm", [128, 128], BF16)
    c_cos = din("c_cos", [128, NLAT])
    c_sin = din("c_sin", [128, NLAT])
    c_dm = din("c_dm", [128, 2, 128])
    c_dm2 = din("c_dm2", [128, 2, 128])
    c_dsel = din("c_dsel", [128, 2, 6])
    c_msk = din("c_msk", [128, 2, 512])
    c_cmk = din("c_cmk", [128, 2, 3, 128], BF16)
    hgn_row = din("hgn_row", [DEPTH, D])
    hg_lb = din("hg_lb", [DEPTH, 2, D])
    hysw_d = din("hysw", [128, DEPTH, 24, 3])
    hysb_d = din("hysb", [128, DEPTH, 24])
    hyb_d = din("hyb", [128, DEPTH, 2, KC])
    hy_w1 = din("hy_w1", [DEPTH, 33, 64])
    hy_w2 = din("hy_w2", [DEPTH, 64, 64])
    hy_w3 = din("hy_w3", [DEPTH, 64, 4 * D])
    hy_fc = din("hy_fc", [64, DEPTH, 4])
    tabs = {}
    for nm, n_ in (("l", NLAT), ("c", NCTX)):
        nlt = n_ // 128
        tcw = min(n_, 512)
        tabs[nm] = (din("Ft_" + nm, [2 * nlt, 128, nlt, 128], BF16), din("FTt_" + nm, [n_ // tcw, 128, 2 * nlt, tcw], BF16),
                    din("zfT_" + nm, [33, n_]), din("dec_" + nm, [n_, D]), din("arow_" + nm, [128, 2 * nlt]))
    out_d = nc.dram_tensor("out", [NLAT, D], F32, kind="ExternalOutput").ap()

    xT = dscr("xT", [128, KC, TOK], F32)
    hTd = dscr("hTd", [128, KC, TOK], BF16)
    gT = dscr("gT", [3, 128, KC, TOK], BF16)
    mT = dscr("mT", [128, KC, TOK], BF16)
    brT = dscr("brT", [128, KC, TOK], BF16)
    hx1T = dscr("hx1T", [128, KC, TOK], BF16)
    hx2T = dscr("hx2T", [128, KC, TOK], BF16)
    hvT = dscr("hvT", [128, KC, TOK], BF16)
    hz1T = dscr("hz1T", [128, KC, TOK], BF16)
    Hs_l = dscr("Hs_l", [2 * NLAT, 2, D], BF16)
    Hs_c = dscr("Hs_c", [2 * NCTX, 2, D], BF16)
    zq = dscr("zq", [TOK, D], BF16)
    zi = dscr("zi", [TOK, D], BF16)
    zg = dscr("zg", [TOK, D], BF16)
    zf = dscr("zf", [TOK, D], F32)
    zb = dscr("zb", [TOK, D], F32)

    dumped = {}

    def dump(name, ap, shape, dt=F32, reads=()):
        if ("D:" + name) not in dbg or name in dumped:
            return
        dumped[name] = 1
        tw = nc.dram_tensor("dbg_" + name, list(shape), dt, kind="ExternalOutput").ap()
        P.dma(tw, ap, reads=list(reads), final=True)

    with contextlib.ExitStack() as st:
        sems_eng = {e: st.enter_context(nc.semaphore("s_" + e)) for e in ENGS}
        sems_dma = [st.enter_context(nc.semaphore("d%d" % i)) for i in range(2 * N_DMA_SEMS)]
        psf_t = [nc.alloc_psum_tensor("psf%d" % i, [128, 512], F32) for i in range(6)]
        psb_t = [nc.alloc_psum_tensor("psb%d" % i, [128, 1024], BF16) for i in range(2)]
        psf = Rot(psf_t, "psf")
        psb = Rot(psb_t, "psb")

        idf = ar.alloc("idf", [128, 128], F32)
        idb = ar.alloc("idb", [128, 128], BF16)
        onesb = ar.alloc("onesb", [128, 128], BF16)
        rm = ar.alloc("rm", [128, 128], BF16)
        modT = ar.alloc("modT", [128, DEPTH, 48, 2], F32)
        adab = ar.alloc("adab", [128, DEPTH, 48], F32)
        g1s = ar.alloc("g1s", [128, DEPTH, KC], F32)
        g2s = ar.alloc("g2s", [128, DEPTH, KC], F32)
        A1 = ar.alloc("A1", [128, DEPTH, KC, 2], F32)
        A2 = ar.alloc("A2", [128, DEPTH, KC, 2], F32)
        qkgs = ar.alloc("qkgs", [128, DEPTH, 2], F32)
        epsc = ar.alloc("epsc", [128, 1], F32)
        P.memset(epsc[:], EPS, writes=["const"])
        hysw = ar.alloc("hysw", [128, DEPTH, 24, 3], F32)
        hysb = ar.alloc("hysb", [128, DEPTH, 24], F32)
        hyb = ar.alloc("hyb", [128, DEPTH, 2, KC], F32)
        for t_, d_ in ((idf, c_idf), (idb, c_idb), (onesb, c_onesb), (rm, c_rm), (adab, ada_bc),
                       (g1s, g1c), (g2s, g2c), (qkgs, qkg), (hysw, hysw_d), (hysb, hysb_d), (hyb, hyb_d)):
            P.dma(t_[:], d_, writes=["const"])

        def wcast_load(dst, src, key, eng="gpsimd"):
            return P.dma(dst, src, writes=[key], eng=eng)

        def stage_adaln():
            m = ar.mark()
            cs = ar.alloc("cs", [128, KC, 2], F32)
            wb = Rot([ar.alloc("adaw", [128, KC, 512], F32) for _ in range(2)], "adaw")
            P.dma(cs[:], ccol, writes=["cs"])
            P.act(cs[:], cs[:], AF.Silu, reads=["cs"], writes=["cs"])
            for l in range(DEPTH):
                wv = ada_w[l].rearrange("(k p) c -> p k c", p=128)
                for cb in range(12):
                    w, wk = wb.next()
                    P.dma(w[:], wv[:, :, cb * 512:(cb + 1) * 512], writes=[wk])
                    for j in range(4):
                        ot = cb * 4 + j
                        ps, pk = psf.next()
                        for k in range(KC):
                            P.mm(ps[:, 0:2], w[:, k, j * 128:(j + 1) * 128], cs[:, k, :], start=(k == 0),
                                 stop=(k == KC - 1), reads=[wk, "cs"], writes=[pk])
                        P.ts(modT[:, l, ot, :], ps[:, 0:2], adab[:, l, ot:ot + 1], None, ALU.add,
                             reads=[pk, "const"], writes=["modT"])
                for which in range(2):
                    P.ts(A1[:, l, :, which], modT[:, l, 8:16, which], 1.0, None, ALU.add, reads=["modT"], writes=["A1"])
                    P.tt(A1[:, l, :, which], A1[:, l, :, which], g1s[:, l, :], ALU.mult, reads=["A1", "const"], writes=["A1"])
                    P.ts(A2[:, l, :, which], modT[:, l, 32:40, which], 1.0, None, ALU.add, reads=["modT"], writes=["A2"])
                    P.tt(A2[:, l, :, which], A2[:, l, :, which], g2s[:, l, :], ALU.mult, reads=["A2", "const"], writes=["A2"])
            P.barrier()
            ar.reset(m)

        def which_of(s):
            return 1 if s < NCTX else 0

        def stage_x0():
            m = ar.mark()
            xin = Rot([ar.alloc("xin", [128, D], F32) for _ in range(2)], "xin")
            xst = Rot([ar.alloc("xst", [128, KC, 128], F32) for _ in range(2)], "xst")
            for t in range(NTILE):
                src = ctx_b[t * 128:(t + 1) * 128, :] if t < 2 else x_b[(t - 2) * 128:(t - 1) * 128, :]
                xi, xk = xin.next()
                xs, sk = xst.next()
                P.dma(xi[:], src, writes=[xk])
                for half in range(2):
                    ps, pk = psf.next()
                    for kk in range(4):
                        k = half * 4 + kk
                        P.tr(ps[:, kk * 128:(kk + 1) * 128], xi[:, k * 128:(k + 1) * 128], idf[:],
                             reads=[xk, "const"], writes=[pk])
                    for kk in range(4):
                        P.copy(xs[:, half * 4 + kk, :], ps[:, kk * 128:(kk + 1) * 128], reads=[pk], writes=[sk],
                               eng=("vector" if kk % 2 == 0 else "scalar"))
                P.dma(xT[:, :, t * 128:(t + 1) * 128], xs[:], reads=[sk], writes=[("xT", (t - 2) // 4 if t >= 2 else -1)])
            P.barrier()
            ar.reset(m)

        def xkey(ci):
            return ("xT", ci - 1 if ci > 0 else -1)

        def norm_chunk(l, which_norm, xc, xck, n, s, dst, dstk, tmp):
            A = A1 if which_norm == 1 else A2
            shift0 = 0 if which_norm == 1 else 24
            w = which_of(s)
            sq, rstd, t32 = tmp
            P.act(sq[:, :, :n], xc[:, :, :n], AF.Square, reads=[xck], writes=["nsq"])
            ps, pk = psf.next()
            for k in range(KC):
                P.mm(ps[:, :n], onesb[:], sq[:, k, :n], start=(k == 0), stop=(k == KC - 1),
                     reads=["nsq", "const"], writes=[pk])
            P.act(rstd[:, :n], ps[:, :n], AF.Ln, scale=1.0 / D, bias=epsc[:, 0:1], reads=[pk, "const"], writes=["nrstd"])
            P.act(rstd[:, :n], rstd[:, :n], AF.Exp, scale=-0.5, reads=["nrstd"], writes=["nrstd"])
            for k in range(KC):
                P.stt(t32[:, k, :n], xc[:, k, :n], A[:, l, k, w:w + 1], rstd[:, :n], ALU.mult, ALU.mult,
                      reads=[xck, "nrstd", "A1", "A2"], writes=["nt32"])
                P.act(dst[:, k, :n], t32[:, k, :n], AF.Identity, bias=modT[:, l, shift0 + k, w:w + 1],
                      reads=["nt32", "modT"], writes=[dstk])

        def alloc_norm_tmp(nmax):
            return (ar.alloc("nsq", [128, KC, nmax], BF16), ar.alloc("nrstd", [128, nmax], F32),
                    ar.alloc("nt32", [128, KC, nmax], F32))

        def stage_norm1(l, hT):
            m = ar.mark()
            xcb = Rot([ar.alloc("xc", [128, KC, 512], F32) for _ in range(2)], "xc")
            tmp = alloc_norm_tmp(512)
            for ci, (s, n) in enumerate(CH):
                xc, xck = xcb.next()
                P.dma(xc[:, :, :n], xT[:, :, s:s + n], reads=[xkey(ci)], writes=[xck])
                norm_chunk(l, 1, xc, xck, n, s, hT[:, :, s:s + n], ("hT", ci), tmp)
                if "hTd" in dbg:
                    P.dma(hTd[:, :, s:s + n], hT[:, :, s:s + n], reads=[("hT", ci)], writes=["hTd"])
            P.barrier()
            ar.reset(m)

        HT_ALL = [("hT", ci) for ci in range(len(CH))]

        def proj_fm(wsrc, col0, ncols, hT, consumer, wrot):
            wv = wsrc.rearrange("(k p) c -> p k c", p=128)
            prot = Rot(psf_t[0:3], "psf")
            active = []

            def advance():
                for g_ in list(active):
                    try:
                        next(g_)
                    except StopIteration:
                        active.remove(g_)

            for cb in range(0, ncols, 512):
                nb = min(512, ncols - cb)
                w, wk = wrot.next()
                wcast_load(w[:, :, :nb], wv[:, :, col0 + cb:col0 + cb + nb], wk)
                for j in range(nb // 128):
                    for ci, (s, n) in enumerate(CH):
                        ps, pk = prot.next()
                        for k in range(KC):
                            P.mm(ps[:, :n], w[:, k, j * 128:(j + 1) * 128], hT[:, k, s:s + n], start=(k == 0),
                                 stop=(k == KC - 1), reads=[wk, ("hT", ci)], writes=[pk])
                        advance()
                        g_ = consumer((cb // 128) + j, ci, s, n, ps, pk)
                        if g_ is not None:
                            active.append(g_)
            while active:
                advance()

        def stage_gates(l, hT):
            m = ar.mark()
            wrot = Rot([ar.alloc("wg", [128, KC, 512], BF16) for _ in range(2)], "wg")
            stg = Rot([ar.alloc("gst", [128, 512], BF16) for _ in range(4)], "gst")

            def cons(jg, ci, s, n, ps, pk):
                b, bk = stg.next()
                P.act(b[:, :n], ps[:, :n], AF.Sigmoid, reads=[pk], writes=[bk])
                P.dma(gT[jg // 8, :, jg % 8, s:s + n], b[:, :n], reads=[bk], writes=["gT"])
                yield

            proj_fm(w_in[l], OFF_GATE, 3 * D, hT, cons, wrot)
            P.barrier()
            ar.reset(m)

        def branch_proj(l, br, src_fn, first):
            m = ar.mark()
            wb = ar.alloc("wbr", [128, KC, D], BF16)
            wv = w_branch[l, br].rearrange("(k p) c -> p k c", p=128)
            for hh in range(2):
                wcast_load(wb[:, :, hh * 512:(hh + 1) * 512], wv[:, :, hh * 512:(hh + 1) * 512], ("wbr", hh))
            gch = Rot([ar.alloc("gch", [128, KC, 512], BF16) for _ in range(2)], "gch")
            mold = Rot([ar.alloc("mold", [128, KC, 512], BF16) for _ in range(2)], "mold")
            mnew = Rot([ar.alloc("mnew", [128, KC, 512], BF16) for _ in range(2)], "mnew")
            t32 = Rot([ar.alloc("bt32", [128, 512], F32) for _ in range(2)], "bt32")
            for ci, (s, n) in enumerate(CH):
                if l == DEPTH - 1 and ci == 0:
                    continue
                src, srck = src_fn(ci, s, n)
                g, gk = gch.next()
                P.dma(g[:, :, :n], gT[br, :, :, s:s + n], reads=["gT"], writes=[gk])
                mn, mnk = mnew.next()
                if not first:
                    mo, mok = mold.next()
                    P.dma(mo[:, :, :n], mT[:, :, s:s + n], reads=[("mT", ci)], writes=[mok])
                for j in range(KC):
                    ps, pk = psf.next()
                    for k in range(KC):
                        P.mm(ps[:, :n], wb[:, k, j * 128:(j + 1) * 128], src[:, k, :n], start=(k == 0),
                             stop=(k == KC - 1), reads=[("wbr", j // 4)] + list(srck), writes=[pk])
                    if first:
                        P.tt(mn[:, j, :n], ps[:, :n], g[:, j, :n], ALU.mult, reads=[pk, gk], writes=[mnk])
                    else:
                        t, tk = t32.next()
                        P.tt(t[:, :n], ps[:, :n], g[:, j, :n], ALU.mult, reads=[pk, gk], writes=[tk])
                        P.tt(mn[:, j, :n], t[:, :n], mo[:, j, :n], ALU.add, reads=[tk, mok], writes=[mnk], eng="gpsimd")
                P.dma(mT[:, :, s:s + n], mn[:, :, :n], reads=[mnk], writes=[("mT", ci)])
            P.barrier()
            ar.reset(m)

        def stage_attention(l, hT, attT):
            m = ar.mark()
            wrot = Rot([ar.alloc("wqk", [128, KC, 512], BF16) for _ in range(2)], "wqk")
            cosT = ar.alloc("cosT", [128, NLAT], F32)
            sinT = ar.alloc("sinT", [128, NLAT], F32)
            P.dma(cosT[:], c_cos, writes=["rope"])
            P.dma(sinT[:], c_sin, writes=["rope"])
            qT = ar.alloc("qT", [128, 10, TOK], BF16)
            vtm = ar.alloc("vtm", [128, NTILE, 256], BF16)
            sqb = Rot([ar.alloc("sqb", [128, 512], BF16) for _ in range(2)], "sqb")
            qnb = Rot([ar.alloc("qnb", [128, 512], BF16) for _ in range(2)], "qnb")
            rsb = Rot([ar.alloc("rsb", [128, 512], F32) for _ in range(2)], "rsb")
            t1b = Rot([ar.alloc("t1b", [128, 512], F32) for _ in range(2)], "t1b")
            t2b = Rot([ar.alloc("t2b", [128, 512], F32) for _ in range(2)], "t2b")

            def cons_qk(jg, ci, s, n, ps, pk):
                is_k = jg >= 8
                gcol = qkgs[:, l, 1:2] if is_k else qkgs[:, l, 0:1]
                sq, sqk = sqb.next()
                P.act(sq[:, :n], ps[:, :n], AF.Square, reads=[pk], writes=[sqk])
                ps2, pk2 = psf_t[3], "psf#3"
                P.mm(ps2[:, :n], onesb[:], sq[:, :n], reads=[sqk, "const"], writes=[pk2])
                yield
                rs, rsk = rsb.next()
                P.act(rs[:, :n], ps2[:, :n], AF.Ln, scale=1.0 / 128, bias=epsc[:, 0:1], reads=[pk2, "const"], writes=[rsk])
                P.act(rs[:, :n], rs[:, :n], AF.Exp, scale=-0.5, reads=[rsk], writes=[rsk])
                dstk = ("qT", jg, ci)
                if ci == 0:
                    P.stt(qT[:, jg, s:s + n], ps[:, :n], gcol, rs[:, :n], ALU.mult, ALU.mult,
                          reads=[pk, rsk, "const"], writes=[dstk])
                    return
                qn, qnk = qnb.next()
                P.stt(qn[:, :n], ps[:, :n], gcol, rs[:, :n], ALU.mult, ALU.mult, reads=[pk, rsk, "const"], writes=[qnk])
                ps3, pk3 = psf_t[4], "psf#4"
                P.mm(ps3[:, :n], rm[:], qn[:, :n], reads=[qnk, "const"], writes=[pk3])
                yield
                t1, t1k = t1b.next()
                t2, t2k = t2b.next()
                ls = s - NCTX
                P.tt(t1[:, :n], qn[:, :n], cosT[:, ls:ls + n], ALU.mult, reads=[qnk, "rope"], writes=[t1k], eng="gpsimd")
                P.tt(t2[:, :n], ps3[:, :n], sinT[:, ls:ls + n], ALU.mult, reads=[pk3, "rope"], writes=[t2k])
                P.tt(qT[:, jg, s:s + n], t1[:, :n], t2[:, :n], ALU.add, reads=[t1k, t2k], writes=[dstk])

            proj_fm(w_in[l], OFF_Q, 1024 + 256, hT, cons_qk, wrot)
            wv_t, wvk = wrot.next()
            wcast_load(wv_t[:, :, :256], w_in[l].rearrange("(k p) c -> p k c", p=128)[:, :, OFF_V:OFF_V + 256], wvk)
            for t in range(NTILE):
                ps, pk = psf.next()
                ci = 0 if t < 2 else 1 + (t - 2) // 4
                for k in range(KC):
                    P.mm(ps[:, :256], hT[:, k, t * 128:(t + 1) * 128], wv_t[:, k, :256], start=(k == 0),
                         stop=(k == KC - 1), reads=[wvk, ("hT", ci)], writes=[pk])
                P.copy(vtm[:, t, :], ps[:, :256], reads=[pk], writes=[("vtm", t)], eng="scalar")
            pTb = Rot([ar.alloc("pTb", [128, 512], BF16) for _ in range(3)], "pTb")
            psO_t, psS_t = psf_t[4], psf_t[5]
            psST = Rot(psf_t[0:4], "psf")
            rsum = ar.alloc("rsum", [128, 512], F32)
            scale = 128.0 ** -0.5
            for h in range(8):
                g = h // 4
                for ci, (s, n) in enumerate(CH):
                    if l == DEPTH - 1 and ci == 0:
                        continue
                    kts = [0, 1] if ci == 0 else list(range(NTILE))
                    kc_of = lambda kt: 0 if kt < 2 else 1 + (kt - 2) // 4
                    pend = None
                    for i, kt in enumerate(kts + [None]):
                        cur = None
                        if kt is not None:
                            ps, pk = psST.next()
                            P.mm(ps[:, :n], qT[:, 8 + g, kt * 128:(kt + 1) * 128], qT[:, h, s:s + n],
                                 reads=[("qT", 8 + g, kc_of(kt)), ("qT", h, ci)], writes=[pk])
                            pt, ptk = pTb.next()
                            P.act(pt[:, :n], ps[:, :n], AF.Exp, scale=scale, reads=[pk], writes=[ptk])
                            cur = (kt, pt, ptk)
                        if pend is not None:
                            pkt, ppt, pptk = pend
                            first = (pkt == kts[0])
                            last = (pkt == kts[-1])
                            P.mm(psO_t[:, :n], vtm[:, pkt, g * 128:(g + 1) * 128], ppt[:, :n], start=first, stop=last,
                                 reads=[("vtm", pkt), pptk], writes=["psf#4"])
                            P.mm(psS_t[:, :n], onesb[:], ppt[:, :n], start=first, stop=last,
                                 reads=["const", pptk], writes=["psf#5"])
                        pend = cur
                    P.recip(rsum[:, :n], psS_t[:, :n], reads=["psf#5"], writes=["rsum"])
                    P.tt(attT[:, h, s:s + n], psO_t[:, :n], rsum[:, :n], ALU.mult, reads=["psf#4", "rsum"],
                         writes=[("attT", ci)])
            if "brT" in dbg:
                P.dma(brT[:], attT[:], reads=[("attT", ci) for ci in range(5)], writes=["brT"])
            P.barrier()
            ar.reset(m)

        def stage_wout(l):
            m = ar.mark()
            wo = ar.alloc("wo", [128, KC, D], BF16)
            wv = w_out[l].rearrange("(k p) c -> p k c", p=128)
            for hh in range(2):
                wcast_load(wo[:, :, hh * 512:(hh + 1) * 512], wv[:, :, hh * 512:(hh + 1) * 512], ("wo", hh))
            mch = Rot([ar.alloc("mch", [128, KC, 512], BF16) for _ in range(2)], "mch")
            xcb = Rot([ar.alloc("xc", [128, KC, 512], F32) for _ in range(2)], "xc")
            for ci, (s, n) in enumerate(CH):
                if l == DEPTH - 1 and ci == 0:
                    continue
                w = which_of(s)
                mc, mck = mch.next()
                xc, xck = xcb.next()
                P.dma(mc[:, :, :n], mT[:, :, s:s + n], reads=[("mT", ci)], writes=[mck])
                P.dma(xc[:, :, :n], xT[:, :, s:s + n], reads=[xkey(ci)], writes=[xck])
                for j in range(KC):
                    ps, pk = psf.next()
                    for k in range(KC):
                        P.mm(ps[:, :n], wo[:, k, j * 128:(j + 1) * 128], mc[:, k, :n], start=(k == 0),
                             stop=(k == KC - 1), reads=[("wo", j // 4), mck], writes=[pk])
                    P.stt(xc[:, j, :n], ps[:, :n], modT[:, l, 16 + j, w:w + 1], xc[:, j, :n], ALU.mult, ALU.add,
                          reads=[pk, xck, "modT"], writes=[xck])
                P.dma(xT[:, :, s:s + n], xc[:, :, :n], reads=[xck], writes=[xkey(ci)])
            P.barrier()
            ar.reset(m)

        def stage_mlp(l):
            m = ar.mark()
            NM = 256
            w1 = ar.alloc("w1", [128, KC, DFF], BF16)
            w2 = ar.alloc("w2", [128, 32, D], BF16)
            w1v = w_mlp1[l].rearrange("(k p) c -> p k c", p=128)
            w2v = w_mlp2[l].rearrange("(f p) c -> p f c", p=128)
            for i in range(8):
                wcast_load(w1[:, :, i * 512:(i + 1) * 512], w1v[:, :, i * 512:(i + 1) * 512], ("w1", i))
            for i in range(8):
                wcast_load(w2[:, i * 4:(i + 1) * 4, :], w2v[:, i * 4:(i + 1) * 4, :], ("w2", i))
            xcb = Rot([ar.alloc("xc", [128, KC, NM], F32) for _ in range(2)], "xc")
            h2b = Rot([ar.alloc("h2", [128, KC, NM], BF16) for _ in range(1)], "h2")
            ub = Rot([ar.alloc("u", [128, 32, NM], BF16) for _ in range(1)], "u")
            rl = Rot([ar.alloc("rl", [128, NM], F32) for _ in range(3)], "rl")
            tmp = alloc_norm_tmp(NM)
            for s in range(0, TOK, NM):
                if l == DEPTH - 1 and s < NCTX:
                    continue
                n = NM
                ci = 0 if s < NCTX else 1 + (s - NCTX) // 512
                xc, xck = xcb.next()
                h2, h2k = h2b.next()
                u, uk = ub.next()
                P.dma(xc[:], xT[:, :, s:s + n], reads=[xkey(ci)], writes=[xck])
                norm_chunk(l, 2, xc, xck, n, s, h2, h2k, tmp)
                for f in range(32):
                    ps, pk = psf.next()
                    for k in range(KC):
                        P.mm(ps[:, :n], w1[:, k, f * 128:(f + 1) * 128], h2[:, k, :n], start=(k == 0),
                             stop=(k == KC - 1), reads=[("w1", f // 4), h2k], writes=[pk])
                    r, rk = rl.next()
                    P.act(r[:, :n], ps[:, :n], AF.Relu, reads=[pk], writes=[rk])
                    P.tt(u[:, f, :n], r[:, :n], r[:, :n], ALU.mult, reads=[rk], writes=[uk],
                         eng=("gpsimd" if f % 2 == 0 else "vector"))
                w = which_of(s)
                for j in range(KC):
                    ps, pk = psf.next()
                    for f in range(32):
                        P.mm(ps[:, :n], w2[:, f, j * 128:(j + 1) * 128], u[:, f, :n], start=(f == 0), stop=(f == 31),
                             reads=[("w2", f // 4), uk], writes=[pk])
                    P.stt(xc[:, j, :n], ps[:, :n], modT[:, l, 40 + j, w:w + 1], xc[:, j, :n], ALU.mult, ALU.add,
                          reads=[pk, xck, "modT"], writes=[xck])
                P.dma(xT[:, :, s:s + n], xc[:], reads=[xck], writes=[xkey(ci)])
            P.barrier()
            ar.reset(m)

        def stage_final():
            m = ar.mark()
            xin = Rot([ar.alloc("fxi", [128, KC, 128], F32) for _ in range(2)], "fxi")
            xo = Rot([ar.alloc("fxo", [128, D], F32) for _ in range(2)], "fxo")
            for t in range(2, NTILE):
                ci = 1 + (t - 2) // 4
                xi, xk = xin.next()
                o, ok = xo.next()
                P.dma(xi[:], xT[:, :, t * 128:(t + 1) * 128], reads=[xkey(ci)], writes=[xk])
                for half in range(2):
                    ps, pk = psf.next()
                    for kk in range(4):
                        k = half * 4 + kk
                        P.tr(ps[:, kk * 128:(kk + 1) * 128], xi[:, k, :], idf[:], reads=[xk, "const"], writes=[pk])
                    P.copy(o[:, half * 512:(half + 1) * 512], ps[:], reads=[pk], writes=[ok],
                           eng=("vector" if half == 0 else "scalar"))
                P.dma(out_d[(t - 2) * 128:(t - 1) * 128, :], o[:], reads=[ok], final=True)
            ar.reset(m)

        def stage_hgrn(l, hT):
            m = ar.mark()
            m1 = ar.mark()
            wrot = Rot([ar.alloc("wh", [128, KC, 512], BF16) for _ in range(2)], "wh")
            st32 = Rot([ar.alloc("hs32", [128, 512], F32) for _ in range(3)], "hs32")
            st16 = Rot([ar.alloc("hs16", [128, 512], BF16) for _ in range(3)], "hs16")
            dsts = [(zq, BF16), (zf, F32), (zb, F32), (zi, BF16), (zg, BF16)]
            wv = w_in[l].rearrange("(k p) c -> p k c", p=128)
            for cb in range(10):
                w, wk = wrot.next()
                wcast_load(w[:], wv[:, :, OFF_HQ + cb * 512:OFF_HQ + (cb + 1) * 512], wk)
                dst, dt = dsts[cb // 2]
                c0 = (cb % 2) * 512
                for t in range(NTILE):
                    ps, pk = psf.next()
                    ci = 0 if t < 2 else 1 + (t - 2) // 4
                    for k in range(KC):
                        P.mm(ps[:], hT[:, k, t * 128:(t + 1) * 128], w[:, k, :], start=(k == 0), stop=(k == KC - 1),
                             reads=[wk, ("hT", ci)], writes=[pk])
                    sb, sk = (st32 if dt == F32 else st16).next()
                    P.copy(sb[:], ps[:], reads=[pk], writes=[sk], eng=("vector" if t % 2 == 0 else "scalar"))
                    P.dma(dst[t * 128:(t + 1) * 128, c0:c0 + 512], sb[:], reads=[sk], writes=[("zh", cb // 2, t)])
            P.barrier()
            ar.reset(m1)
            Dm = ar.alloc("Dm", [128, 2, 128], F32)
            Dsel = ar.alloc("Dsel", [128, 2, 6], F32)
            msk = ar.alloc("msk", [128, 2, 512], F32)
            gn = ar.alloc("gn", [128, D], F32)
            lbt = ar.alloc("lbt", [128, 2, D], F32)
            oml = ar.alloc("oml", [128, 2, D], F32)
            P.dma(Dm[:], c_dm, writes=["hgc"])
            P.dma(Dsel[:], c_dsel, writes=["hgc"])
            P.dma(msk[:], c_msk, writes=["hgc"])
            P.dma(gn[:], hgn_row[l].partition_broadcast(128), writes=["hgc"])
            if l == 0:
                P.memset(lbt[:], 0.0, writes=["lbt"])
                P.memset(oml[:], 1.0, writes=["oml"])
            else:
                for dr in range(2):
                    P.dma(lbt[:, dr, :], hg_lb[1, dr].partition_broadcast(128), writes=["lbt"])
                    P.dma(oml[:, dr, :], hg_lb[0, dr].partition_broadcast(128), writes=["oml"])
                P.tt(lbt[:], lbt[:], oml[:], ALU.subtract, reads=["lbt", "oml"], writes=["lbt"])
                P.act(lbt[:], lbt[:], AF.Sigmoid, reads=["lbt"], writes=["lbt"])
                P.ts(oml[:], lbt[:], -1.0, 1.0, ALU.mult, ALU.add, reads=["lbt"], writes=["oml"])
            o_b = ar.alloc("o_b", [128, NTILE, D], BF16)
            S = ar.alloc("S", [128, 2, KC, 128], F32)
            P.memset(S[:], 0.0, writes=[("S%d" % d_, h_) for d_ in range(2) for h_ in range(KC)])
            qin = Rot([ar.alloc("hq", [128, D], BF16) for _ in range(2)], "hq")
            zin = Rot([ar.alloc("hz", [128, D], F32) for _ in range(2)], "hz")
            vin = Rot([ar.alloc("hv", [128, D], BF16) for _ in range(3)], "hv")
            gin = Rot([ar.alloc("hgi", [128, D], BF16) for _ in range(2)], "hgi")
            sig = ar.alloc("sig", [128, D], F32)
            logf = ar.alloc("logf", [128, D], F32)
            kk = ar.alloc("kk", [128, D], F32)
            Ep = ar.alloc("Ep", [128, D], F32)
            Em = ar.alloc("Em", [128, D], F32)
            qt = ar.alloc("qt", [128, D], BF16)
            ktR = Rot([ar.alloc("kt", [128, D], BF16) for _ in range(2)], "kt")
            kt2R = Rot([ar.alloc("kt2", [128, D], BF16) for _ in range(2)], "kt2")
            Dm2 = ar.alloc("Dm2", [128, 2, 128], F32)
            zbf = ar.alloc("zbf", [128, 512], BF16)
            P.memset(zbf[:], 0.0, writes=["hgc"])
            P.dma(Dm2[:], c_dm2, writes=["hgc"])
            qtTR = Rot([ar.alloc("qtT", [128, KC, 128], BF16) for _ in range(2)], "qtT")
            ktTR = Rot([ar.alloc("ktT", [128, KC, 128], BF16) for _ in range(2)], "ktT")
            ktBR = Rot([ar.alloc("ktB", [128, KC, 128], BF16) for _ in range(2)], "ktB")
            ktF = ar.alloc("ktF", [128, KC, 128], BF16)
            qtCR = Rot([ar.alloc("qtC", [128, KC, 128], BF16) for _ in range(2)], "qtC")
            cmk = ar.alloc("cmk", [128, 2, 3, 128], BF16)
            P.dma(cmk[:], c_cmk, writes=["hgc"])
            sclR = Rot([ar.alloc("scl", [128, KC, 6], F32) for _ in range(2)], "scl")
            sm = ar.alloc("sm", [128, KC, 128], BF16)
            Ss = ar.alloc("Ss", [128, KC, 128], BF16)
            o32 = ar.alloc("o32", [128, D], F32)
            sq32 = ar.alloc("sq32", [128, D], F32)
            silR = Rot([ar.alloc("sil", [128, D], BF16) for _ in range(2)], "sil")
            ssum = ar.alloc("ssum", [128, KC], F32)
            hgb = ar.alloc("hgb", [128, D], BF16)
            hgT = Rot([ar.alloc("hgT", [128, KC, 128], BF16) for _ in range(1)], "hgT")
            B0, B1, B2, B3, B4, B5 = ["psf#%d" % i for i in range(6)]
            p0, p1, p2, p3, p4, p5 = psf_t

            def tile_pass(t, dr):
                Sk = "S%d" % dr
                kt, kt_k = ktR.next()
                kt2, kt2_k = kt2R.next()
                qtT, qtT_k = qtTR.next()
                qtC, qtC_k = qtCR.next()
                ktT, ktT_k = ktTR.next()
                ktB, ktB_k = ktBR.next()
                scl, scl_k = sclR.next()
                q, qk = qin.next()
                z, zk = zin.next()
                v, vk = vin.next()
                r0 = t * 128
                P.dma(q[:], zq[r0:r0 + 128, :], reads=[("zh", 0, t)], writes=[qk])
                P.dma(z[:], (zf if dr == 0 else zb)[r0:r0 + 128, :], reads=[("zh", 1 + dr, t)], writes=[zk])
                P.dma(v[:], zi[r0:r0 + 128, :], reads=[("zh", 3, t)], writes=[vk])
                if dr == 0:
                    g, gk = gin.next()
                    P.dma(g[:], zg[r0:r0 + 128, :], reads=[("zh", 4, t)], writes=[gk])
                yield "P1"
                P.act(sig[:], z[:], AF.Sigmoid, reads=[zk], writes=["sig"])
                if dr == 0:
                    sil, silk = silR.next()
                    P.act(sil[:], g[:], AF.Silu, reads=[gk], writes=[silk])
                    P.tt(sil[:], sil[:], gn[:], ALU.mult, reads=[silk, "hgc"], writes=[silk], eng="gpsimd")
                yield
                P.tt(sig[:], sig[:], oml[:, dr, :], ALU.mult, reads=["sig", "oml"], writes=["sig"])
                yield
                P.tt(sig[:], sig[:], lbt[:, dr, :], ALU.add, reads=["sig", "lbt"], writes=["sig"])
                yield
                P.ts(kk[:], sig[:], -1.0, 1.0, ALU.mult, ALU.add, reads=["sig"], writes=["kk"], eng="gpsimd")
                P.ts(logf[:], sig[:], 1e-6, None, ALU.max, reads=["sig"], writes=["logf"])
                yield
                P.act(logf[:], logf[:], AF.Ln, reads=["logf"], writes=["logf"])
                yield
                for hf_ in range(2):
                    cs_ = slice(hf_ * 512, (hf_ + 1) * 512)
                    P.mm(p0[:], Dm[:, dr, :], logf[:, cs_], reads=["logf", "hgc"], writes=[B0])
                    yield
                    P.act(Ep[:, cs_], p0[:], AF.Exp, reads=[B0], writes=["Ep"])
                    P.act(Em[:, cs_], p0[:], AF.Exp, scale=-1.0, reads=[B0], writes=["Em"])
                    yield
                P.tt(qt[:], q[:], Ep[:], ALU.mult, reads=[qk, "Ep"], writes=["qt"])
                P.tt(kt[:], kk[:], Em[:], ALU.mult, reads=["kk", "Em"], writes=[kt_k], eng="gpsimd")
                yield
                for hf_ in range(2):
                    cs_ = slice(hf_ * 512, (hf_ + 1) * 512)
                    P.mm(p0[:], Dm2[:, dr, :], logf[:, cs_], reads=["logf", "hgc"], writes=[B0])
                    yield
                    P.act(Em[:, cs_], p0[:], AF.Exp, reads=[B0, kt_k], writes=["Em"])
                    yield
                P.tt(kt2[:], kk[:], Em[:], ALU.mult, reads=["kk", "Em"], writes=[kt2_k])
                for h in range(KC):
                    P.mm(p0[:, h * 6:(h + 1) * 6], logf[:, h * 128:(h + 1) * 128], Dsel[:, dr, :],
                         reads=["logf", "hgc"], writes=[B0])
                yield
                P.act(scl[:].rearrange("p h c -> p (h c)"), p0[:, 0:48], AF.Exp, reads=[B0], writes=[scl_k])
                fl = lambda a_: a_[:].rearrange("p h c -> p (h c)")
                for h in range(KC):
                    P.tr(psb_t[0][:, h * 128:(h + 1) * 128], qt[:, h * 128:(h + 1) * 128], idb[:],
                         reads=["qt", "const"], writes=["psb#0"])
                yield
                P.copy(fl(qtT), psb_t[0][:], reads=["psb#0"], writes=[qtT_k], eng="scalar")
                yield
                for h in range(KC):
                    P.tr(psb_t[0][:, h * 128:(h + 1) * 128], kt[:, h * 128:(h + 1) * 128], idb[:],
                         reads=[kt_k, "const"], writes=["psb#0"])
                P.tt(qtC[:], qtT[:], cmk[:, dr, 2, :].unsqueeze(1).to_broadcast([128, KC, 128]), ALU.mult,
                     reads=[qtT_k, "hgc"], writes=[qtC_k], eng="gpsimd")
                yield
                P.copy(fl(ktF), psb_t[0][:], reads=["psb#0"], writes=["ktF"], eng="scalar")
                yield
                P.tt(ktT[:], ktF[:], cmk[:, dr, 0, :].unsqueeze(1).to_broadcast([128, KC, 128]), ALU.mult,
                     reads=["ktF", "hgc"], writes=[ktT_k])
                P.tt(ktB[:], ktF[:], cmk[:, dr, 1, :].unsqueeze(1).to_broadcast([128, KC, 128]), ALU.mult,
                     reads=["ktF", "hgc"], writes=[ktB_k], eng="gpsimd")
                yield "P2"
                for hg in range(2):
                    for hh in range(4):
                        h = hg * 4 + hh
                        P.mm(p1[:, hh * 128:(hh + 1) * 128], ktT[:, h, :], qtT[:, h, :], start=True, stop=False,
                             reads=[ktT_k, qtT_k], writes=[B1])
                        P.mm(p1[:, hh * 128:(hh + 1) * 128], ktB[:, h, :], qtC[:, h, :], start=False, stop=True,
                             reads=[ktB_k, qtC_k], writes=[B1])
                    yield
                    P.tt(sm[:, hg * 4:(hg + 1) * 4, :].rearrange("p h c -> p (h c)"), p1[:], msk[:, dr, :], ALU.mult,
                         reads=[B1, "hgc"], writes=[("sm", hg)])
                    yield
                for hf_ in range(2):
                    pp, bk = (p2, B2) if hf_ == 0 else (p3, B3)
                    if dr == 0:
                        P.mm(pp[:], idb[:], o_b[:, t, hf_ * 512:(hf_ + 1) * 512], start=True, stop=False,
                             reads=["const", ("o_b", t)], writes=[bk])
                    else:
                        P.mm(pp[:], idb[:], zbf[:], start=True, stop=False, reads=["const", "hgc"], writes=[bk])
                for h in range(KC):
                    pp, bk = (p2, B2) if h < 4 else (p3, B3)
                    P.mm(pp[:, (h % 4) * 128:(h % 4 + 1) * 128], sm[:, h, :], v[:, h * 128:(h + 1) * 128],
                         start=False, stop=False, reads=[("sm", h // 4), vk], writes=[bk])
                yield
                order = (0, 1) if dr == 0 else (1, 0)
                for oi, cc in enumerate(order):
                    lo, hi = cc * 64, cc * 64 + 64
                    P.tt(Ss[:], S[:, dr, :, :], scl[:, :, cc * 3:cc * 3 + 1].to_broadcast([128, KC, 128]), ALU.mult,
                         reads=[(Sk, h_) for h_ in range(KC)] + [scl_k], writes=["Ss"])
                    for h in range(KC):
                        pp, bk = (p4, B4) if h < 4 else (p5, B5)
                        P.mm(pp[:, (h % 4) * 128:(h % 4 + 1) * 128], kt2[lo:hi, h * 128:(h + 1) * 128],
                             v[lo:hi, h * 128:(h + 1) * 128], reads=[kt2_k, vk], writes=[bk])
                    yield
                    for h in range(KC):
                        pp, bk = (p2, B2) if h < 4 else (p3, B3)
                        P.mm(pp[lo:hi, (h % 4) * 128:(h % 4 + 1) * 128], qtT[:, h, lo:hi], Ss[:, h, :],
                             start=False, stop=False, reads=[qtT_k, "Ss"], writes=[bk])
                    yield
                    for h in range(KC):
                        pp, bk = (p4, B4) if h < 4 else (p5, B5)
                        P.stt(S[:, dr, h, :], S[:, dr, h, :], scl[:, h, cc * 3 + 1:cc * 3 + 2],
                              pp[:, (h % 4) * 128:(h % 4 + 1) * 128], ALU.mult, ALU.add,
                              reads=[(Sk, h), scl_k, bk], writes=[(Sk, h)])
                        if h == 3:
                            yield
                    yield
                for hf_ in range(2):
                    pp, bk = (p2, B2) if hf_ == 0 else (p3, B3)
                    P.mm(pp[:], idb[:], zbf[:], start=False, stop=True, reads=["const", "hgc"], writes=[bk])
                if dr == 1:
                    P.copy(o_b[:, t, 0:512], p2[:], reads=[B2], writes=[("o_b", t)], eng="scalar")
                    P.copy(o_b[:, t, 512:1024], p3[:], reads=[B3], writes=[("o_b", t)])
                    return
                P.act(sq32[:, 0:512], p2[:], AF.Square, reads=[B2], writes=["sq32"])
                P.act(sq32[:, 512:1024], p3[:], AF.Square, reads=[B3], writes=["sq32"])
                yield
                P.op("vector", lambda e: e.reduce_sum(ssum[:], sq32[:].rearrange("p (h c) -> p h c", h=KC), AX.X),
                     reads=["sq32"], writes=["ssum"])
                yield
                P.act(ssum[:], ssum[:], AF.Ln, scale=1.0 / 128, bias=epsc[:, 0:1], reads=["ssum", "const"], writes=["ssum"])
                P.act(ssum[:], ssum[:], AF.Exp, scale=-0.5, reads=["ssum"], writes=["ssum"])
                yield
                for hf_ in range(2):
                    pp, bk = (p2, B2) if hf_ == 0 else (p3, B3)
                    P.tt(o32[:, hf_ * 512:(hf_ + 1) * 512].rearrange("p (h c) -> p h c", h=4),
                         pp[:].rearrange("p (h c) -> p h c", h=4),
                         ssum[:, hf_ * 4:(hf_ + 1) * 4].unsqueeze(2).to_broadcast([128, 4, 128]), ALU.mult,
                         reads=[bk, "ssum"], writes=["o32"])
                    yield
                P.tt(hgb[:], o32[:], sil[:], ALU.mult, reads=["o32", silk], writes=["hgb"])
                yield
                pb, pbk = psb_t[1], "psb#1"
                for h in range(KC):
                    P.tr(pb[:, h * 128:(h + 1) * 128], hgb[:, h * 128:(h + 1) * 128], idb[:], reads=["hgb", "const"], writes=[pbk])
                yield
                ht, htk = hgT.next()
                P.copy(ht[:].rearrange("p h c -> p (h c)"), pb[:], reads=[pbk], writes=[htk], eng="scalar")
                ci = 0 if t < 2 else 1 + (t - 2) // 4
                P.dma(brT[:, :, r0:r0 + 128], ht[:], reads=[htk], writes=[("brT", ci)])

            seq = [(t, 1) for t in [1, 0] + list(range(NTILE - 1, 1, -1))] + [(t, 0) for t in range(NTILE)]
            gens = [tile_pass(t, dr) for t, dr in seq]
            NG = len(gens)

            def run_to(g_, marker):
                for v_ in g_:
                    if v_ == marker:
                        return

            def step(g_):
                try:
                    return next(g_)
                except StopIteration:
                    return "END"

            run_to(gens[0], "P1")
            run_to(gens[0], "P2")
            run_to(gens[1], "P1")
            for i in range(NG):
                if i + 2 < NG:
                    run_to(gens[i + 2], "P1")
                d1 = i + 1 >= NG
                d2 = False
                while not (d1 and d2):
                    if not d2:
                        d2 = step(gens[i]) == "END"
                    if not d1:
                        d1 = step(gens[i + 1]) in ("P2", "END")
            P.barrier()
            ar.reset(m)

        def br_src_loader():
            bufs = Rot([ar.alloc("brc", [128, KC, 512], BF16) for _ in range(2)], "brc")

            def fn(ci, s, n):
                b, bk = bufs.next()
                P.dma(b[:, :, :n], brT[:, :, s:s + n], reads=[("brT", ci)], writes=[bk])
                return b, [bk]
            return fn

        def stage_hy_proj(l, hT):
            m = ar.mark()
            wrot = Rot([ar.alloc("wy", [128, KC, 512], BF16) for _ in range(2)], "wy")
            padl = Rot([ar.alloc("padl", [128, NLAT + 2], F32) for _ in range(2)], "padl")
            padc = Rot([ar.alloc("padc", [128, NCTX + 2], F32) for _ in range(2)], "padc")
            ubuf = Rot([ar.alloc("hyu", [128, NLAT], BF16) for _ in range(2)], "hyu")
            tmpu = ar.alloc("hytmp", [128, NLAT], F32)
            for r_ in (padl, padc):
                for t_, k_ in zip(r_.tiles, r_.keys):
                    P.memset(t_[:], 0.0, writes=[k_])
            cur = {}

            def conv(pad, padk, n, jg, off):
                w = lambda i: hysw[:, l, jg, i:i + 1]
                P.ts(tmpu[:, :n], pad[:, 0:n], w(0), hysb[:, l, jg:jg + 1], ALU.mult, ALU.add,
                     reads=[padk, "const"], writes=["hytmp"])
                P.stt(tmpu[:, :n], pad[:, 1:n + 1], w(1), tmpu[:, :n], ALU.mult, ALU.add,
                      reads=[padk, "const", "hytmp"], writes=["hytmp"])
                u, uk = ubuf.next()
                P.stt(u[:, :n], pad[:, 2:n + 2], w(2), tmpu[:, :n], ALU.mult, ALU.add,
                      reads=[padk, "const", "hytmp"], writes=[uk])
                dst = (hx1T, hx2T, hvT)[jg // 8]
                P.dma(dst[:, jg % 8, off:off + n], u[:, :n], reads=[uk], writes=["hyT"])

            def cons(jg, ci, s, n, ps, pk):
                if ci == 0:
                    cur["c"] = padc.next()
                    cur["l"] = padl.next()
                    pc, pck = cur["c"]
                    P.copy(pc[:, 1:1 + n], ps[:, :n], reads=[pk], writes=[pck], eng="scalar")
                    if l == 0:
                        conv(pc, pck, NCTX, jg, 0)
                else:
                    pl_, plk = cur["l"]
                    a = 1 + s - NCTX
                    P.copy(pl_[:, a:a + n], ps[:, :n], reads=[pk], writes=[plk], eng=("scalar" if ci % 2 else "vector"))
                    if ci == 4:
                        conv(pl_, plk, NLAT, jg, NCTX)
                yield

            proj_fm(w_in[l], OFF_HY, 3 * D, hT, cons, wrot)
            P.barrier()
            ar.reset(m)

        def stage_hyena_seq(l, n, off, tabs, Hs_d):
            NLT = n // 128
            NRT = 2 * NLT
            TCW = min(n, 512)
            NTC = n // TCW
            Ft_d, FTt_d, zfT_d, dec_d, arow_d = tabs
            import os
            if int(os.environ.get("HYCUT", "9")) == 0:
                return
            m = ar.mark()
            arow = ar.alloc("arow", [128, NRT], F32)
            P.dma(arow[:], arow_d, writes=["hyk"])
            mf = ar.mark()
            w1s = ar.alloc("w1s", [64, 64], F32)
            w2s = ar.alloc("w2s", [64, 64], F32)
            fc = ar.alloc("fc", [64, 4], F32)
            fb = ar.alloc("fb", [64, 2], F32)
            zfT = ar.alloc("zfT", [64, n], F32)
            h1T = ar.alloc("h1T", [64, n], F32)
            h2Tb = ar.alloc("h2Tb", [64, n], BF16)
            w3b = ar.alloc("w3b", [64, 4 * D], BF16)
            v32 = ar.alloc("v32", [64, 512], F32)
            kint = ar.alloc("kint", [64, 512], mybir.dt.int32)
            kf = ar.alloc("kf", [64, 512], F32)
            mg = ar.alloc("mg", [64, 512], F32)
            P.memset(w1s[:], 0.0, writes=["hyk"])
            P.memset(zfT[:], 0.0, writes=["hyk"])
            P.dma(w1s[0:33, :], hy_w1[l], writes=["hyk"])
            P.dma(w2s[:], hy_w2[l], writes=["hyk"])
            P.dma(fc[:], hy_fc[:, l, :], writes=["hyk"])
            P.dma(zfT[0:33, :], zfT_d, writes=["hyk"])
            for i_ in range(8):
                P.dma(w3b[:, i_ * 512:(i_ + 1) * 512], hy_w3[l][:, i_ * 512:(i_ + 1) * 512], writes=["hyk"], eng="gpsimd")
            P.tt(fb[:], fc[:, 2:4], fc[:, 0:2], ALU.mult, reads=["hyk"], writes=["fb"])

            def mlp_layer(lhsT, rhs, K, li, out):
                for pc in range(0, n, 512):
                    np_ = min(512, n - pc)
                    ps, pk = psf.next()
                    P.mm(ps[0:64, :np_], lhsT[0:K, :], rhs[0:K, pc:pc + np_], reads=["hyk", "h1T"], writes=[pk])
                    P.ts(v32[:, :np_], ps[0:64, :np_], fc[:, li:li + 1], fb[:, li:li + 1], ALU.mult, ALU.add,
                         reads=[pk, "hyk", "fb"], writes=["v32"])
                    P.ts(v32[:, :np_], v32[:, :np_], 1.0 / (2 * math.pi), 16.0, ALU.mult, ALU.add, reads=["v32"], writes=["v32"])
                    P.copy(kint[:, :np_], v32[:, :np_], reads=["v32"], writes=["kint"])
                    P.copy(kf[:, :np_], kint[:, :np_], reads=["kint"], writes=["kf"])
                    P.tt(v32[:, :np_], v32[:, :np_], kf[:, :np_], ALU.subtract, reads=["v32", "kf"], writes=["v32"])
                    P.ts(mg[:, :np_], v32[:, :np_], 0.5, None, ALU.is_gt, reads=["v32"], writes=["mg"])
                    P.tt(v32[:, :np_], v32[:, :np_], mg[:, :np_], ALU.subtract, reads=["v32", "mg"], writes=["v32"])
                    P.act(out[:, pc:pc + np_], v32[:, :np_], AF.Sin, scale=float(2 * math.pi), reads=["v32"],
                          writes=["h1T" if out is h1T else "h2Tb"])

            import os
            HYCUT = int(os.environ.get("HYCUT", "9"))
            mlp_layer(w1s, zfT, 64, 0, h1T)
            mlp_layer(w2s, h1T, 64, 1, h2Tb)
            if HYCUT <= 1:
                P.barrier(); ar.reset(m); return
            fsum = ar.alloc("fsum", [128, 4, NLT, 512], BF16)
            fdiff = ar.alloc("fdiff", [128, 4, NLT, 512], BF16)
            rn = ar.alloc("rn", [128, 4, 512], F32)
            decb = Rot([ar.alloc("decb", [128, 512], F32) for _ in range(2)], "decb")
            fwb = Rot([ar.alloc("fwb", [128, 512], F32) for _ in range(2)], "fwb")
            bwb = Rot([ar.alloc("bwb", [128, 512], F32) for _ in range(2)], "bwb")
            a1b = Rot([ar.alloc("a1b", [128, 512], BF16) for _ in range(2)], "a1b")
            a2b = Rot([ar.alloc("a2b", [128, 512], BF16) for _ in range(2)], "a2b")
            ps5 = Rot(psf_t[0:5], "psf")
            psN = psf_t[5]
            for cmb in range(4):
                o, cb = cmb // 2, cmb % 2
                for lt in range(NLT):
                    dc, dck = decb.next()
                    P.dma(dc[:], dec_d[lt * 128:(lt + 1) * 128, cb * 512:(cb + 1) * 512], writes=[dck])
                    psF, pkF = ps5.next()
                    psB, pkB = ps5.next()
                    c0 = o * 2 * D + cb * 512
                    P.mm(psF[:], h2Tb[:, lt * 128:(lt + 1) * 128], w3b[:, c0:c0 + 512], reads=["h2Tb", "hyk"], writes=[pkF])
                    P.mm(psB[:], h2Tb[:, lt * 128:(lt + 1) * 128], w3b[:, c0 + D:c0 + D + 512], reads=["h2Tb", "hyk"], writes=[pkB])
                    fw, fwk = fwb.next()
                    bw, bwk = bwb.next()
                    P.tt(fw[:], psF[:], dc[:], ALU.mult, reads=[pkF, dck], writes=[fwk])
                    P.tt(bw[:], psB[:], dc[:], ALU.mult, reads=[pkB, dck], writes=[bwk])
                    if lt == 0:
                        P.memset(bw[0:1, :], 0.0, writes=[bwk])
                    P.tt(fsum[:, cmb, lt, :], fw[:], bw[:], ALU.add, reads=[fwk, bwk], writes=["fsum"], eng="gpsimd")
                    P.tt(fdiff[:, cmb, lt, :], fw[:], bw[:], ALU.subtract, reads=[fwk, bwk], writes=["fdiff"], eng="gpsimd")
                    a1, a1k = a1b.next()
                    a2, a2k = a2b.next()
                    P.act(a1[:], fw[:], AF.Abs, reads=[fwk], writes=[a1k])
                    P.act(a2[:], bw[:], AF.Abs, reads=[bwk], writes=[a2k])
                    P.mm(psN[:], onesb[:], a1[:], start=(lt == 0), stop=False, reads=[a1k, "const"], writes=["psf#5"])
                    P.mm(psN[:], onesb[:], a2[:], start=False, stop=(lt == NLT - 1), reads=[a2k, "const"], writes=["psf#5"])
                P.ts(rn[:, cmb, :], psN[:], EPS, None, ALU.add, reads=["psf#5"], writes=["rn"])
                P.recip(rn[:, cmb, :], rn[:, cmb, :], reads=["rn"], writes=["rn"])
            if HYCUT <= 2:
                P.barrier(); ar.reset(m); return
            ftb = Rot([ar.alloc("ftb", [128, NLT, 128], BF16) for _ in range(2)], "ftb")
            hst = Rot([ar.alloc("hst", [128, 512], BF16) for _ in range(3)], "hst")
            for rt in range(NRT):
                ft, ftk = ftb.next()
                P.dma(ft[:], Ft_d[rt], writes=[ftk])
                for cmb in range(4):
                    o, cb = cmb // 2, cmb % 2
                    src, srck = (fsum, "fsum") if rt < NLT else (fdiff, "fdiff")
                    ps, pk = psf.next()
                    for lt in range(NLT):
                        P.mm(ps[:], ft[:, lt, :], src[:, cmb, lt, :], start=(lt == 0), stop=(lt == NLT - 1),
                             reads=[ftk, srck], writes=[pk])
                    hs, hsk = hst.next()
                    P.stt(hs[:], ps[:], arow[:, rt:rt + 1], rn[:, cmb, :], ALU.mult, ALU.mult, reads=[pk, "hyk", "rn"], writes=[hsk])
                    if rt == NLT:
                        ps2, pk2 = psf.next()
                        for lt in range(NLT):
                            P.mm(ps2[0:1, :], ft[:, lt, 0:1], fsum[:, cmb, lt, :], start=(lt == 0), stop=(lt == NLT - 1),
                                 reads=[ftk, "fsum"], writes=[pk2])
                        P.stt(hs[0:1, :], ps2[0:1, :], arow[0:1, rt:rt + 1], rn[0:1, cmb, :], ALU.mult, ALU.mult,
                              reads=[pk2, "hyk", "rn"], writes=[hsk])
                    P.dma(Hs_d[rt * 128:(rt + 1) * 128, o, cb * 512:(cb + 1) * 512], hs[:], reads=[hsk], writes=["Hs"])
            P.barrier()
            ar.reset(mf)
            if HYCUT <= 3:
                P.barrier(); ar.reset(m); return
            Yc = ar.alloc("Yc", [128, NRT, D], BF16)
            for o in range(2):
                srcT = hvT if o == 0 else hz1T
                gateT = hx1T if o == 0 else hx2T
                dstT = hz1T if o == 0 else brT
                m2 = ar.mark()
                z = ar.alloc("z", [128, NLT, D], BF16)
                scb = Rot([ar.alloc("scb", [128, KC, TCW], BF16) for _ in range(2)], "scb")
                for tc in range(NTC):
                    sc_, sck = scb.next()
                    P.dma(sc_[:], srcT[:, :, off + tc * TCW:off + (tc + 1) * TCW], writes=[sck])
                    for tt_ in range(TCW // 128):
                        lt = tc * (TCW // 128) + tt_
                        pb, pbk = psb.next()
                        for k in range(KC):
                            P.tr(pb[:, k * 128:(k + 1) * 128], sc_[:, k, tt_ * 128:(tt_ + 1) * 128], idb[:],
                                 reads=[sck, "const"], writes=[pbk])
                        P.copy(z[:, lt, :], pb[:], reads=[pbk], writes=["z"], eng=("vector" if lt % 2 else "scalar"))
                fab = Rot([ar.alloc("fab", [128, NLT, 128], BF16) for _ in range(4)], "fab")
                hab = Rot([ar.alloc("hab", [128, D], BF16) for _ in range(4)], "hab")
                tb = [Rot([ar.alloc("tb%d" % i, [128, 512], F32) for _ in range(2)], "tb%d" % i) for i in range(4)]
                for i in range(NLT):
                    FA, FAk = fab.next()
                    FB, FBk = fab.next()
                    HA, HAk = hab.next()
                    HB, HBk = hab.next()
                    P.dma(FA[:], Ft_d[i], writes=[FAk])
                    P.dma(FB[:], Ft_d[NLT + i], writes=[FBk])
                    P.dma(HA[:], Hs_d[i * 128:(i + 1) * 128, o, :], writes=[HAk])
                    P.dma(HB[:], Hs_d[(NLT + i) * 128:(NLT + i + 1) * 128, o, :], writes=[HBk])
                    for hf_ in range(2):
                        cs_ = slice(hf_ * 512, (hf_ + 1) * 512)
                        psA, pkA = psf.next()
                        psB, pkB = psf.next()
                        for lt in range(NLT):
                            P.mm(psA[:], FA[:, lt, :], z[:, lt, cs_], start=(lt == 0), stop=(lt == NLT - 1),
                                 reads=[FAk, "z"], writes=[pkA])
                        for lt in range(NLT):
                            P.mm(psB[:], FB[:, lt, :], z[:, lt, cs_], start=(lt == 0), stop=(lt == NLT - 1),
                                 reads=[FBk, "z"], writes=[pkB])
                        (t1, t1k), (t2, t2k), (t3, t3k), (t4, t4k) = [r_.next() for r_ in tb]
                        P.tt(t1[:], psA[:], HA[:, cs_], ALU.mult, reads=[pkA, HAk], writes=[t1k])
                        P.tt(t2[:], psB[:], HB[:, cs_], ALU.mult, reads=[pkB, HBk], writes=[t2k])
                        P.tt(t3[:], psA[:], HB[:, cs_], ALU.mult, reads=[pkA, HBk], writes=[t3k])
                        P.tt(t4[:], psB[:], HA[:, cs_], ALU.mult, reads=[pkB, HAk], writes=[t4k])
                        P.tt(Yc[:, i, cs_], t1[:], t2[:], ALU.subtract, reads=[t1k, t2k], writes=["Yc"], eng="gpsimd")
                        P.tt(Yc[:, NLT + i, cs_], t3[:], t4[:], ALU.add, reads=[t3k, t4k], writes=["Yc"], eng="gpsimd")
                        if i == 0:
                            P.tt(Yc[0:1, 0, cs_], psA[0:1, :], HA[0:1, cs_], ALU.mult, reads=[pkA, HAk, "Yc"], writes=["Yc"])
                            P.tt(Yc[0:1, NLT, cs_], psB[0:1, :], HB[0:1, cs_], ALU.mult, reads=[pkB, HBk, "Yc"], writes=["Yc"])
                P.barrier()
                ar.reset(m2)
                ftt = Rot([ar.alloc("ftt", [128, NRT, TCW], BF16) for _ in range(2)], "ftt")
                zib = Rot([ar.alloc("zib", [128, KC, TCW], BF16) for _ in range(2)], "zib")
                xgb = Rot([ar.alloc("xgb", [128, KC, TCW], BF16) for _ in range(2)], "xgb")
                ocb = Rot([ar.alloc("ocb", [128, KC, TCW], BF16) for _ in range(2)], "ocb")
                t5b = Rot([ar.alloc("t5b", [128, TCW], F32) for _ in range(2)], "t5b")
                for tc in range(NTC):
                    ft, ftk = ftt.next()
                    zi_, zik = zib.next()
                    xg, xgk = xgb.next()
                    oc, ock = ocb.next()
                    sl = slice(off + tc * TCW, off + (tc + 1) * TCW)
                    for hh in range(2):
                        P.dma(ft[:, hh * NLT:(hh + 1) * NLT, :], FTt_d[tc, :, hh * NLT:(hh + 1) * NLT, :], writes=[(ftk, hh)])
                    P.dma(zi_[:], srcT[:, :, sl], writes=[zik])
                    P.dma(xg[:], gateT[:, :, sl], writes=[xgk])
                    for j in range(KC):
                        ps, pk = psf.next()
                        for rt in range(NRT):
                            P.mm(ps[:, :TCW], Yc[:, rt, j * 128:(j + 1) * 128], ft[:, rt, :], start=(rt == 0),
                                 stop=(rt == NRT - 1), reads=["Yc", (ftk, rt // NLT)], writes=[pk])
                        t5, t5k = t5b.next()
                        P.stt(t5[:], zi_[:, j, :], hyb[:, l, o, j:j + 1], ps[:, :TCW], ALU.mult, ALU.add,
                              reads=[zik, "const", pk], writes=[t5k])
                        P.tt(oc[:, j, :], t5[:], xg[:, j, :], ALU.mult, reads=[t5k, xgk], writes=[ock], eng="gpsimd")
                    ci = 0 if off == 0 else 1 + tc
                    P.dma(dstT[:, :, sl], oc[:], reads=[ock], writes=[("brT", ci) if o == 1 else "hz1T"])
                P.barrier()
                ar.reset(m2)
            P.barrier()
            ar.reset(m)

        stages = stages or ("adaln", "x0", "n1", "gates", "att", "hgrn", "hy", "wout", "mlp", "final")
        if "adaln" in stages:
            stage_adaln()
        if "x0" in stages:
            stage_x0()
        for l in range(DEPTH):
            lm = ar.mark()
            hT = ar.alloc("hT", [128, KC, TOK], BF16)
            if "n1" in stages:
                stage_norm1(l, hT)
            if "gates" in stages:
                stage_gates(l, hT)
            if "att" in stages:
                am = ar.mark()
                attT = ar.alloc("attT", [128, KC, TOK], BF16)
                stage_attention(l, hT, attT)
                branch_proj(l, 0, lambda ci, s, n: (attT[:, :, s:s + n], [("attT", ci)]), True)
                ar.reset(am)
            if "hy" in stages:
                stage_hy_proj(l, hT)
            if "hgrn" in stages:
                stage_hgrn(l, hT)
                bm = ar.mark()
                branch_proj(l, 1, br_src_loader(), "att" not in stages)
                ar.reset(bm)
            P.barrier()
            ar.reset(lm)
            if "hy" in stages:
                if l == 0:
                    stage_hyena_seq(l, NCTX, 0, tabs["c"], Hs_c)
                stage_hyena_seq(l, NLAT, NCTX, tabs["l"], Hs_l)
                bm = ar.mark()
                branch_proj(l, 2, br_src_loader(), not ("att" in stages or "hgrn" in stages))
                ar.reset(bm)
            if "wout" in stages:
                stage_wout(l)
            if "mlp" in stages:
                stage_mlp(l)
            if stages and "stop_l0" in stages:
                break
        if "final" in stages:
            stage_final()
        if dbg:
            P.barrier()
            for name in dbg:
                if name in scr:
                    a, shp, dt = scr[name]
                    tw = nc.dram_tensor("dbg_" + name, shp, dt, kind="ExternalOutput").ap()
                    P.dma(tw, a, final=True)
        P.emit(sems_eng, sems_dma)
    return nc


def _bf(a):
    return np.asarray(a, dtype=np.float32).astype(ml_dtypes.bfloat16)


def _col(v, k=KC):
    return np.ascontiguousarray(np.asarray(v, np.float32).reshape(k, 128).T)


def make_consts():
    c = {}
    c["c_idf"] = np.eye(128, dtype=np.float32)
    c["c_idb"] = _bf(np.eye(128))
    c["c_onesb"] = _bf(np.ones((128, 128)))
    rmm = np.zeros((128, 128), np.float32)
    for base in (0, 64):
        for i in range(32):
            rmm[base + i + 32, base + i] = -1.0
            rmm[base + i, base + i + 32] = 1.0
    c["c_rm"] = _bf(rmm)
    half = 64
    inv = (10000.0 ** (-np.arange(0, half, 2, dtype=np.float32) / half)).astype(np.float32)
    t = np.arange(NLAT)
    row = (t // 64).astype(np.float32)[:, None] * inv
    colv = (t % 64).astype(np.float32)[:, None] * inv
    ang = np.concatenate([row, row, colv, colv], axis=-1).astype(np.float32)
    mid = 31
    dm = np.zeros((128, 2, 128), np.float32)
    dsel = np.zeros((128, 2, 6), np.float32)
    msk = np.zeros((128, 2, 128), np.float32)
    for cc in range(2):
        for sp in range(64):
            for s_ in range(64):
                dm[cc * 64 + sp, 0, cc * 64 + s_] = float(sp <= s_) - float(sp <= mid)
                dm[cc * 64 + sp, 1, cc * 64 + s_] = float(sp >= s_) - float(sp >= mid)
                msk[cc * 64 + sp, 0, cc * 64 + s_] = float(sp <= s_)
                msk[cc * 64 + sp, 1, cc * 64 + s_] = float(sp >= s_)
            dsel[cc * 64 + sp, 0, cc * 3 + 0] = float(sp <= mid)
            dsel[cc * 64 + sp, 0, cc * 3 + 1] = 1.0
            dsel[cc * 64 + sp, 0, cc * 3 + 2] = float(sp > mid)
            dsel[cc * 64 + sp, 1, cc * 3 + 0] = float(sp >= mid)
            dsel[cc * 64 + sp, 1, cc * 3 + 1] = 1.0
            dsel[cc * 64 + sp, 1, cc * 3 + 2] = float(sp < mid)
    pos = np.arange(128) % 64
    cm = np.zeros((128, 2, 3, 128), np.float32)
    cm[:, 0, 0] = (pos <= mid); cm[:, 0, 1] = (pos > mid); cm[:, 0, 2] = (pos >= mid)
    cm[:, 1, 0] = (pos >= mid); cm[:, 1, 1] = (pos < mid); cm[:, 1, 2] = (pos <= mid)
    c["c_cmk"] = _bf(cm)
    dm2 = np.zeros((128, 2, 128), np.float32)
    for cc in range(2):
        for sp in range(64):
            for s_ in range(64):
                dm2[cc * 64 + sp, 0, cc * 64 + s_] = float(sp > mid) - dm[cc * 64 + sp, 0, cc * 64 + s_]
                dm2[cc * 64 + sp, 1, cc * 64 + s_] = float(sp < mid) - dm[cc * 64 + sp, 1, cc * 64 + s_]
    c["c_dm2"] = dm2
    c["c_dm"] = dm
    c["c_dsel"] = dsel
    c["c_msk"] = np.ascontiguousarray(np.tile(msk, (1, 1, 4)))
    c["c_cos"] = np.ascontiguousarray(np.cos(ang).T.astype(np.float32))
    c["c_sin"] = np.ascontiguousarray(np.sin(ang).T.astype(np.float32))
    return c


def make_hy_tables(n):
    nlt = n // 128
    nrt = 2 * nlt
    tcw = min(n, 512)
    N2 = 2 * n
    t = np.arange(n, dtype=np.int64)[:, None]
    r = np.arange(N2, dtype=np.int64)[None, :]
    f = np.where(r <= n, r, r - n)
    ang = 2.0 * np.pi * ((t * f) % N2).astype(np.float64) / N2
    F = np.where(r <= n, np.cos(ang), np.sin(ang)).astype(np.float32)
    Ft = F.reshape(nlt, 128, nrt, 128).transpose(2, 1, 0, 3)
    FTt = F.reshape(n // tcw, tcw, nrt, 128).transpose(0, 3, 2, 1)
    tt = np.linspace(0.0, 1.0, n, dtype=np.float32)[:, None]
    w = (np.float32(2.0 * math.pi / n) * np.arange(n, dtype=np.float32))[:, None]
    fr = np.linspace(1e-4, 15, 16, dtype=np.float32)[None, :]
    zf = np.concatenate([tt, np.cos(fr * w), -np.sin(fr * w)], axis=-1).astype(np.float32)
    mind, maxd = math.log(1e-2) / 1.5, math.log(1e-2) / 0.3
    deltas = np.abs(np.linspace(mind, maxd, D, dtype=np.float32))
    dec = np.exp(-tt * deltas).astype(np.float32)
    a = np.full(N2, 2.0 / N2, np.float32)
    a[0] = 1.0 / N2
    a[n] = 1.0 / N2
    return {"Ft": _bf(np.ascontiguousarray(Ft)), "FTt": _bf(np.ascontiguousarray(FTt)),
            "zfT": np.ascontiguousarray(zf.T), "dec": dec, "arow": np.ascontiguousarray(a.reshape(nrt, 128).T)}


def make_in_maps(inputs):
    f = lambda k: np.ascontiguousarray(np.asarray(inputs[k], dtype=np.float32))
    shared = dict(make_consts())
    shared["ada_w"] = f("ada_w")
    ab = f("ada_b")
    shared["ada_bc"] = np.ascontiguousarray(np.stack([_col(ab[l], 48) for l in range(DEPTH)], axis=1))
    shared["g1c"] = np.ascontiguousarray(np.stack([_col(f("norm1_g")[l]) for l in range(DEPTH)], axis=1))
    shared["g2c"] = np.ascontiguousarray(np.stack([_col(f("norm2_g")[l]) for l in range(DEPTH)], axis=1))
    shared["w_in"] = f("w_in")
    shared["qkg"] = np.ascontiguousarray(np.stack([f("q_norm_g"), f("k_norm_g")], axis=-1).transpose(1, 0, 2))
    shared["hgn_row"] = np.ascontiguousarray(np.tile(f("hg_norm_g"), (1, 8)))
    shared["hg_lb"] = f("hg_lb_raw")
    for nm, n_ in (("l", NLAT), ("c", NCTX)):
        for k_, v_ in make_hy_tables(n_).items():
            shared[k_ + "_" + nm] = v_
    sw = f("hy_short_w")
    shared["hysw"] = np.ascontiguousarray(sw.reshape(DEPTH, 3, 24, 128).transpose(3, 0, 2, 1))
    shared["hysb"] = np.ascontiguousarray(f("hy_short_b").reshape(DEPTH, 24, 128).transpose(2, 0, 1))
    shared["hyb"] = np.ascontiguousarray(f("hy_bias").reshape(DEPTH, 2, KC, 128).transpose(3, 0, 1, 2))
    shared["hy_w1"] = f("hy_filt_w1")
    shared["hy_w2"] = f("hy_filt_w2")
    shared["hy_w3"] = f("hy_filt_w3")
    fq = f("hy_freq")
    shared["hy_fc"] = np.ascontiguousarray(np.stack([fq[:, 0], fq[:, 1], f("hy_filt_b1"), f("hy_filt_b2")], axis=-1).transpose(1, 0, 2))
    shared["w_branch"] = f("w_branch")
    shared["w_out"] = f("w_out")
    shared["w_mlp1"] = f("w_mlp1")
    shared["w_mlp2"] = f("w_mlp2")
    x, c, ctx, c_ctx = f("x"), f("c"), f("ctx"), f("c_ctx")
    maps = []
    for b in range(8):
        m = dict(shared)
        m["x_b"] = x[b]
        m["ctx_b"] = ctx[b]
        m["ccol"] = np.ascontiguousarray(np.stack([_col(c[b]), _col(c_ctx)], axis=-1))
        maps.append(m)
    return maps


_NC_CACHE = {}


def kernel(**inputs):
    if "nc" not in _NC_CACHE:
        _NC_CACHE["nc"] = build_program()
    nc = _NC_CACHE["nc"]
    maps = make_in_maps(inputs)
    res = run_bass_kernel_spmd(nc, maps, core_ids=list(range(8)))
    return np.stack([np.asarray(r["out"], dtype=np.float32) for r in res.results], axis=0)
```

```python
import contextlib
import math
import numpy as np
import ml_dtypes
import concourse.bass as bass
import concourse.mybir as mybir
from concourse.bass_utils import run_bass_kernel_spmd

F32 = mybir.dt.float32
BF16 = mybir.dt.bfloat16
AF = mybir.ActivationFunctionType
ALU = mybir.AluOpType
AX = mybir.AxisListType

ENGS = ("tensor", "vector", "scalar", "gpsimd", "sync")
N_DMA_SEMS = 24
SAME_ENGINE_SYNC = True

D = 1024
KC = 8
NLAT = 2048
NCTX = 256
TOK = NLAT + NCTX
NTILE = TOK // 128
DEPTH = 2
DFF = 4096
IN_TOTAL = 12800
OFF_Q, OFF_K, OFF_V = 0, 1024, 1280
OFF_HQ, OFF_HF, OFF_HB, OFF_HI, OFF_HG = 1536, 2560, 3584, 4608, 5632
OFF_HY = 6656
OFF_GATE = 9728
EPS = 1e-6
CH = [(0, 256), (256, 512), (768, 512), (1280, 512), (1792, 512)]
SB_BASE = 16512
SB_LIMIT = 229376


class Op:
    __slots__ = ("eng", "fn", "deps", "dma", "sem", "val", "needed", "idx")

    def __init__(self, eng, fn, deps, dma):
        self.eng, self.fn, self.deps, self.dma = eng, fn, deps, dma
        self.sem = None
        self.val = None
        self.needed = False


class Prog:
    def __init__(self, nc):
        self.nc = nc
        self.q = {e: [] for e in ENGS}
        self.last_write = {}
        self.readers = {}
        self.dma_count = 0
        self.dma_hist = [[], []]
        self.final_ops = []

    def op(self, eng, fn, reads=(), writes=(), dma=False, final=False):
        reads = list(reads) + ["ALL"]
        deps = set()
        for r in reads:
            w = self.last_write.get(r)
            if w is not None:
                deps.add(w)
        for wkey in writes:
            w = self.last_write.get(wkey)
            if w is not None:
                deps.add(w)
            for o in self.readers.get(wkey, {}).values():
                deps.add(o)
        o = Op(eng, fn, deps, dma)
        if dma:
            pool = 1 if eng == "gpsimd" else 0
            hist = self.dma_hist[pool]
            n = len(hist)
            o.sem = pool * N_DMA_SEMS + n % N_DMA_SEMS
            o.val = 16 * (n // N_DMA_SEMS + 1)
            if n >= N_DMA_SEMS:
                deps.add(hist[n - N_DMA_SEMS])
            hist.append(o)
        o.idx = len(self.q[eng])
        self.q[eng].append(o)
        rk = ("dma", id(o)) if dma else eng
        for r in reads:
            self.readers.setdefault(r, {})[rk] = o
        for wkey in writes:
            self.last_write[wkey] = o
            self.readers[wkey] = {}
        if final:
            self.final_ops.append(o)
        return o

    def barrier(self):
        self.op("sync", lambda e: e.nop(), reads=(), writes=["ALL"])

    def dma(self, out, in_, reads=(), writes=(), eng="sync", final=False, **kw):
        return self.op(eng, lambda e: e.dma_start(out=out, in_=in_, **kw), reads, writes, dma=True, final=final)

    def mm(self, out, lhsT, rhs, start=True, stop=True, reads=(), writes=()):
        return self.op("tensor", lambda e: e.matmul(out, lhsT, rhs, start=start, stop=stop), reads, writes)

    def tr(self, out, in_, ident, reads=(), writes=()):
        return self.op("tensor", lambda e: e.transpose(out, in_, ident), reads, writes)

    def act(self, out, in_, func, reads=(), writes=(), eng="scalar", **kw):
        return self.op(eng, lambda e: e.activation(out, in_, func, **kw), reads, writes)

    def tt(self, out, in0, in1, op, reads=(), writes=(), eng="vector"):
        return self.op(eng, lambda e: e.tensor_tensor(out, in0, in1, op), reads, writes)

    def ts(self, out, in0, s1, s2, op0, op1=None, reads=(), writes=(), eng="vector", **kw):
        if op1 is None:
            return self.op(eng, lambda e: e.tensor_scalar(out, in0, s1, s2, op0, **kw), reads, writes)
        return self.op(eng, lambda e: e.tensor_scalar(out, in0, s1, s2, op0, op1, **kw), reads, writes)

    def stt(self, out, in0, scalar, in1, op0, op1, reads=(), writes=(), eng="vector"):
        return self.op(eng, lambda e: e.scalar_tensor_tensor(out, in0, scalar, in1, op0, op1), reads, writes)

    def copy(self, out, in_, reads=(), writes=(), eng="vector"):
        if eng == "scalar":
            return self.op(eng, lambda e: e.copy(out, in_), reads, writes)
        return self.op(eng, lambda e: e.tensor_copy(out, in_), reads, writes)

    def recip(self, out, in_, reads=(), writes=()):
        return self.op("vector", lambda e: e.reciprocal(out, in_), reads, writes)

    def memset(self, ap, val, writes=(), eng="vector"):
        return self.op(eng, lambda e: e.memset(ap, val), (), writes)

    def emit(self, sems_eng, sems_dma):
        nc = self.nc
        for e in ENGS:
            for o in self.q[e]:
                for d in o.deps:
                    d.needed = True
        for o in self.final_ops:
            o.needed = True
        for e in ENGS:
            cnt = 0
            for o in self.q[e]:
                if o.dma:
                    o.sem = sems_dma[o.sem]
                    continue
                o.sem = sems_eng[e]
                if o.needed:
                    cnt += 1
                    o.val = cnt
        final_ops = self.final_ops

        def run_engine(ename, eobj):
            waited = {}
            for o in self.q[ename]:
                need = {}
                for d in o.deps:
                    if d.eng == ename and not d.dma:
                        if ename == "tensor" or not SAME_ENGINE_SYNC:
                            continue
                    k = id(d.sem)
                    if k not in need or need[k][1] < d.val:
                        need[k] = (d.sem, d.val)
                for k, (s, v) in need.items():
                    if waited.get(k, 0) >= v:
                        continue
                    eobj.wait_ge(s, v)
                    waited[k] = v
                ins = o.fn(eobj)
                if o.dma:
                    ins.then_inc(o.sem, 16)
                elif o.needed:
                    ins.then_inc(o.sem, 1)
            if ename == "sync":
                for o in final_ops:
                    eobj.wait_ge(o.sem, o.val)

        with nc.Block() as block:
            @block.tensor
            def _(e):
                run_engine("tensor", e)

            @block.vector
            def _(e):
                run_engine("vector", e)

            @block.scalar
            def _(e):
                run_engine("scalar", e)

            @block.gpsimd
            def _(e):
                run_engine("gpsimd", e)

            @block.sync
            def _(e):
                run_engine("sync", e)


def _dtbytes(dt):
    return 2 if dt == BF16 else 4


class Arena:
    def __init__(self, nc):
        self.nc = nc
        self.off = SB_BASE
        self.n = 0
        self.peak = 0

    def alloc(self, name, shape, dt):
        sz = int(np.prod(shape[1:])) * _dtbytes(dt)
        off = (self.off + 63) // 64 * 64
        self.n += 1
        t = self.nc.alloc_sbuf_tensor_at("%s_%d" % (name, self.n), list(shape), dt, offset=off)
        self.off = off + sz
        self.peak = max(self.peak, self.off)
        assert self.off <= SB_LIMIT, ("SBUF overflow", name, self.off)
        return t

    def mark(self):
        return self.off

    def reset(self, m):
        self.off = m


class Rot:
    def __init__(self, tiles, name):
        self.tiles = tiles
        self.keys = ["%s#%d" % (name, i) for i in range(len(tiles))]
        self.i = -1

    def next(self):
        self.i = (self.i + 1) % len(self.tiles)
        return self.tiles[self.i], self.keys[self.i]


def build_program(dbg=None, stages=None):
    dbg = dbg or ()
    nc = bass.Bass("TRN2", target_bir_lowering=False)
    P = Prog(nc)
    ar = Arena(nc)

    def din(name, shape, dt=F32):
        return nc.dram_tensor(name, list(shape), dt, kind="ExternalInput").ap()

    scr = {}

    def dscr(name, shape, dt):
        a = nc.dram_tensor(name, list(shape), dt, kind="Internal").ap()
        scr[name] = (a, list(shape), dt)
        return a

    x_b = din("x_b", [NLAT, D])
    ctx_b = din("ctx_b", [NCTX, D])
    ccol = din("ccol", [128, KC, 2])
    ada_w = din("ada_w", [DEPTH, D, 6 * D])
    ada_bc = din("ada_bc", [128, DEPTH, 48])
    g1c = din("g1c", [128, DEPTH, KC])
    g2c = din("g2c", [128, DEPTH, KC])
    w_in = din("w_in", [DEPTH, D, IN_TOTAL])
    qkg = din("qkg", [128, DEPTH, 2])
    w_branch = din("w_branch", [DEPTH, 3, D, D])
    w_out = din("w_out", [DEPTH, D, D])
    w_mlp1 = din("w_mlp1", [DEPTH, D, DFF])
    w_mlp2 = din("w_mlp2", [DEPTH, DFF, D])
    c_idf = din("c_idf", [128, 128])
    c_idb = din("c_idb", [128, 128], BF16)
    c_onesb = din("c_onesb", [128, 128], BF16)
    c_rm = din("c_rm", [128, 128], BF16)
    c_cos = din("c_cos", [128, NLAT])
    c_sin = din("c_sin", [128, NLAT])
    c_dm = din("c_dm", [128, 2, 128])
    c_dm2 = din("c_dm2", [128, 2, 128])
    c_dsel = din("c_dsel", [128, 2, 6])
    c_msk = din("c_msk", [128, 2, 512])
    c_cmk = din("c_cmk", [128, 2, 3, 128], BF16)
    hgn_row = din("hgn_row", [DEPTH, D])
    hg_lb = din("hg_lb", [DEPTH, 2, D])
    hysw_d = din("hysw", [128, DEPTH, 24, 3])
    hysb_d = din("hysb", [128, DEPTH, 24])
    hyb_d = din("hyb", [128, DEPTH, 2, KC])
    hy_w1 = din("hy_w1", [DEPTH, 33, 64])
    hy_w2 = din("hy_w2", [DEPTH, 64, 64])
    hy_w3 = din("hy_w3", [DEPTH, 64, 4 * D])
    hy_fc = din("hy_fc", [64, DEPTH, 4])
    tabs = {}
    for nm, n_ in (("l", NLAT), ("c", NCTX)):
        nlt = n_ // 128
        tcw = min(n_, 512)
        tabs[nm] = (din("Ft_" + nm, [2 * nlt, 128, nlt, 128], BF16), din("FTt_" + nm, [n_ // tcw, 128, 2 * nlt, tcw], BF16),
                    din("zfT_" + nm, [33, n_]), din("dec_" + nm, [n_, D]), din("arow_" + nm, [128, 2 * nlt]))
    out_d = nc.dram_tensor("out", [NLAT, D], F32, kind="ExternalOutput").ap()

    xT = dscr("xT", [128, KC, TOK], F32)
    hTd = dscr("hTd", [128, KC, TOK], BF16)
    gT = dscr("gT", [3, 128, KC, TOK], BF16)
    mT = dscr("mT", [128, KC, TOK], BF16)
    brT = dscr("brT", [128, KC, TOK], BF16)
    hx1T = dscr("hx1T", [128, KC, TOK], BF16)
    hx2T = dscr("hx2T", [128, KC, TOK], BF16)
    hvT = dscr("hvT", [128, KC, TOK], BF16)
    hz1T = dscr("hz1T", [128, KC, TOK], BF16)
    Hs_l = dscr("Hs_l", [2 * NLAT, 2, D], BF16)
    Hs_c = dscr("Hs_c", [2 * NCTX, 2, D], BF16)
    zq = dscr("zq", [TOK, D], BF16)
    zi = dscr("zi", [TOK, D], BF16)
    zg = dscr("zg", [TOK, D], BF16)
    zf = dscr("zf", [TOK, D], F32)
    zb = dscr("zb", [TOK, D], F32)

    dumped = {}

    def dump(name, ap, shape, dt=F32, reads=()):
        if ("D:" + name) not in dbg or name in dumped:
            return
        dumped[name] = 1
        tw = nc.dram_tensor("dbg_" + name, list(shape), dt, kind="ExternalOutput").ap()
        P.dma(tw, ap, reads=list(reads), final=True)

    with contextlib.ExitStack() as st:
        sems_eng = {e: st.enter_context(nc.semaphore("s_" + e)) for e in ENGS}
        sems_dma = [st.enter_context(nc.semaphore("d%d" % i)) for i in range(2 * N_DMA_SEMS)]
        psf_t = [nc.alloc_psum_tensor("psf%d" % i, [128, 512], F32) for i in range(6)]
        psb_t = [nc.alloc_psum_tensor("psb%d" % i, [128, 1024], BF16) for i in range(2)]
        psf = Rot(psf_t, "psf")
        psb = Rot(psb_t, "psb")

        idf = ar.alloc("idf", [128, 128], F32)
        idb = ar.alloc("idb", [128, 128], BF16)
        onesb = ar.alloc("onesb", [128, 128], BF16)
        rm = ar.alloc("rm", [128, 128], BF16)
        modT = ar.alloc("modT", [128, DEPTH, 48, 2], F32)
        adab = ar.alloc("adab", [128, DEPTH, 48], F32)
        g1s = ar.alloc("g1s", [128, DEPTH, KC], F32)
        g2s = ar.alloc("g2s", [128, DEPTH, KC], F32)
        A1 = ar.alloc("A1", [128, DEPTH, KC, 2], F32)
        A2 = ar.alloc("A2", [128, DEPTH, KC, 2], F32)
        qkgs = ar.alloc("qkgs", [128, DEPTH, 2], F32)
        epsc = ar.alloc("epsc", [128, 1], F32)
        P.memset(epsc[:], EPS, writes=["const"])
        hysw = ar.alloc("hysw", [128, DEPTH, 24, 3], F32)
        hysb = ar.alloc("hysb", [128, DEPTH, 24], F32)
        hyb = ar.alloc("hyb", [128, DEPTH, 2, KC], F32)
        for t_, d_ in ((idf, c_idf), (idb, c_idb), (onesb, c_onesb), (rm, c_rm), (adab, ada_bc),
                       (g1s, g1c), (g2s, g2c), (qkgs, qkg), (hysw, hysw_d), (hysb, hysb_d), (hyb, hyb_d)):
            P.dma(t_[:], d_, writes=["const"])

        def wcast_load(dst, src, key, eng="gpsimd"):
            return P.dma(dst, src, writes=[key], eng=eng)

        def stage_adaln():
            m = ar.mark()
            cs = ar.alloc("cs", [128, KC, 2], F32)
            wb = Rot([ar.alloc("adaw", [128, KC, 512], F32) for _ in range(2)], "adaw")
            P.dma(cs[:], ccol, writes=["cs"])
            P.act(cs[:], cs[:], AF.Silu, reads=["cs"], writes=["cs"])
            for l in range(DEPTH):
                wv = ada_w[l].rearrange("(k p) c -> p k c", p=128)
                for cb in range(12):
                    w, wk = wb.next()
                    P.dma(w[:], wv[:, :, cb * 512:(cb + 1) * 512], writes=[wk])
                    for j in range(4):
                        ot = cb * 4 + j
                        ps, pk = psf.next()
                        for k in range(KC):
                            P.mm(ps[:, 0:2], w[:, k, j * 128:(j + 1) * 128], cs[:, k, :], start=(k == 0),
                                 stop=(k == KC - 1), reads=[wk, "cs"], writes=[pk])
                        P.ts(modT[:, l, ot, :], ps[:, 0:2], adab[:, l, ot:ot + 1], None, ALU.add,
                             reads=[pk, "const"], writes=["modT"])
                for which in range(2):
                    P.ts(A1[:, l, :, which], modT[:, l, 8:16, which], 1.0, None, ALU.add, reads=["modT"], writes=["A1"])
                    P.tt(A1[:, l, :, which], A1[:, l, :, which], g1s[:, l, :], ALU.mult, reads=["A1", "const"], writes=["A1"])
                    P.ts(A2[:, l, :, which], modT[:, l, 32:40, which], 1.0, None, ALU.add, reads=["modT"], writes=["A2"])
                    P.tt(A2[:, l, :, which], A2[:, l, :, which], g2s[:, l, :], ALU.mult, reads=["A2", "const"], writes=["A2"])
            P.barrier()
            ar.reset(m)

        def which_of(s):
            return 1 if s < NCTX else 0

        def stage_x0():
            m = ar.mark()
            xin = Rot([ar.alloc("xin", [128, D], F32) for _ in range(2)], "xin")
            xst = Rot([ar.alloc("xst", [128, KC, 128], F32) for _ in range(2)], "xst")
            for t in range(NTILE):
                src = ctx_b[t * 128:(t + 1) * 128, :] if t < 2 else x_b[(t - 2) * 128:(t - 1) * 128, :]
                xi, xk = xin.next()
                xs, sk = xst.next()
                P.dma(xi[:], src, writes=[xk])
                for half in range(2):
                    ps, pk = psf.next()
                    for kk in range(4):
                        k = half * 4 + kk
                        P.tr(ps[:, kk * 128:(kk + 1) * 128], xi[:, k * 128:(k + 1) * 128], idf[:],
                             reads=[xk, "const"], writes=[pk])
                    for kk in range(4):
                        P.copy(xs[:, half * 4 + kk, :], ps[:, kk * 128:(kk + 1) * 128], reads=[pk], writes=[sk],
                               eng=("vector" if kk % 2 == 0 else "scalar"))
                P.dma(xT[:, :, t * 128:(t + 1) * 128], xs[:], reads=[sk], writes=[("xT", (t - 2) // 4 if t >= 2 else -1)])
            P.barrier()
            ar.reset(m)

        def xkey(ci):
            return ("xT", ci - 1 if ci > 0 else -1)

        def norm_chunk(l, which_norm, xc, xck, n, s, dst, dstk, tmp):
            A = A1 if which_norm == 1 else A2
            shift0 = 0 if which_norm == 1 else 24
            w = which_of(s)
            sq, rstd, t32 = tmp
            P.act(sq[:, :, :n], xc[:, :, :n], AF.Square, reads=[xck], writes=["nsq"])
            ps, pk = psf.next()
            for k in range(KC):
                P.mm(ps[:, :n], onesb[:], sq[:, k, :n], start=(k == 0), stop=(k == KC - 1),
                     reads=["nsq", "const"], writes=[pk])
            P.act(rstd[:, :n], ps[:, :n], AF.Ln, scale=1.0 / D, bias=epsc[:, 0:1], reads=[pk, "const"], writes=["nrstd"])
            P.act(rstd[:, :n], rstd[:, :n], AF.Exp, scale=-0.5, reads=["nrstd"], writes=["nrstd"])
            for k in range(KC):
                P.stt(t32[:, k, :n], xc[:, k, :n], A[:, l, k, w:w + 1], rstd[:, :n], ALU.mult, ALU.mult,
                      reads=[xck, "nrstd", "A1", "A2"], writes=["nt32"])
                P.act(dst[:, k, :n], t32[:, k, :n], AF.Identity, bias=modT[:, l, shift0 + k, w:w + 1],
                      reads=["nt32", "modT"], writes=[dstk])

        def alloc_norm_tmp(nmax):
            return (ar.alloc("nsq", [128, KC, nmax], BF16), ar.alloc("nrstd", [128, nmax], F32),
                    ar.alloc("nt32", [128, KC, nmax], F32))

        def stage_norm1(l, hT):
            m = ar.mark()
            xcb = Rot([ar.alloc("xc", [128, KC, 512], F32) for _ in range(2)], "xc")
            tmp = alloc_norm_tmp(512)
            for ci, (s, n) in enumerate(CH):
                xc, xck = xcb.next()
                P.dma(xc[:, :, :n], xT[:, :, s:s + n], reads=[xkey(ci)], writes=[xck])
                norm_chunk(l, 1, xc, xck, n, s, hT[:, :, s:s + n], ("hT", ci), tmp)
                if "hTd" in dbg:
                    P.dma(hTd[:, :, s:s + n], hT[:, :, s:s + n], reads=[("hT", ci)], writes=["hTd"])
            P.barrier()
            ar.reset(m)

        HT_ALL = [("hT", ci) for ci in range(len(CH))]

        def proj_fm(wsrc, col0, ncols, hT, consumer, wrot):
            wv = wsrc.rearrange("(k p) c -> p k c", p=128)
            prot = Rot(psf_t[0:3], "psf")
            active = []

            def advance():
                for g_ in list(active):
                    try:
                        next(g_)
                    except StopIteration:
                        active.remove(g_)

            for cb in range(0, ncols, 512):
                nb = min(512, ncols - cb)
                w, wk = wrot.next()
                wcast_load(w[:, :, :nb], wv[:, :, col0 + cb:col0 + cb + nb], wk)
                for j in range(nb // 128):
                    for ci, (s, n) in enumerate(CH):
                        ps, pk = prot.next()
                        for k in range(KC):
                            P.mm(ps[:, :n], w[:, k, j * 128:(j + 1) * 128], hT[:, k, s:s + n], start=(k == 0),
                                 stop=(k == KC - 1), reads=[wk, ("hT", ci)], writes=[pk])
                        advance()
                        g_ = consumer((cb // 128) + j, ci, s, n, ps, pk)
                        if g_ is not None:
                            active.append(g_)
            while active:
                advance()

        def stage_gates(l, hT):
            m = ar.mark()
            wrot = Rot([ar.alloc("wg", [128, KC, 512], BF16) for _ in range(2)], "wg")
            stg = Rot([ar.alloc("gst", [128, 512], BF16) for _ in range(4)], "gst")

            def cons(jg, ci, s, n, ps, pk):
                b, bk = stg.next()
                P.act(b[:, :n], ps[:, :n], AF.Sigmoid, reads=[pk], writes=[bk])
                P.dma(gT[jg // 8, :, jg % 8, s:s + n], b[:, :n], reads=[bk], writes=["gT"])
                yield

            proj_fm(w_in[l], OFF_GATE, 3 * D, hT, cons, wrot)
            P.barrier()
            ar.reset(m)

        def branch_proj(l, br, src_fn, first):
            m = ar.mark()
            wb = ar.alloc("wbr", [128, KC, D], BF16)
            wv = w_branch[l, br].rearrange("(k p) c -> p k c", p=128)
            for hh in range(2):
                wcast_load(wb[:, :, hh * 512:(hh + 1) * 512], wv[:, :, hh * 512:(hh + 1) * 512], ("wbr", hh))
            gch = Rot([ar.alloc("gch", [128, KC, 512], BF16) for _ in range(2)], "gch")
            mold = Rot([ar.alloc("mold", [128, KC, 512], BF16) for _ in range(2)], "mold")
            mnew = Rot([ar.alloc("mnew", [128, KC, 512], BF16) for _ in range(2)], "mnew")
            t32 = Rot([ar.alloc("bt32", [128, 512], F32) for _ in range(2)], "bt32")
            for ci, (s, n) in enumerate(CH):
                if l == DEPTH - 1 and ci == 0:
                    continue
                src, srck = src_fn(ci, s, n)
                g, gk = gch.next()
                P.dma(g[:, :, :n], gT[br, :, :, s:s + n], reads=["gT"], writes=[gk])
                mn, mnk = mnew.next()
                if not first:
                    mo, mok = mold.next()
                    P.dma(mo[:, :, :n], mT[:, :, s:s + n], reads=[("mT", ci)], writes=[mok])
                for j in range(KC):
                    ps, pk = psf.next()
                    for k in range(KC):
                        P.mm(ps[:, :n], wb[:, k, j * 128:(j + 1) * 128], src[:, k, :n], start=(k == 0),
                             stop=(k == KC - 1), reads=[("wbr", j // 4)] + list(srck), writes=[pk])
                    if first:
                        P.tt(mn[:, j, :n], ps[:, :n], g[:, j, :n], ALU.mult, reads=[pk, gk], writes=[mnk])
                    else:
                        t, tk = t32.next()
                        P.tt(t[:, :n], ps[:, :n], g[:, j, :n], ALU.mult, reads=[pk, gk], writes=[tk])
                        P.tt(mn[:, j, :n], t[:, :n], mo[:, j, :n], ALU.add, reads=[tk, mok], writes=[mnk], eng="gpsimd")
                P.dma(mT[:, :, s:s + n], mn[:, :, :n], reads=[mnk], writes=[("mT", ci)])
            P.barrier()
            ar.reset(m)

        def stage_attention(l, hT, attT):
            m = ar.mark()
            wrot = Rot([ar.alloc("wqk", [128, KC, 512], BF16) for _ in range(2)], "wqk")
            cosT = ar.alloc("cosT", [128, NLAT], F32)
            sinT = ar.alloc("sinT", [128, NLAT], F32)
            P.dma(cosT[:], c_cos, writes=["rope"])
            P.dma(sinT[:], c_sin, writes=["rope"])
            qT = ar.alloc("qT", [128, 10, TOK], BF16)
            vtm = ar.alloc("vtm", [128, NTILE, 256], BF16)
            sqb = Rot([ar.alloc("sqb", [128, 512], BF16) for _ in range(2)], "sqb")
            qnb = Rot([ar.alloc("qnb", [128, 512], BF16) for _ in range(2)], "qnb")
            rsb = Rot([ar.alloc("rsb", [128, 512], F32) for _ in range(2)], "rsb")
            t1b = Rot([ar.alloc("t1b", [128, 512], F32) for _ in range(2)], "t1b")
            t2b = Rot([ar.alloc("t2b", [128, 512], F32) for _ in range(2)], "t2b")

            def cons_qk(jg, ci, s, n, ps, pk):
                is_k = jg >= 8
                gcol = qkgs[:, l, 1:2] if is_k else qkgs[:, l, 0:1]
                sq, sqk = sqb.next()
                P.act(sq[:, :n], ps[:, :n], AF.Square, reads=[pk], writes=[sqk])
                ps2, pk2 = psf_t[3], "psf#3"
                P.mm(ps2[:, :n], onesb[:], sq[:, :n], reads=[sqk, "const"], writes=[pk2])
                yield
                rs, rsk = rsb.next()
                P.act(rs[:, :n], ps2[:, :n], AF.Ln, scale=1.0 / 128, bias=epsc[:, 0:1], reads=[pk2, "const"], writes=[rsk])
                P.act(rs[:, :n], rs[:, :n], AF.Exp, scale=-0.5, reads=[rsk], writes=[rsk])
                dstk = ("qT", jg, ci)
                if ci == 0:
                    P.stt(qT[:, jg, s:s + n], ps[:, :n], gcol, rs[:, :n], ALU.mult, ALU.mult,
                          reads=[pk, rsk, "const"], writes=[dstk])
                    return
                qn, qnk = qnb.next()
                P.stt(qn[:, :n], ps[:, :n], gcol, rs[:, :n], ALU.mult, ALU.mult, reads=[pk, rsk, "const"], writes=[qnk])
                ps3, pk3 = psf_t[4], "psf#4"
                P.mm(ps3[:, :n], rm[:], qn[:, :n], reads=[qnk, "const"], writes=[pk3])
                yield
                t1, t1k = t1b.next()
                t2, t2k = t2b.next()
                ls = s - NCTX
                P.tt(t1[:, :n], qn[:, :n], cosT[:, ls:ls + n], ALU.mult, reads=[qnk, "rope"], writes=[t1k], eng="gpsimd")
                P.tt(t2[:, :n], ps3[:, :n], sinT[:, ls:ls + n], ALU.mult, reads=[pk3, "rope"], writes=[t2k])
                P.tt(qT[:, jg, s:s + n], t1[:, :n], t2[:, :n], ALU.add, reads=[t1k, t2k], writes=[dstk])

            proj_fm(w_in[l], OFF_Q, 1024 + 256, hT, cons_qk, wrot)
            wv_t, wvk = wrot.next()
            wcast_load(wv_t[:, :, :256], w_in[l].rearrange("(k p) c -> p k c", p=128)[:, :, OFF_V:OFF_V + 256], wvk)
            for t in range(NTILE):
                ps, pk = psf.next()
                ci = 0 if t < 2 else 1 + (t - 2) // 4
                for k in range(KC):
                    P.mm(ps[:, :256], hT[:, k, t * 128:(t + 1) * 128], wv_t[:, k, :256], start=(k == 0),
                         stop=(k == KC - 1), reads=[wvk, ("hT", ci)], writes=[pk])
                P.copy(vtm[:, t, :], ps[:, :256], reads=[pk], writes=[("vtm", t)], eng="scalar")
            pTb = Rot([ar.alloc("pTb", [128, 512], BF16) for _ in range(3)], "pTb")
            psO_t, psS_t = psf_t[4], psf_t[5]
            psST = Rot(psf_t[0:4], "psf")
            rsum = ar.alloc("rsum", [128, 512], F32)
            scale = 128.0 ** -0.5
            for h in range(8):
                g = h // 4
                for ci, (s, n) in enumerate(CH):
                    if l == DEPTH - 1 and ci == 0:
                        continue
                    kts = [0, 1] if ci == 0 else list(range(NTILE))
                    kc_of = lambda kt: 0 if kt < 2 else 1 + (kt - 2) // 4
                    pend = None
                    for i, kt in enumerate(kts + [None]):
                        cur = None
                        if kt is not None:
                            ps, pk = psST.next()
                            P.mm(ps[:, :n], qT[:, 8 + g, kt * 128:(kt + 1) * 128], qT[:, h, s:s + n],
                                 reads=[("qT", 8 + g, kc_of(kt)), ("qT", h, ci)], writes=[pk])
                            pt, ptk = pTb.next()
                            P.act(pt[:, :n], ps[:, :n], AF.Exp, scale=scale, reads=[pk], writes=[ptk])
                            cur = (kt, pt, ptk)
                        if pend is not None:
                            pkt, ppt, pptk = pend
                            first = (pkt == kts[0])
                            last = (pkt == kts[-1])
                            P.mm(psO_t[:, :n], vtm[:, pkt, g * 128:(g + 1) * 128], ppt[:, :n], start=first, stop=last,
                                 reads=[("vtm", pkt), pptk], writes=["psf#4"])
                            P.mm(psS_t[:, :n], onesb[:], ppt[:, :n], start=first, stop=last,
                                 reads=["const", pptk], writes=["psf#5"])
                        pend = cur
                    P.recip(rsum[:, :n], psS_t[:, :n], reads=["psf#5"], writes=["rsum"])
                    P.tt(attT[:, h, s:s + n], psO_t[:, :n], rsum[:, :n], ALU.mult, reads=["psf#4", "rsum"],
                         writes=[("attT", ci)])
            if "brT" in dbg:
                P.dma(brT[:], attT[:], reads=[("attT", ci) for ci in range(5)], writes=["brT"])
            P.barrier()
            ar.reset(m)

        def stage_wout(l):
            m = ar.mark()
            wo = ar.alloc("wo", [128, KC, D], BF16)
            wv = w_out[l].rearrange("(k p) c -> p k c", p=128)
            for hh in range(2):
                wcast_load(wo[:, :, hh * 512:(hh + 1) * 512], wv[:, :, hh * 512:(hh + 1) * 512], ("wo", hh))
            mch = Rot([ar.alloc("mch", [128, KC, 512], BF16) for _ in range(2)], "mch")
            xcb = Rot([ar.alloc("xc", [128, KC, 512], F32) for _ in range(2)], "xc")
            for ci, (s, n) in enumerate(CH):
                if l == DEPTH - 1 and ci == 0:
                    continue
                w = which_of(s)
                mc, mck = mch.next()
                xc, xck = xcb.next()
                P.dma(mc[:, :, :n], mT[:, :, s:s + n], reads=[("mT", ci)], writes=[mck])
                P.dma(xc[:, :, :n], xT[:, :, s:s + n], reads=[xkey(ci)], writes=[xck])
                for j in range(KC):
                    ps, pk = psf.next()
                    for k in range(KC):
                        P.mm(ps[:, :n], wo[:, k, j * 128:(j + 1) * 128], mc[:, k, :n], start=(k == 0),
                             stop=(k == KC - 1), reads=[("wo", j // 4), mck], writes=[pk])
                    P.stt(xc[:, j, :n], ps[:, :n], modT[:, l, 16 + j, w:w + 1], xc[:, j, :n], ALU.mult, ALU.add,
                          reads=[pk, xck, "modT"], writes=[xck])
                P.dma(xT[:, :, s:s + n], xc[:, :, :n], reads=[xck], writes=[xkey(ci)])
            P.barrier()
            ar.reset(m)

        def stage_mlp(l):
            m = ar.mark()
            NM = 256
            w1 = ar.alloc("w1", [128, KC, DFF], BF16)
            w2 = ar.alloc("w2", [128, 32, D], BF16)
            w1v = w_mlp1[l].rearrange("(k p) c -> p k c", p=128)
            w2v = w_mlp2[l].rearrange("(f p) c -> p f c", p=128)
            for i in range(8):
                wcast_load(w1[:, :, i * 512:(i + 1) * 512], w1v[:, :, i * 512:(i + 1) * 512], ("w1", i))
            for i in range(8):
                wcast_load(w2[:, i * 4:(i + 1) * 4, :], w2v[:, i * 4:(i + 1) * 4, :], ("w2", i))
            xcb = Rot([ar.alloc("xc", [128, KC, NM], F32) for _ in range(2)], "xc")
            h2b = Rot([ar.alloc("h2", [128, KC, NM], BF16) for _ in range(1)], "h2")
            ub = Rot([ar.alloc("u", [128, 32, NM], BF16) for _ in range(1)], "u")
            rl = Rot([ar.alloc("rl", [128, NM], F32) for _ in range(3)], "rl")
            tmp = alloc_norm_tmp(NM)
            for s in range(0, TOK, NM):
                if l == DEPTH - 1 and s < NCTX:
                    continue
                n = NM
                ci = 0 if s < NCTX else 1 + (s - NCTX) // 512
                xc, xck = xcb.next()
                h2, h2k = h2b.next()
                u, uk = ub.next()
                P.dma(xc[:], xT[:, :, s:s + n], reads=[xkey(ci)], writes=[xck])
                norm_chunk(l, 2, xc, xck, n, s, h2, h2k, tmp)
                for f in range(32):
                    ps, pk = psf.next()
                    for k in range(KC):
                        P.mm(ps[:, :n], w1[:, k, f * 128:(f + 1) * 128], h2[:, k, :n], start=(k == 0),
                             stop=(k == KC - 1), reads=[("w1", f // 4), h2k], writes=[pk])
                    r, rk = rl.next()
                    P.act(r[:, :n], ps[:, :n], AF.Relu, reads=[pk], writes=[rk])
                    P.tt(u[:, f, :n], r[:, :n], r[:, :n], ALU.mult, reads=[rk], writes=[uk],
                         eng=("gpsimd" if f % 2 == 0 else "vector"))
                w = which_of(s)
                for j in range(KC):
                    ps, pk = psf.next()
                    for f in range(32):
                        P.mm(ps[:, :n], w2[:, f, j * 128:(j + 1) * 128], u[:, f, :n], start=(f == 0), stop=(f == 31),
                             reads=[("w2", f // 4), uk], writes=[pk])
                    P.stt(xc[:, j, :n], ps[:, :n], modT[:, l, 40 + j, w:w + 1], xc[:, j, :n], ALU.mult, ALU.add,
                          reads=[pk, xck, "modT"], writes=[xck])
                P.dma(xT[:, :, s:s + n], xc[:], reads=[xck], writes=[xkey(ci)])
            P.barrier()
            ar.reset(m)

        def stage_final():
            m = ar.mark()
            xin = Rot([ar.alloc("fxi", [128, KC, 128], F32) for _ in range(2)], "fxi")
            xo = Rot([ar.alloc("fxo", [128, D], F32) for _ in range(2)], "fxo")
            for t in range(2, NTILE):
                ci = 1 + (t - 2) // 4
                xi, xk = xin.next()
                o, ok = xo.next()
                P.dma(xi[:], xT[:, :, t * 128:(t + 1) * 128], reads=[xkey(ci)], writes=[xk])
                for half in range(2):
                    ps, pk = psf.next()
                    for kk in range(4):
                        k = half * 4 + kk
                        P.tr(ps[:, kk * 128:(kk + 1) * 128], xi[:, k, :], idf[:], reads=[xk, "const"], writes=[pk])
                    P.copy(o[:, half * 512:(half + 1) * 512], ps[:], reads=[pk], writes=[ok],
                           eng=("vector" if half == 0 else "scalar"))
                P.dma(out_d[(t - 2) * 128:(t - 1) * 128, :], o[:], reads=[ok], final=True)
            ar.reset(m)

        def stage_hgrn(l, hT):
            m = ar.mark()
            m1 = ar.mark()
            wrot = Rot([ar.alloc("wh", [128, KC, 512], BF16) for _ in range(2)], "wh")
            st32 = Rot([ar.alloc("hs32", [128, 512], F32) for _ in range(3)], "hs32")
            st16 = Rot([ar.alloc("hs16", [128, 512], BF16) for _ in range(3)], "hs16")
            dsts = [(zq, BF16), (zf, F32), (zb, F32), (zi, BF16), (zg, BF16)]
            wv = w_in[l].rearrange("(k p) c -> p k c", p=128)
            for cb in range(10):
                w, wk = wrot.next()
                wcast_load(w[:], wv[:, :, OFF_HQ + cb * 512:OFF_HQ + (cb + 1) * 512], wk)
                dst, dt = dsts[cb // 2]
                c0 = (cb % 2) * 512
                for t in range(NTILE):
                    ps, pk = psf.next()
                    ci = 0 if t < 2 else 1 + (t - 2) // 4
                    for k in range(KC):
                        P.mm(ps[:], hT[:, k, t * 128:(t + 1) * 128], w[:, k, :], start=(k == 0), stop=(k == KC - 1),
                             reads=[wk, ("hT", ci)], writes=[pk])
                    sb, sk = (st32 if dt == F32 else st16).next()
                    P.copy(sb[:], ps[:], reads=[pk], writes=[sk], eng=("vector" if t % 2 == 0 else "scalar"))
                    P.dma(dst[t * 128:(t + 1) * 128, c0:c0 + 512], sb[:], reads=[sk], writes=[("zh", cb // 2, t)])
            P.barrier()
            ar.reset(m1)
            Dm = ar.alloc("Dm", [128, 2, 128], F32)
            Dsel = ar.alloc("Dsel", [128, 2, 6], F32)
            msk = ar.alloc("msk", [128, 2, 512], F32)
            gn = ar.alloc("gn", [128, D], F32)
            lbt = ar.alloc("lbt", [128, 2, D], F32)
            oml = ar.alloc("oml", [128, 2, D], F32)
            P.dma(Dm[:], c_dm, writes=["hgc"])
            P.dma(Dsel[:], c_dsel, writes=["hgc"])
            P.dma(msk[:], c_msk, writes=["hgc"])
            P.dma(gn[:], hgn_row[l].partition_broadcast(128), writes=["hgc"])
            if l == 0:
                P.memset(lbt[:], 0.0, writes=["lbt"])
                P.memset(oml[:], 1.0, writes=["oml"])
            else:
                for dr in range(2):
                    P.dma(lbt[:, dr, :], hg_lb[1, dr].partition_broadcast(128), writes=["lbt"])
                    P.dma(oml[:, dr, :], hg_lb[0, dr].partition_broadcast(128), writes=["oml"])
                P.tt(lbt[:], lbt[:], oml[:], ALU.subtract, reads=["lbt", "oml"], writes=["lbt"])
                P.act(lbt[:], lbt[:], AF.Sigmoid, reads=["lbt"], writes=["lbt"])
                P.ts(oml[:], lbt[:], -1.0, 1.0, ALU.mult, ALU.add, reads=["lbt"], writes=["oml"])
            o_b = ar.alloc("o_b", [128, NTILE, D], BF16)
            S = ar.alloc("S", [128, 2, KC, 128], F32)
            P.memset(S[:], 0.0, writes=[("S%d" % d_, h_) for d_ in range(2) for h_ in range(KC)])
            qin = Rot([ar.alloc("hq", [128, D], BF16) for _ in range(2)], "hq")
            zin = Rot([ar.alloc("hz", [128, D], F32) for _ in range(2)], "hz")
            vin = Rot([ar.alloc("hv", [128, D], BF16) for _ in range(3)], "hv")
            gin = Rot([ar.alloc("hgi", [128, D], BF16) for _ in range(2)], "hgi")
            sig = ar.alloc("sig", [128, D], F32)
            logf = ar.alloc("logf", [128, D], F32)
            kk = ar.alloc("kk", [128, D], F32)
            Ep = ar.alloc("Ep", [128, D], F32)
            Em = ar.alloc("Em", [128, D], F32)
            qt = ar.alloc("qt", [128, D], BF16)
            ktR = Rot([ar.alloc("kt", [128, D], BF16) for _ in range(2)], "kt")
            kt2R = Rot([ar.alloc("kt2", [128, D], BF16) for _ in range(2)], "kt2")
            Dm2 = ar.alloc("Dm2", [128, 2, 128], F32)
            zbf = ar.alloc("zbf", [128, 512], BF16)
            P.memset(zbf[:], 0.0, writes=["hgc"])
            P.dma(Dm2[:], c_dm2, writes=["hgc"])
            qtTR = Rot([ar.alloc("qtT", [128, KC, 128], BF16) for _ in range(2)], "qtT")
            ktTR = Rot([ar.alloc("ktT", [128, KC, 128], BF16) for _ in range(2)], "ktT")
            ktBR = Rot([ar.alloc("ktB", [128, KC, 128], BF16) for _ in range(2)], "ktB")
            ktF = ar.alloc("ktF", [128, KC, 128], BF16)
            qtCR = Rot([ar.alloc("qtC", [128, KC, 128], BF16) for _ in range(2)], "qtC")
            cmk = ar.alloc("cmk", [128, 2, 3, 128], BF16)
            P.dma(cmk[:], c_cmk, writes=["hgc"])
            sclR = Rot([ar.alloc("scl", [128, KC, 6], F32) for _ in range(2)], "scl")
            sm = ar.alloc("sm", [128, KC, 128], BF16)
            Ss = ar.alloc("Ss", [128, KC, 128], BF16)
            o32 = ar.alloc("o32", [128, D], F32)
            sq32 = ar.alloc("sq32", [128, D], F32)
            silR = Rot([ar.alloc("sil", [128, D], BF16) for _ in range(2)], "sil")
            ssum = ar.alloc("ssum", [128, KC], F32)
            hgb = ar.alloc("hgb", [128, D], BF16)
            hgT = Rot([ar.alloc("hgT", [128, KC, 128], BF16) for _ in range(1)], "hgT")
            B0, B1, B2, B3, B4, B5 = ["psf#%d" % i for i in range(6)]
            p0, p1, p2, p3, p4, p5 = psf_t

            def tile_pass(t, dr):
                Sk = "S%d" % dr
                kt, kt_k = ktR.next()
                kt2, kt2_k = kt2R.next()
                qtT, qtT_k = qtTR.next()
                qtC, qtC_k = qtCR.next()
                ktT, ktT_k = ktTR.next()
                ktB, ktB_k = ktBR.next()
                scl, scl_k = sclR.next()
                q, qk = qin.next()
                z, zk = zin.next()
                v, vk = vin.next()
                r0 = t * 128
                P.dma(q[:], zq[r0:r0 + 128, :], reads=[("zh", 0, t)], writes=[qk])
                P.dma(z[:], (zf if dr == 0 else zb)[r0:r0 + 128, :], reads=[("zh", 1 + dr, t)], writes=[zk])
                P.dma(v[:], zi[r0:r0 + 128, :], reads=[("zh", 3, t)], writes=[vk])
                if dr == 0:
                    g, gk = gin.next()
                    P.dma(g[:], zg[r0:r0 + 128, :], reads=[("zh", 4, t)], writes=[gk])
                yield "P1"
                P.act(sig[:], z[:], AF.Sigmoid, reads=[zk], writes=["sig"])
                if dr == 0:
                    sil, silk = silR.next()
                    P.act(sil[:], g[:], AF.Silu, reads=[gk], writes=[silk])
                    P.tt(sil[:], sil[:], gn[:], ALU.mult, reads=[silk, "hgc"], writes=[silk], eng="gpsimd")
                yield
                P.tt(sig[:], sig[:], oml[:, dr, :], ALU.mult, reads=["sig", "oml"], writes=["sig"])
                yield
                P.tt(sig[:], sig[:], lbt[:, dr, :], ALU.add, reads=["sig", "lbt"], writes=["sig"])
                yield
                P.ts(kk[:], sig[:], -1.0, 1.0, ALU.mult, ALU.add, reads=["sig"], writes=["kk"], eng="gpsimd")
                P.ts(logf[:], sig[:], 1e-6, None, ALU.max, reads=["sig"], writes=["logf"])
                yield
                P.act(logf[:], logf[:], AF.Ln, reads=["logf"], writes=["logf"])
                yield
                for hf_ in range(2):
                    cs_ = slice(hf_ * 512, (hf_ + 1) * 512)
                    P.mm(p0[:], Dm[:, dr, :], logf[:, cs_], reads=["logf", "hgc"], writes=[B0])
                    yield
                    P.act(Ep[:, cs_], p0[:], AF.Exp, reads=[B0], writes=["Ep"])
                    P.act(Em[:, cs_], p0[:], AF.Exp, scale=-1.0, reads=[B0], writes=["Em"])
                    yield
                P.tt(qt[:], q[:], Ep[:], ALU.mult, reads=[qk, "Ep"], writes=["qt"])
                P.tt(kt[:], kk[:], Em[:], ALU.mult, reads=["kk", "Em"], writes=[kt_k], eng="gpsimd")
                yield
                for hf_ in range(2):
                    cs_ = slice(hf_ * 512, (hf_ + 1) * 512)
                    P.mm(p0[:], Dm2[:, dr, :], logf[:, cs_], reads=["logf", "hgc"], writes=[B0])
                    yield
                    P.act(Em[:, cs_], p0[:], AF.Exp, reads=[B0, kt_k], writes=["Em"])
                    yield
                P.tt(kt2[:], kk[:], Em[:], ALU.mult, reads=["kk", "Em"], writes=[kt2_k])
                for h in range(KC):
                    P.mm(p0[:, h * 6:(h + 1) * 6], logf[:, h * 128:(h + 1) * 128], Dsel[:, dr, :],
                         reads=["logf", "hgc"], writes=[B0])
                yield
                P.act(scl[:].rearrange("p h c -> p (h c)"), p0[:, 0:48], AF.Exp, reads=[B0], writes=[scl_k])
                fl = lambda a_: a_[:].rearrange("p h c -> p (h c)")
                for h in range(KC):
                    P.tr(psb_t[0][:, h * 128:(h + 1) * 128], qt[:, h * 128:(h + 1) * 128], idb[:],
                         reads=["qt", "const"], writes=["psb#0"])
                yield
                P.copy(fl(qtT), psb_t[0][:], reads=["psb#0"], writes=[qtT_k], eng="scalar")
                yield
                for h in range(KC):
                    P.tr(psb_t[0][:, h * 128:(h + 1) * 128], kt[:, h * 128:(h + 1) * 128], idb[:],
                         reads=[kt_k, "const"], writes=["psb#0"])
                P.tt(qtC[:], qtT[:], cmk[:, dr, 2, :].unsqueeze(1).to_broadcast([128, KC, 128]), ALU.mult,
                     reads=[qtT_k, "hgc"], writes=[qtC_k], eng="gpsimd")
                yield
                P.copy(fl(ktF), psb_t[0][:], reads=["psb#0"], writes=["ktF"], eng="scalar")
                yield
                P.tt(ktT[:], ktF[:], cmk[:, dr, 0, :].unsqueeze(1).to_broadcast([128, KC, 128]), ALU.mult,
                     reads=["ktF", "hgc"], writes=[ktT_k])
                P.tt(ktB[:], ktF[:], cmk[:, dr, 1, :].unsqueeze(1).to_broadcast([128, KC, 128]), ALU.mult,
                     reads=["ktF", "hgc"], writes=[ktB_k], eng="gpsimd")
                yield "P2"
                for hg in range(2):
                    for hh in range(4):
                        h = hg * 4 + hh
                        P.mm(p1[:, hh * 128:(hh + 1) * 128], ktT[:, h, :], qtT[:, h, :], start=True, stop=False,
                             reads=[ktT_k, qtT_k], writes=[B1])
                        P.mm(p1[:, hh * 128:(hh + 1) * 128], ktB[:, h, :], qtC[:, h, :], start=False, stop=True,
                             reads=[ktB_k, qtC_k], writes=[B1])
                    yield
                    P.tt(sm[:, hg * 4:(hg + 1) * 4, :].rearrange("p h c -> p (h c)"), p1[:], msk[:, dr, :], ALU.mult,
                         reads=[B1, "hgc"], writes=[("sm", hg)])
                    yield
                for hf_ in range(2):
                    pp, bk = (p2, B2) if hf_ == 0 else (p3, B3)
                    if dr == 0:
                        P.mm(pp[:], idb[:], o_b[:, t, hf_ * 512:(hf_ + 1) * 512], start=True, stop=False,
                             reads=["const", ("o_b", t)], writes=[bk])
                    else:
                        P.mm(pp[:], idb[:], zbf[:], start=True, stop=False, reads=["const", "hgc"], writes=[bk])
                for h in range(KC):
                    pp, bk = (p2, B2) if h < 4 else (p3, B3)
                    P.mm(pp[:, (h % 4) * 128:(h % 4 + 1) * 128], sm[:, h, :], v[:, h * 128:(h + 1) * 128],
                         start=False, stop=False, reads=[("sm", h // 4), vk], writes=[bk])
                yield
                order = (0, 1) if dr == 0 else (1, 0)
                for oi, cc in enumerate(order):
                    lo, hi = cc * 64, cc * 64 + 64
                    P.tt(Ss[:], S[:, dr, :, :], scl[:, :, cc * 3:cc * 3 + 1].to_broadcast([128, KC, 128]), ALU.mult,
                         reads=[(Sk, h_) for h_ in range(KC)] + [scl_k], writes=["Ss"])
                    for h in range(KC):
                        pp, bk = (p4, B4) if h < 4 else (p5, B5)
                        P.mm(pp[:, (h % 4) * 128:(h % 4 + 1) * 128], kt2[lo:hi, h * 128:(h + 1) * 128],
                             v[lo:hi, h * 128:(h + 1) * 128], reads=[kt2_k, vk], writes=[bk])
                    yield
                    for h in range(KC):
                        pp, bk = (p2, B2) if h < 4 else (p3, B3)
                        P.mm(pp[lo:hi, (h % 4) * 128:(h % 4 + 1) * 128], qtT[:, h, lo:hi], Ss[:, h, :],
                             start=False, stop=False, reads=[qtT_k, "Ss"], writes=[bk])
                    yield
                    for h in range(KC):
                        pp, bk = (p4, B4) if h < 4 else (p5, B5)
                        P.stt(S[:, dr, h, :], S[:, dr, h, :], scl[:, h, cc * 3 + 1:cc * 3 + 2],
                              pp[:, (h % 4) * 128:(h % 4 + 1) * 128], ALU.mult, ALU.add,
                              reads=[(Sk, h), scl_k, bk], writes=[(Sk, h)])
                        if h == 3:
                            yield
                    yield
                for hf_ in range(2):
                    pp, bk = (p2, B2) if hf_ == 0 else (p3, B3)
                    P.mm(pp[:], idb[:], zbf[:], start=False, stop=True, reads=["const", "hgc"], writes=[bk])
                if dr == 1:
                    P.copy(o_b[:, t, 0:512], p2[:], reads=[B2], writes=[("o_b", t)], eng="scalar")
                    P.copy(o_b[:, t, 512:1024], p3[:], reads=[B3], writes=[("o_b", t)])
                    return
                P.act(sq32[:, 0:512], p2[:], AF.Square, reads=[B2], writes=["sq32"])
                P.act(sq32[:, 512:1024], p3[:], AF.Square, reads=[B3], writes=["sq32"])
                yield
                P.op("vector", lambda e: e.reduce_sum(ssum[:], sq32[:].rearrange("p (h c) -> p h c", h=KC), AX.X),
                     reads=["sq32"], writes=["ssum"])
                yield
                P.act(ssum[:], ssum[:], AF.Ln, scale=1.0 / 128, bias=epsc[:, 0:1], reads=["ssum", "const"], writes=["ssum"])
                P.act(ssum[:], ssum[:], AF.Exp, scale=-0.5, reads=["ssum"], writes=["ssum"])
                yield
                for hf_ in range(2):
                    pp, bk = (p2, B2) if hf_ == 0 else (p3, B3)
                    P.tt(o32[:, hf_ * 512:(hf_ + 1) * 512].rearrange("p (h c) -> p h c", h=4),
                         pp[:].rearrange("p (h c) -> p h c", h=4),
                         ssum[:, hf_ * 4:(hf_ + 1) * 4].unsqueeze(2).to_broadcast([128, 4, 128]), ALU.mult,
                         reads=[bk, "ssum"], writes=["o32"])
                    yield
                P.tt(hgb[:], o32[:], sil[:], ALU.mult, reads=["o32", silk], writes=["hgb"])
                yield
                pb, pbk = psb_t[1], "psb#1"
                for h in range(KC):
                    P.tr(pb[:, h * 128:(h + 1) * 128], hgb[:, h * 128:(h + 1) * 128], idb[:], reads=["hgb", "const"], writes=[pbk])
                yield
                ht, htk = hgT.next()
                P.copy(ht[:].rearrange("p h c -> p (h c)"), pb[:], reads=[pbk], writes=[htk], eng="scalar")
                ci = 0 if t < 2 else 1 + (t - 2) // 4
                P.dma(brT[:, :, r0:r0 + 128], ht[:], reads=[htk], writes=[("brT", ci)])

            seq = [(t, 1) for t in [1, 0] + list(range(NTILE - 1, 1, -1))] + [(t, 0) for t in range(NTILE)]
            gens = [tile_pass(t, dr) for t, dr in seq]
            NG = len(gens)

            def run_to(g_, marker):
                for v_ in g_:
                    if v_ == marker:
                        return

            def step(g_):
                try:
                    return next(g_)
                except StopIteration:
                    return "END"

            run_to(gens[0], "P1")
            run_to(gens[0], "P2")
            run_to(gens[1], "P1")
            for i in range(NG):
                if i + 2 < NG:
                    run_to(gens[i + 2], "P1")
                d1 = i + 1 >= NG
                d2 = False
                while not (d1 and d2):
                    if not d2:
                        d2 = step(gens[i]) == "END"
                    if not d1:
                        d1 = step(gens[i + 1]) in ("P2", "END")
            P.barrier()
            ar.reset(m)

        def br_src_loader():
            bufs = Rot([ar.alloc("brc", [128, KC, 512], BF16) for _ in range(2)], "brc")

            def fn(ci, s, n):
                b, bk = bufs.next()
                P.dma(b[:, :, :n], brT[:, :, s:s + n], reads=[("brT", ci)], writes=[bk])
                return b, [bk]
            return fn

        def stage_hy_proj(l, hT):
            m = ar.mark()
            wrot = Rot([ar.alloc("wy", [128, KC, 512], BF16) for _ in range(2)], "wy")
            padl = Rot([ar.alloc("padl", [128, NLAT + 2], F32) for _ in range(2)], "padl")
            padc = Rot([ar.alloc("padc", [128, NCTX + 2], F32) for _ in range(2)], "padc")
            ubuf = Rot([ar.alloc("hyu", [128, NLAT], BF16) for _ in range(2)], "hyu")
            tmpu = ar.alloc("hytmp", [128, NLAT], F32)
            for r_ in (padl, padc):
                for t_, k_ in zip(r_.tiles, r_.keys):
                    P.memset(t_[:], 0.0, writes=[k_])
            cur = {}

            def conv(pad, padk, n, jg, off):
                w = lambda i: hysw[:, l, jg, i:i + 1]
                P.ts(tmpu[:, :n], pad[:, 0:n], w(0), hysb[:, l, jg:jg + 1], ALU.mult, ALU.add,
                     reads=[padk, "const"], writes=["hytmp"])
                P.stt(tmpu[:, :n], pad[:, 1:n + 1], w(1), tmpu[:, :n], ALU.mult, ALU.add,
                      reads=[padk, "const", "hytmp"], writes=["hytmp"])
                u, uk = ubuf.next()
                P.stt(u[:, :n], pad[:, 2:n + 2], w(2), tmpu[:, :n], ALU.mult, ALU.add,
                      reads=[padk, "const", "hytmp"], writes=[uk])
                dst = (hx1T, hx2T, hvT)[jg // 8]
                P.dma(dst[:, jg % 8, off:off + n], u[:, :n], reads=[uk], writes=["hyT"])

            def cons(jg, ci, s, n, ps, pk):
                if ci == 0:
                    cur["c"] = padc.next()
                    cur["l"] = padl.next()
                    pc, pck = cur["c"]
                    P.copy(pc[:, 1:1 + n], ps[:, :n], reads=[pk], writes=[pck], eng="scalar")
                    if l == 0:
                        conv(pc, pck, NCTX, jg, 0)
                else:
                    pl_, plk = cur["l"]
                    a = 1 + s - NCTX
                    P.copy(pl_[:, a:a + n], ps[:, :n], reads=[pk], writes=[plk], eng=("scalar" if ci % 2 else "vector"))
                    if ci == 4:
                        conv(pl_, plk, NLAT, jg, NCTX)
                yield

            proj_fm(w_in[l], OFF_HY, 3 * D, hT, cons, wrot)
            P.barrier()
            ar.reset(m)

        def stage_hyena_seq(l, n, off, tabs, Hs_d):
            NLT = n // 128
            NRT = 2 * NLT
            TCW = min(n, 512)
            NTC = n // TCW
            Ft_d, FTt_d, zfT_d, dec_d, arow_d = tabs
            m = ar.mark()
            arow = ar.alloc("arow", [128, NRT], F32)
            P.dma(arow[:], arow_d, writes=["hyk"])
            mf = ar.mark()
            w1s = ar.alloc("w1s", [64, 64], F32)
            w2s = ar.alloc("w2s", [64, 64], F32)
            fc = ar.alloc("fc", [64, 4], F32)
            fb = ar.alloc("fb", [64, 2], F32)
            zfT = ar.alloc("zfT", [64, n], F32)
            h1T = ar.alloc("h1T", [64, n], F32)
            h2Tb = ar.alloc("h2Tb", [64, n], BF16)
            w3b = ar.alloc("w3b", [64, 4 * D], BF16)
            v32 = ar.alloc("v32", [64, 512], F32)
            kint = ar.alloc("kint", [64, 512], mybir.dt.int32)
            kf = ar.alloc("kf", [64, 512], F32)
            mg = ar.alloc("mg", [64, 512], F32)
            P.memset(w1s[:], 0.0, writes=["hyk"])
            P.memset(zfT[:], 0.0, writes=["hyk"])
            P.dma(w1s[0:33, :], hy_w1[l], writes=["hyk"])
            P.dma(w2s[:], hy_w2[l], writes=["hyk"])
            P.dma(fc[:], hy_fc[:, l, :], writes=["hyk"])
            P.dma(zfT[0:33, :], zfT_d, writes=["hyk"])
            for i_ in range(8):
                P.dma(w3b[:, i_ * 512:(i_ + 1) * 512], hy_w3[l][:, i_ * 512:(i_ + 1) * 512], writes=["hyk"], eng="gpsimd")
            P.tt(fb[:], fc[:, 2:4], fc[:, 0:2], ALU.mult, reads=["hyk"], writes=["fb"])

            def mlp_layer(lhsT, rhs, K, li, out):
                for pc in range(0, n, 512):
                    np_ = min(512, n - pc)
                    ps, pk = psf.next()
                    P.mm(ps[0:64, :np_], lhsT[0:K, :], rhs[0:K, pc:pc + np_], reads=["hyk", "h1T"], writes=[pk])
                    P.ts(v32[:, :np_], ps[0:64, :np_], fc[:, li:li + 1], fb[:, li:li + 1], ALU.mult, ALU.add,
                         reads=[pk, "hyk", "fb"], writes=["v32"])
                    P.ts(v32[:, :np_], v32[:, :np_], 1.0 / (2 * math.pi), 16.0, ALU.mult, ALU.add, reads=["v32"], writes=["v32"])
                    P.copy(kint[:, :np_], v32[:, :np_], reads=["v32"], writes=["kint"])
                    P.copy(kf[:, :np_], kint[:, :np_], reads=["kint"], writes=["kf"])
                    P.tt(v32[:, :np_], v32[:, :np_], kf[:, :np_], ALU.subtract, reads=["v32", "kf"], writes=["v32"])
                    P.ts(mg[:, :np_], v32[:, :np_], 0.5, None, ALU.is_gt, reads=["v32"], writes=["mg"])
                    P.tt(v32[:, :np_], v32[:, :np_], mg[:, :np_], ALU.subtract, reads=["v32", "mg"], writes=["v32"])
                    P.act(out[:, pc:pc + np_], v32[:, :np_], AF.Sin, scale=float(2 * math.pi), reads=["v32"],
                          writes=["h1T" if out is h1T else "h2Tb"])

            mlp_layer(w1s, zfT, 64, 0, h1T)
            mlp_layer(w2s, h1T, 64, 1, h2Tb)
            fsum = ar.alloc("fsum", [128, 4, NLT, 512], BF16)
            fdiff = ar.alloc("fdiff", [128, 4, NLT, 512], BF16)
            rn = ar.alloc("rn", [128, 4, 512], F32)
            decb = Rot([ar.alloc("decb", [128, 512], F32) for _ in range(2)], "decb")
            fwb = Rot([ar.alloc("fwb", [128, 512], F32) for _ in range(2)], "fwb")
            bwb = Rot([ar.alloc("bwb", [128, 512], F32) for _ in range(2)], "bwb")
            a1b = Rot([ar.alloc("a1b", [128, 512], BF16) for _ in range(2)], "a1b")
            a2b = Rot([ar.alloc("a2b", [128, 512], BF16) for _ in range(2)], "a2b")
            ps5 = Rot(psf_t[0:5], "psf")
            psN = psf_t[5]
            for cmb in range(4):
                o, cb = cmb // 2, cmb % 2
                for lt in range(NLT):
                    dc, dck = decb.next()
                    P.dma(dc[:], dec_d[lt * 128:(lt + 1) * 128, cb * 512:(cb + 1) * 512], writes=[dck])
                    psF, pkF = ps5.next()
                    psB, pkB = ps5.next()
                    c0 = o * 2 * D + cb * 512
                    P.mm(psF[:], h2Tb[:, lt * 128:(lt + 1) * 128], w3b[:, c0:c0 + 512], reads=["h2Tb", "hyk"], writes=[pkF])
                    P.mm(psB[:], h2Tb[:, lt * 128:(lt + 1) * 128], w3b[:, c0 + D:c0 + D + 512], reads=["h2Tb", "hyk"], writes=[pkB])
                    fw, fwk = fwb.next()
                    bw, bwk = bwb.next()
                    P.tt(fw[:], psF[:], dc[:], ALU.mult, reads=[pkF, dck], writes=[fwk])
                    P.tt(bw[:], psB[:], dc[:], ALU.mult, reads=[pkB, dck], writes=[bwk])
                    if lt == 0:
                        P.memset(bw[0:1, :], 0.0, writes=[bwk])
                    P.tt(fsum[:, cmb, lt, :], fw[:], bw[:], ALU.add, reads=[fwk, bwk], writes=["fsum"], eng="gpsimd")
                    P.tt(fdiff[:, cmb, lt, :], fw[:], bw[:], ALU.subtract, reads=[fwk, bwk], writes=["fdiff"], eng="gpsimd")
                    a1, a1k = a1b.next()
                    a2, a2k = a2b.next()
                    P.act(a1[:], fw[:], AF.Abs, reads=[fwk], writes=[a1k])
                    P.act(a2[:], bw[:], AF.Abs, reads=[bwk], writes=[a2k])
                    P.mm(psN[:], onesb[:], a1[:], start=(lt == 0), stop=False, reads=[a1k, "const"], writes=["psf#5"])
                    P.mm(psN[:], onesb[:], a2[:], start=False, stop=(lt == NLT - 1), reads=[a2k, "const"], writes=["psf#5"])
                P.ts(rn[:, cmb, :], psN[:], EPS, None, ALU.add, reads=["psf#5"], writes=["rn"])
                P.recip(rn[:, cmb, :], rn[:, cmb, :], reads=["rn"], writes=["rn"])
            ftb = Rot([ar.alloc("ftb", [128, NLT, 128], BF16) for _ in range(2)], "ftb")
            hst = Rot([ar.alloc("hst", [128, 512], BF16) for _ in range(3)], "hst")
            for rt in range(NRT):
                ft, ftk = ftb.next()
                P.dma(ft[:], Ft_d[rt], writes=[ftk])
                for cmb in range(4):
                    o, cb = cmb // 2, cmb % 2
                    src, srck = (fsum, "fsum") if rt < NLT else (fdiff, "fdiff")
                    ps, pk = psf.next()
                    for lt in range(NLT):
                        P.mm(ps[:], ft[:, lt, :], src[:, cmb, lt, :], start=(lt == 0), stop=(lt == NLT - 1),
                             reads=[ftk, srck], writes=[pk])
                    hs, hsk = hst.next()
                    P.stt(hs[:], ps[:], arow[:, rt:rt + 1], rn[:, cmb, :], ALU.mult, ALU.mult, reads=[pk, "hyk", "rn"], writes=[hsk])
                    if rt == NLT:
                        ps2, pk2 = psf.next()
                        for lt in range(NLT):
                            P.mm(ps2[0:1, :], ft[:, lt, 0:1], fsum[:, cmb, lt, :], start=(lt == 0), stop=(lt == NLT - 1),
                                 reads=[ftk, "fsum"], writes=[pk2])
                        P.stt(hs[0:1, :], ps2[0:1, :], arow[0:1, rt:rt + 1], rn[0:1, cmb, :], ALU.mult, ALU.mult,
                              reads=[pk2, "hyk", "rn"], writes=[hsk])
                    P.dma(Hs_d[rt * 128:(rt + 1) * 128, o, cb * 512:(cb + 1) * 512], hs[:], reads=[hsk], writes=["Hs"])
            P.barrier()
            ar.reset(mf)
            Yc = ar.alloc("Yc", [128, NRT, D], BF16)
            for o in range(2):
                srcT = hvT if o == 0 else hz1T
                gateT = hx1T if o == 0 else hx2T
                dstT = hz1T if o == 0 else brT
                m2 = ar.mark()
                z = ar.alloc("z", [128, NLT, D], BF16)
                scb = Rot([ar.alloc("scb", [128, KC, TCW], BF16) for _ in range(2)], "scb")
                for tc in range(NTC):
                    sc_, sck = scb.next()
                    P.dma(sc_[:], srcT[:, :, off + tc * TCW:off + (tc + 1) * TCW], writes=[sck])
                    for tt_ in range(TCW // 128):
                        lt = tc * (TCW // 128) + tt_
                        pb, pbk = psb.next()
                        for k in range(KC):
                            P.tr(pb[:, k * 128:(k + 1) * 128], sc_[:, k, tt_ * 128:(tt_ + 1) * 128], idb[:],
                                 reads=[sck, "const"], writes=[pbk])
                        P.copy(z[:, lt, :], pb[:], reads=[pbk], writes=["z"], eng=("vector" if lt % 2 else "scalar"))
                fab = Rot([ar.alloc("fab", [128, NLT, 128], BF16) for _ in range(4)], "fab")
                hab = Rot([ar.alloc("hab", [128, D], BF16) for _ in range(4)], "hab")
                tb = [Rot([ar.alloc("tb%d" % i, [128, 512], F32) for _ in range(2)], "tb%d" % i) for i in range(4)]
                for i in range(NLT):
                    FA, FAk = fab.next()
                    FB, FBk = fab.next()
                    HA, HAk = hab.next()
                    HB, HBk = hab.next()
                    P.dma(FA[:], Ft_d[i], writes=[FAk])
                    P.dma(FB[:], Ft_d[NLT + i], writes=[FBk])
                    P.dma(HA[:], Hs_d[i * 128:(i + 1) * 128, o, :], writes=[HAk])
                    P.dma(HB[:], Hs_d[(NLT + i) * 128:(NLT + i + 1) * 128, o, :], writes=[HBk])
                    for hf_ in range(2):
                        cs_ = slice(hf_ * 512, (hf_ + 1) * 512)
                        psA, pkA = psf.next()
                        psB, pkB = psf.next()
                        for lt in range(NLT):
                            P.mm(psA[:], FA[:, lt, :], z[:, lt, cs_], start=(lt == 0), stop=(lt == NLT - 1),
                                 reads=[FAk, "z"], writes=[pkA])
                        for lt in range(NLT):
                            P.mm(psB[:], FB[:, lt, :], z[:, lt, cs_], start=(lt == 0), stop=(lt == NLT - 1),
                                 reads=[FBk, "z"], writes=[pkB])
                        (t1, t1k), (t2, t2k), (t3, t3k), (t4, t4k) = [r_.next() for r_ in tb]
                        P.tt(t1[:], psA[:], HA[:, cs_], ALU.mult, reads=[pkA, HAk], writes=[t1k])
                        P.tt(t2[:], psB[:], HB[:, cs_], ALU.mult, reads=[pkB, HBk], writes=[t2k])
                        P.tt(t3[:], psA[:], HB[:, cs_], ALU.mult, reads=[pkA, HBk], writes=[t3k])
                        P.tt(t4[:], psB[:], HA[:, cs_], ALU.mult, reads=[pkB, HAk], writes=[t4k])
                        P.tt(Yc[:, i, cs_], t1[:], t2[:], ALU.subtract, reads=[t1k, t2k], writes=["Yc"], eng="gpsimd")
                        P.tt(Yc[:, NLT + i, cs_], t3[:], t4[:], ALU.add, reads=[t3k, t4k], writes=["Yc"], eng="gpsimd")
                        if i == 0:
                            P.tt(Yc[0:1, 0, cs_], psA[0:1, :], HA[0:1, cs_], ALU.mult, reads=[pkA, HAk, "Yc"], writes=["Yc"])
                            P.tt(Yc[0:1, NLT, cs_], psB[0:1, :], HB[0:1, cs_], ALU.mult, reads=[pkB, HBk, "Yc"], writes=["Yc"])
                P.barrier()
                ar.reset(m2)
                ftt = Rot([ar.alloc("ftt", [128, NRT, TCW], BF16) for _ in range(2)], "ftt")
                zib = Rot([ar.alloc("zib", [128, KC, TCW], BF16) for _ in range(2)], "zib")
                xgb = Rot([ar.alloc("xgb", [128, KC, TCW], BF16) for _ in range(2)], "xgb")
                ocb = Rot([ar.alloc("ocb", [128, KC, TCW], BF16) for _ in range(2)], "ocb")
                t5b = Rot([ar.alloc("t5b", [128, TCW], F32) for _ in range(2)], "t5b")
                for tc in range(NTC):
                    ft, ftk = ftt.next()
                    zi_, zik = zib.next()
                    xg, xgk = xgb.next()
                    oc, ock = ocb.next()
                    sl = slice(off + tc * TCW, off + (tc + 1) * TCW)
                    for hh in range(2):
                        P.dma(ft[:, hh * NLT:(hh + 1) * NLT, :], FTt_d[tc, :, hh * NLT:(hh + 1) * NLT, :], writes=[(ftk, hh)])
                    P.dma(zi_[:], srcT[:, :, sl], writes=[zik])
                    P.dma(xg[:], gateT[:, :, sl], writes=[xgk])
                    for j in range(KC):
                        ps, pk = psf.next()
                        for rt in range(NRT):
                            P.mm(ps[:, :TCW], Yc[:, rt, j * 128:(j + 1) * 128], ft[:, rt, :], start=(rt == 0),
                                 stop=(rt == NRT - 1), reads=["Yc", (ftk, rt // NLT)], writes=[pk])
                        t5, t5k = t5b.next()
                        P.stt(t5[:], zi_[:, j, :], hyb[:, l, o, j:j + 1], ps[:, :TCW], ALU.mult, ALU.add,
                              reads=[zik, "const", pk], writes=[t5k])
                        P.tt(oc[:, j, :], t5[:], xg[:, j, :], ALU.mult, reads=[t5k, xgk], writes=[ock], eng="gpsimd")
                    ci = 0 if off == 0 else 1 + tc
                    P.dma(dstT[:, :, sl], oc[:], reads=[ock], writes=[("brT", ci) if o == 1 else "hz1T"])
                P.barrier()
                ar.reset(m2)
            P.barrier()
            ar.reset(m)

        stages = stages or ("adaln", "x0", "n1", "gates", "att", "hgrn", "hy", "wout", "mlp", "final")
        if "adaln" in stages:
            stage_adaln()
        if "x0" in stages:
            stage_x0()
        for l in range(DEPTH):
            lm = ar.mark()
            hT = ar.alloc("hT", [128, KC, TOK], BF16)
            if "n1" in stages:
                stage_norm1(l, hT)
            if "gates" in stages:
                stage_gates(l, hT)
            if "att" in stages:
                am = ar.mark()
                attT = ar.alloc("attT", [128, KC, TOK], BF16)
                stage_attention(l, hT, attT)
                branch_proj(l, 0, lambda ci, s, n: (attT[:, :, s:s + n], [("attT", ci)]), True)
                ar.reset(am)
            if "hy" in stages:
                stage_hy_proj(l, hT)
            if "hgrn" in stages:
                stage_hgrn(l, hT)
                bm = ar.mark()
                branch_proj(l, 1, br_src_loader(), "att" not in stages)
                ar.reset(bm)
            P.barrier()
            ar.reset(lm)
            if "hy" in stages:
                if l == 0:
                    stage_hyena_seq(l, NCTX, 0, tabs["c"], Hs_c)
                stage_hyena_seq(l, NLAT, NCTX, tabs["l"], Hs_l)
                bm = ar.mark()
                branch_proj(l, 2, br_src_loader(), not ("att" in stages or "hgrn" in stages))
                ar.reset(bm)
            if "wout" in stages:
                stage_wout(l)
            if "mlp" in stages:
                stage_mlp(l)
            if stages and "stop_l0" in stages:
                break
        if "final" in stages:
            stage_final()
        if dbg:
            P.barrier()
            for name in dbg:
                if name in scr:
                    a, shp, dt = scr[name]
                    tw = nc.dram_tensor("dbg_" + name, shp, dt, kind="ExternalOutput").ap()
                    P.dma(tw, a, final=True)
        P.emit(sems_eng, sems_dma)
    return nc


def _bf(a):
    return np.asarray(a, dtype=np.float32).astype(ml_dtypes.bfloat16)


def _col(v, k=KC):
    return np.ascontiguousarray(np.asarray(v, np.float32).reshape(k, 128).T)


def make_consts():
    c = {}
    c["c_idf"] = np.eye(128, dtype=np.float32)
    c["c_idb"] = _bf(np.eye(128))
    c["c_onesb"] = _bf(np.ones((128, 128)))
    rmm = np.zeros((128, 128), np.float32)
    for base in (0, 64):
        for i in range(32):
            rmm[base + i + 32, base + i] = -1.0
            rmm[base + i, base + i + 32] = 1.0
    c["c_rm"] = _bf(rmm)
    half = 64
    inv = (10000.0 ** (-np.arange(0, half, 2, dtype=np.float32) / half)).astype(np.float32)
    t = np.arange(NLAT)
    row = (t // 64).astype(np.float32)[:, None] * inv
    colv = (t % 64).astype(np.float32)[:, None] * inv
    ang = np.concatenate([row, row, colv, colv], axis=-1).astype(np.float32)
    mid = 31
    dm = np.zeros((128, 2, 128), np.float32)
    dsel = np.zeros((128, 2, 6), np.float32)
    msk = np.zeros((128, 2, 128), np.float32)
    for cc in range(2):
        for sp in range(64):
            for s_ in range(64):
                dm[cc * 64 + sp, 0, cc * 64 + s_] = float(sp <= s_) - float(sp <= mid)
                dm[cc * 64 + sp, 1, cc * 64 + s_] = float(sp >= s_) - float(sp >= mid)
                msk[cc * 64 + sp, 0, cc * 64 + s_] = float(sp <= s_)
                msk[cc * 64 + sp, 1, cc * 64 + s_] = float(sp >= s_)
            dsel[cc * 64 + sp, 0, cc * 3 + 0] = float(sp <= mid)
            dsel[cc * 64 + sp, 0, cc * 3 + 1] = 1.0
            dsel[cc * 64 + sp, 0, cc * 3 + 2] = float(sp > mid)
            dsel[cc * 64 + sp, 1, cc * 3 + 0] = float(sp >= mid)
            dsel[cc * 64 + sp, 1, cc * 3 + 1] = 1.0
            dsel[cc * 64 + sp, 1, cc * 3 + 2] = float(sp < mid)
    pos = np.arange(128) % 64
    cm = np.zeros((128, 2, 3, 128), np.float32)
    cm[:, 0, 0] = (pos <= mid); cm[:, 0, 1] = (pos > mid); cm[:, 0, 2] = (pos >= mid)
    cm[:, 1, 0] = (pos >= mid); cm[:, 1, 1] = (pos < mid); cm[:, 1, 2] = (pos <= mid)
    c["c_cmk"] = _bf(cm)
    dm2 = np.zeros((128, 2, 128), np.float32)
    for cc in range(2):
        for sp in range(64):
            for s_ in range(64):
                dm2[cc * 64 + sp, 0, cc * 64 + s_] = float(sp > mid) - dm[cc * 64 + sp, 0, cc * 64 + s_]
                dm2[cc * 64 + sp, 1, cc * 64 + s_] = float(sp < mid) - dm[cc * 64 + sp, 1, cc * 64 + s_]
    c["c_dm2"] = dm2
    c["c_dm"] = dm
    c["c_dsel"] = dsel
    c["c_msk"] = np.ascontiguousarray(np.tile(msk, (1, 1, 4)))
    c["c_cos"] = np.ascontiguousarray(np.cos(ang).T.astype(np.float32))
    c["c_sin"] = np.ascontiguousarray(np.sin(ang).T.astype(np.float32))
    return c


def make_hy_tables(n):
    nlt = n // 128
    nrt = 2 * nlt
    tcw = min(n, 512)
    N2 = 2 * n
    t = np.arange(n, dtype=np.int64)[:, None]
    r = np.arange(N2, dtype=np.int64)[None, :]
    f = np.where(r <= n, r, r - n)
    ang = 2.0 * np.pi * ((t * f) % N2).astype(np.float64) / N2
    F = np.where(r <= n, np.cos(ang), np.sin(ang)).astype(np.float32)
    Ft = F.reshape(nlt, 128, nrt, 128).transpose(2, 1, 0, 3)
    FTt = F.reshape(n // tcw, tcw, nrt, 128).transpose(0, 3, 2, 1)
    tt = np.linspace(0.0, 1.0, n, dtype=np.float32)[:, None]
    w = (np.float32(2.0 * math.pi / n) * np.arange(n, dtype=np.float32))[:, None]
    fr = np.linspace(1e-4, 15, 16, dtype=np.float32)[None, :]
    zf = np.concatenate([tt, np.cos(fr * w), -np.sin(fr * w)], axis=-1).astype(np.float32)
    mind, maxd = math.log(1e-2) / 1.5, math.log(1e-2) / 0.3
    deltas = np.abs(np.linspace(mind, maxd, D, dtype=np.float32))
    dec = np.exp(-tt * deltas).astype(np.float32)
    a = np.full(N2, 2.0 / N2, np.float32)
    a[0] = 1.0 / N2
    a[n] = 1.0 / N2
    return {"Ft": _bf(np.ascontiguousarray(Ft)), "FTt": _bf(np.ascontiguousarray(FTt)),
            "zfT": np.ascontiguousarray(zf.T), "dec": dec, "arow": np.ascontiguousarray(a.reshape(nrt, 128).T)}


def make_in_maps(inputs):
    f = lambda k: np.ascontiguousarray(np.asarray(inputs[k], dtype=np.float32))
    shared = dict(make_consts())
    shared["ada_w"] = f("ada_w")
    ab = f("ada_b")
    shared["ada_bc"] = np.ascontiguousarray(np.stack([_col(ab[l], 48) for l in range(DEPTH)], axis=1))
    shared["g1c"] = np.ascontiguousarray(np.stack([_col(f("norm1_g")[l]) for l in range(DEPTH)], axis=1))
    shared["g2c"] = np.ascontiguousarray(np.stack([_col(f("norm2_g")[l]) for l in range(DEPTH)], axis=1))
    shared["w_in"] = f("w_in")
    shared["qkg"] = np.ascontiguousarray(np.stack([f("q_norm_g"), f("k_norm_g")], axis=-1).transpose(1, 0, 2))
    shared["hgn_row"] = np.ascontiguousarray(np.tile(f("hg_norm_g"), (1, 8)))
    shared["hg_lb"] = f("hg_lb_raw")
    for nm, n_ in (("l", NLAT), ("c", NCTX)):
        for k_, v_ in make_hy_tables(n_).items():
            shared[k_ + "_" + nm] = v_
    sw = f("hy_short_w")
    shared["hysw"] = np.ascontiguousarray(sw.reshape(DEPTH, 3, 24, 128).transpose(3, 0, 2, 1))
    shared["hysb"] = np.ascontiguousarray(f("hy_short_b").reshape(DEPTH, 24, 128).transpose(2, 0, 1))
    shared["hyb"] = np.ascontiguousarray(f("hy_bias").reshape(DEPTH, 2, KC, 128).transpose(3, 0, 1, 2))
    shared["hy_w1"] = f("hy_filt_w1")
    shared["hy_w2"] = f("hy_filt_w2")
    shared["hy_w3"] = f("hy_filt_w3")
    fq = f("hy_freq")
    shared["hy_fc"] = np.ascontiguousarray(np.stack([fq[:, 0], fq[:, 1], f("hy_filt_b1"), f("hy_filt_b2")], axis=-1).transpose(1, 0, 2))
    shared["w_branch"] = f("w_branch")
    shared["w_out"] = f("w_out")
    shared["w_mlp1"] = f("w_mlp1")
    shared["w_mlp2"] = f("w_mlp2")
    x, c, ctx, c_ctx = f("x"), f("c"), f("ctx"), f("c_ctx")
    maps = []
    for b in range(8):
        m = dict(shared)
        m["x_b"] = x[b]
        m["ctx_b"] = ctx[b]
        m["ccol"] = np.ascontiguousarray(np.stack([_col(c[b]), _col(c_ctx)], axis=-1))
        maps.append(m)
    return maps


_NC_CACHE = {}


def kernel(**inputs):
    if "nc" not in _NC_CACHE:
        _NC_CACHE["nc"] = build_program()
    nc = _NC_CACHE["nc"]
    maps = make_in_maps(inputs)
    res = run_bass_kernel_spmd(nc, maps, core_ids=list(range(8)))
    return np.stack([np.asarray(r["out"], dtype=np.float32) for r in res.results], axis=0)
```

```python
import contextlib
import math
import numpy as np
import ml_dtypes
import concourse.bass as bass
import concourse.mybir as mybir
from concourse.bass_utils import run_bass_kernel_spmd

F32 = mybir.dt.float32
BF16 = mybir.dt.bfloat16
AF = mybir.ActivationFunctionType
ALU = mybir.AluOpType
AX = mybir.AxisListType

ENGS = ("tensor", "vector", "scalar", "gpsimd", "sync")
N_DMA_SEMS = 24
SAME_ENGINE_SYNC = True

D = 1024
KC = 8
NLAT = 2048
NCTX = 256
TOK = NLAT + NCTX
NTILE = TOK // 128
DEPTH = 2
DFF = 4096
IN_TOTAL = 12800
OFF_Q, OFF_K, OFF_V = 0, 1024, 1280
OFF_HQ, OFF_HF, OFF_HB, OFF_HI, OFF_HG = 1536, 2560, 3584, 4608, 5632
OFF_HY = 6656
OFF_GATE = 9728
EPS = 1e-6
CH = [(0, 256), (256, 512), (768, 512), (1280, 512), (1792, 512)]
SB_BASE = 16512
SB_LIMIT = 229376


class Op:
    __slots__ = ("eng", "fn", "deps", "dma", "sem", "val", "needed", "idx")

    def __init__(self, eng, fn, deps, dma):
        self.eng, self.fn, self.deps, self.dma = eng, fn, deps, dma
        self.sem = None
        self.val = None
        self.needed = False


class Prog:
    def __init__(self, nc):
        self.nc = nc
        self.q = {e: [] for e in ENGS}
        self.last_write = {}
        self.readers = {}
        self.dma_count = 0
        self.dma_hist = [[], []]
        self.final_ops = []

    def op(self, eng, fn, reads=(), writes=(), dma=False, final=False):
        reads = list(reads) + ["ALL"]
        deps = set()
        for r in reads:
            w = self.last_write.get(r)
            if w is not None:
                deps.add(w)
        for wkey in writes:
            w = self.last_write.get(wkey)
            if w is not None:
                deps.add(w)
            for o in self.readers.get(wkey, {}).values():
                deps.add(o)
        o = Op(eng, fn, deps, dma)
        if dma:
            pool = 1 if eng == "gpsimd" else 0
            hist = self.dma_hist[pool]
            n = len(hist)
            o.sem = pool * N_DMA_SEMS + n % N_DMA_SEMS
            o.val = 16 * (n // N_DMA_SEMS + 1)
            if n >= N_DMA_SEMS:
                deps.add(hist[n - N_DMA_SEMS])
            hist.append(o)
        o.idx = len(self.q[eng])
        self.q[eng].append(o)
        rk = ("dma", id(o)) if dma else eng
        for r in reads:
            self.readers.setdefault(r, {})[rk] = o
        for wkey in writes:
            self.last_write[wkey] = o
            self.readers[wkey] = {}
        if final:
            self.final_ops.append(o)
        return o

    def barrier(self):
        self.op("sync", lambda e: e.nop(), reads=(), writes=["ALL"])

    def dma(self, out, in_, reads=(), writes=(), eng="sync", final=False, **kw):
        return self.op(eng, lambda e: e.dma_start(out=out, in_=in_, **kw), reads, writes, dma=True, final=final)

    def mm(self, out, lhsT, rhs, start=True, stop=True, reads=(), writes=()):
        return self.op("tensor", lambda e: e.matmul(out, lhsT, rhs, start=start, stop=stop), reads, writes)

    def tr(self, out, in_, ident, reads=(), writes=()):
        return self.op("tensor", lambda e: e.transpose(out, in_, ident), reads, writes)

    def act(self, out, in_, func, reads=(), writes=(), eng="scalar", **kw):
        return self.op(eng, lambda e: e.activation(out, in_, func, **kw), reads, writes)

    def tt(self, out, in0, in1, op, reads=(), writes=(), eng="vector"):
        return self.op(eng, lambda e: e.tensor_tensor(out, in0, in1, op), reads, writes)

    def ts(self, out, in0, s1, s2, op0, op1=None, reads=(), writes=(), eng="vector", **kw):
        if op1 is None:
            return self.op(eng, lambda e: e.tensor_scalar(out, in0, s1, s2, op0, **kw), reads, writes)
        return self.op(eng, lambda e: e.tensor_scalar(out, in0, s1, s2, op0, op1, **kw), reads, writes)

    def stt(self, out, in0, scalar, in1, op0, op1, reads=(), writes=(), eng="vector"):
        return self.op(eng, lambda e: e.scalar_tensor_tensor(out, in0, scalar, in1, op0, op1), reads, writes)

    def copy(self, out, in_, reads=(), writes=(), eng="vector"):
        if eng == "scalar":
            return self.op(eng, lambda e: e.copy(out, in_), reads, writes)
        return self.op(eng, lambda e: e.tensor_copy(out, in_), reads, writes)

    def recip(self, out, in_, reads=(), writes=()):
        return self.op("vector", lambda e: e.reciprocal(out, in_), reads, writes)

    def memset(self, ap, val, writes=(), eng="vector"):
        return self.op(eng, lambda e: e.memset(ap, val), (), writes)

    def emit(self, sems_eng, sems_dma):
        nc = self.nc
        for e in ENGS:
            for o in self.q[e]:
                for d in o.deps:
                    d.needed = True
        for o in self.final_ops:
            o.needed = True
        for e in ENGS:
            cnt = 0
            for o in self.q[e]:
                if o.dma:
                    o.sem = sems_dma[o.sem]
                    continue
                o.sem = sems_eng[e]
                if o.needed:
                    cnt += 1
                    o.val = cnt
        final_ops = self.final_ops

        def run_engine(ename, eobj):
            waited = {}
            for o in self.q[ename]:
                need = {}
                for d in o.deps:
                    if d.eng == ename and not d.dma:
                        if ename == "tensor" or not SAME_ENGINE_SYNC:
                            continue
                    k = id(d.sem)
                    if k not in need or need[k][1] < d.val:
                        need[k] = (d.sem, d.val)
                for k, (s, v) in need.items():
                    if waited.get(k, 0) >= v:
                        continue
                    eobj.wait_ge(s, v)
                    waited[k] = v
                ins = o.fn(eobj)
                if o.dma:
                    ins.then_inc(o.sem, 16)
                elif o.needed:
                    ins.then_inc(o.sem, 1)
            if ename == "sync":
                for o in final_ops:
                    eobj.wait_ge(o.sem, o.val)

        with nc.Block() as block:
            @block.tensor
            def _(e):
                run_engine("tensor", e)

            @block.vector
            def _(e):
                run_engine("vector", e)

            @block.scalar
            def _(e):
                run_engine("scalar", e)

            @block.gpsimd
            def _(e):
                run_engine("gpsimd", e)

            @block.sync
            def _(e):
                run_engine("sync", e)


def _dtbytes(dt):
    return 2 if dt == BF16 else 4


class Arena:
    def __init__(self, nc):
        self.nc = nc
        self.off = SB_BASE
        self.n = 0
        self.peak = 0

    def alloc(self, name, shape, dt):
        sz = int(np.prod(shape[1:])) * _dtbytes(dt)
        off = (self.off + 63) // 64 * 64
        self.n += 1
        t = self.nc.alloc_sbuf_tensor_at("%s_%d" % (name, self.n), list(shape), dt, offset=off)
        self.off = off + sz
        self.peak = max(self.peak, self.off)
        assert self.off <= SB_LIMIT, ("SBUF overflow", name, self.off)
        return t

    def mark(self):
        return self.off

    def reset(self, m):
        self.off = m


class Rot:
    def __init__(self, tiles, name):
        self.tiles = tiles
        self.keys = ["%s#%d" % (name, i) for i in range(len(tiles))]
        self.i = -1

    def next(self):
        self.i = (self.i + 1) % len(self.tiles)
        return self.tiles[self.i], self.keys[self.i]


def build_program(dbg=None, stages=None):
    dbg = dbg or ()
    nc = bass.Bass("TRN2", target_bir_lowering=False)
    P = Prog(nc)
    ar = Arena(nc)

    def din(name, shape, dt=F32):
        return nc.dram_tensor(name, list(shape), dt, kind="ExternalInput").ap()

    scr = {}

    def dscr(name, shape, dt):
        a = nc.dram_tensor(name, list(shape), dt, kind="Internal").ap()
        scr[name] = (a, list(shape), dt)
        return a

    x_b = din("x_b", [NLAT, D])
    ctx_b = din("ctx_b", [NCTX, D])
    ccol = din("ccol", [128, KC, 2])
    ada_w = din("ada_w", [DEPTH, D, 6 * D])
    ada_bc = din("ada_bc", [128, DEPTH, 48])
    g1c = din("g1c", [128, DEPTH, KC])
    g2c = din("g2c", [128, DEPTH, KC])
    w_in = din("w_in", [DEPTH, D, IN_TOTAL])
    qkg = din("qkg", [128, DEPTH, 2])
    w_branch = din("w_branch", [DEPTH, 3, D, D])
    w_out = din("w_out", [DEPTH, D, D])
    w_mlp1 = din("w_mlp1", [DEPTH, D, DFF])
    w_mlp2 = din("w_mlp2", [DEPTH, DFF, D])
    c_idf = din("c_idf", [128, 128])
    c_idb = din("c_idb", [128, 128], BF16)
    c_onesb = din("c_onesb", [128, 128], BF16)
    c_rm = din("c_rm", [128, 128], BF16)
    c_cos = din("c_cos", [128, NLAT])
    c_sin = din("c_sin", [128, NLAT])
    c_dm = din("c_dm", [128, 2, 128])
    c_dm2 = din("c_dm2", [128, 2, 128])
    c_dsel = din("c_dsel", [128, 2, 6])
    c_msk = din("c_msk", [128, 2, 512])
    c_cmk = din("c_cmk", [128, 2, 3, 128], BF16)
    hgn_row = din("hgn_row", [DEPTH, D])
    hg_lb = din("hg_lb", [DEPTH, 2, D])
    hysw_d = din("hysw", [128, DEPTH, 24, 3])
    hysb_d = din("hysb", [128, DEPTH, 24])
    hyb_d = din("hyb", [128, DEPTH, 2, KC])
    hy_w1 = din("hy_w1", [DEPTH, 33, 64])
    hy_w2 = din("hy_w2", [DEPTH, 64, 64])
    hy_w3 = din("hy_w3", [DEPTH, 64, 4 * D])
    hy_fc = din("hy_fc", [64, DEPTH, 4])
    tabs = {}
    for nm, n_ in (("l", NLAT), ("c", NCTX)):
        nlt = n_ // 128
        tcw = min(n_, 512)
        tabs[nm] = (din("Ft_" + nm, [2 * nlt, 128, nlt, 128], BF16), din("FTt_" + nm, [n_ // tcw, 128, 2 * nlt, tcw], BF16),
                    din("zfT_" + nm, [33, n_]), din("dec_" + nm, [n_, D]), din("arow_" + nm, [128, 2 * nlt]))
    out_d = nc.dram_tensor("out", [NLAT, D], F32, kind="ExternalOutput").ap()

    xT = dscr("xT", [128, KC, TOK], F32)
    hTd = dscr("hTd", [128, KC, TOK], BF16)
    gT = dscr("gT", [3, 128, KC, TOK], BF16)
    mT = dscr("mT", [128, KC, TOK], BF16)
    brT = dscr("brT", [128, KC, TOK], BF16)
    hx1T = dscr("hx1T", [128, KC, TOK], BF16)
    hx2T = dscr("hx2T", [128, KC, TOK], BF16)
    hvT = dscr("hvT", [128, KC, TOK], BF16)
    hz1T = dscr("hz1T", [128, KC, TOK], BF16)
    Hs_l = dscr("Hs_l", [2 * NLAT, 2, D], BF16)
    Hs_c = dscr("Hs_c", [2 * NCTX, 2, D], BF16)
    zq = dscr("zq", [TOK, D], BF16)
    zi = dscr("zi", [TOK, D], BF16)
    zg = dscr("zg", [TOK, D], BF16)
    zf = dscr("zf", [TOK, D], F32)
    zb = dscr("zb", [TOK, D], F32)

    dumped = {}

    def dump(name, ap, shape, dt=F32, reads=()):
        if ("D:" + name) not in dbg or name in dumped:
            return
        dumped[name] = 1
        tw = nc.dram_tensor("dbg_" + name, list(shape), dt, kind="ExternalOutput").ap()
        P.dma(tw, ap, reads=list(reads), final=True)

    with contextlib.ExitStack() as st:
        sems_eng = {e: st.enter_context(nc.semaphore("s_" + e)) for e in ENGS}
        sems_dma = [st.enter_context(nc.semaphore("d%d" % i)) for i in range(2 * N_DMA_SEMS)]
        psf_t = [nc.alloc_psum_tensor("psf%d" % i, [128, 512], F32) for i in range(6)]
        psb_t = [nc.alloc_psum_tensor("psb%d" % i, [128, 1024], BF16) for i in range(2)]
        psf = Rot(psf_t, "psf")
        psb = Rot(psb_t, "psb")

        idf = ar.alloc("idf", [128, 128], F32)
        idb = ar.alloc("idb", [128, 128], BF16)
        onesb = ar.alloc("onesb", [128, 128], BF16)
        rm = ar.alloc("rm", [128, 128], BF16)
        modT = ar.alloc("modT", [128, DEPTH, 48, 2], F32)
        adab = ar.alloc("adab", [128, DEPTH, 48], F32)
        g1s = ar.alloc("g1s", [128, DEPTH, KC], F32)
        g2s = ar.alloc("g2s", [128, DEPTH, KC], F32)
        A1 = ar.alloc("A1", [128, DEPTH, KC, 2], F32)
        A2 = ar.alloc("A2", [128, DEPTH, KC, 2], F32)
        qkgs = ar.alloc("qkgs", [128, DEPTH, 2], F32)
        epsc = ar.alloc("epsc", [128, 1], F32)
        P.memset(epsc[:], EPS, writes=["const"])
        hysw = ar.alloc("hysw", [128, DEPTH, 24, 3], F32)
        hysb = ar.alloc("hysb", [128, DEPTH, 24], F32)
        hyb = ar.alloc("hyb", [128, DEPTH, 2, KC], F32)
        for t_, d_ in ((idf, c_idf), (idb, c_idb), (onesb, c_onesb), (rm, c_rm), (adab, ada_bc),
                       (g1s, g1c), (g2s, g2c), (qkgs, qkg), (hysw, hysw_d), (hysb, hysb_d), (hyb, hyb_d)):
            P.dma(t_[:], d_, writes=["const"])

        def wcast_load(dst, src, key, eng="gpsimd"):
            return P.dma(dst, src, writes=[key], eng=eng)

        def stage_adaln():
            m = ar.mark()
            cs = ar.alloc("cs", [128, KC, 2], F32)
            wb = Rot([ar.alloc("adaw", [128, KC, 512], F32) for _ in range(2)], "adaw")
            P.dma(cs[:], ccol, writes=["cs"])
            P.act(cs[:], cs[:], AF.Silu, reads=["cs"], writes=["cs"])
            for l in range(DEPTH):
                wv = ada_w[l].rearrange("(k p) c -> p k c", p=128)
                for cb in range(12):
                    w, wk = wb.next()
                    P.dma(w[:], wv[:, :, cb * 512:(cb + 1) * 512], writes=[wk])
                    for j in range(4):
                        ot = cb * 4 + j
                        ps, pk = psf.next()
                        for k in range(KC):
                            P.mm(ps[:, 0:2], w[:, k, j * 128:(j + 1) * 128], cs[:, k, :], start=(k == 0),
                                 stop=(k == KC - 1), reads=[wk, "cs"], writes=[pk])
                        P.ts(modT[:, l, ot, :], ps[:, 0:2], adab[:, l, ot:ot + 1], None, ALU.add,
                             reads=[pk, "const"], writes=["modT"])
                for which in range(2):
                    P.ts(A1[:, l, :, which], modT[:, l, 8:16, which], 1.0, None, ALU.add, reads=["modT"], writes=["A1"])
                    P.tt(A1[:, l, :, which], A1[:, l, :, which], g1s[:, l, :], ALU.mult, reads=["A1", "const"], writes=["A1"])
                    P.ts(A2[:, l, :, which], modT[:, l, 32:40, which], 1.0, None, ALU.add, reads=["modT"], writes=["A2"])
                    P.tt(A2[:, l, :, which], A2[:, l, :, which], g2s[:, l, :], ALU.mult, reads=["A2", "const"], writes=["A2"])
            P.barrier()
            ar.reset(m)

        def which_of(s):
            return 1 if s < NCTX else 0

        def stage_x0():
            m = ar.mark()
            xin = Rot([ar.alloc("xin", [128, D], F32) for _ in range(2)], "xin")
            xst = Rot([ar.alloc("xst", [128, KC, 128], F32) for _ in range(2)], "xst")
            for t in range(NTILE):
                src = ctx_b[t * 128:(t + 1) * 128, :] if t < 2 else x_b[(t - 2) * 128:(t - 1) * 128, :]
                xi, xk = xin.next()
                xs, sk = xst.next()
                P.dma(xi[:], src, writes=[xk])
                for half in range(2):
                    ps, pk = psf.next()
                    for kk in range(4):
                        k = half * 4 + kk
                        P.tr(ps[:, kk * 128:(kk + 1) * 128], xi[:, k * 128:(k + 1) * 128], idf[:],
                             reads=[xk, "const"], writes=[pk])
                    for kk in range(4):
                        P.copy(xs[:, half * 4 + kk, :], ps[:, kk * 128:(kk + 1) * 128], reads=[pk], writes=[sk],
                               eng=("vector" if kk % 2 == 0 else "scalar"))
                P.dma(xT[:, :, t * 128:(t + 1) * 128], xs[:], reads=[sk], writes=[("xT", (t - 2) // 4 if t >= 2 else -1)])
            P.barrier()
            ar.reset(m)

        def xkey(ci):
            return ("xT", ci - 1 if ci > 0 else -1)

        def norm_chunk(l, which_norm, xc, xck, n, s, dst, dstk, tmp):
            A = A1 if which_norm == 1 else A2
            shift0 = 0 if which_norm == 1 else 24
            w = which_of(s)
            sq, rstd, t32 = tmp
            P.act(sq[:, :, :n], xc[:, :, :n], AF.Square, reads=[xck], writes=["nsq"])
            ps, pk = psf.next()
            for k in range(KC):
                P.mm(ps[:, :n], onesb[:], sq[:, k, :n], start=(k == 0), stop=(k == KC - 1),
                     reads=["nsq", "const"], writes=[pk])
            P.act(rstd[:, :n], ps[:, :n], AF.Ln, scale=1.0 / D, bias=epsc[:, 0:1], reads=[pk, "const"], writes=["nrstd"])
            P.act(rstd[:, :n], rstd[:, :n], AF.Exp, scale=-0.5, reads=["nrstd"], writes=["nrstd"])
            for k in range(KC):
                P.stt(t32[:, k, :n], xc[:, k, :n], A[:, l, k, w:w + 1], rstd[:, :n], ALU.mult, ALU.mult,
                      reads=[xck, "nrstd", "A1", "A2"], writes=["nt32"])
                P.act(dst[:, k, :n], t32[:, k, :n], AF.Identity, bias=modT[:, l, shift0 + k, w:w + 1],
                      reads=["nt32", "modT"], writes=[dstk])

        def alloc_norm_tmp(nmax):
            return (ar.alloc("nsq", [128, KC, nmax], BF16), ar.alloc("nrstd", [128, nmax], F32),
                    ar.alloc("nt32", [128, KC, nmax], F32))

        def stage_norm1(l, hT):
            m = ar.mark()
            xcb = Rot([ar.alloc("xc", [128, KC, 512], F32) for _ in range(2)], "xc")
            tmp = alloc_norm_tmp(512)
            for ci, (s, n) in enumerate(CH):
                xc, xck = xcb.next()
                P.dma(xc[:, :, :n], xT[:, :, s:s + n], reads=[xkey(ci)], writes=[xck])
                norm_chunk(l, 1, xc, xck, n, s, hT[:, :, s:s + n], ("hT", ci), tmp)
                if "hTd" in dbg:
                    P.dma(hTd[:, :, s:s + n], hT[:, :, s:s + n], reads=[("hT", ci)], writes=["hTd"])
            P.barrier()
            ar.reset(m)

        HT_ALL = [("hT", ci) for ci in range(len(CH))]

        def proj_fm(wsrc, col0, ncols, hT, consumer, wrot):
            wv = wsrc.rearrange("(k p) c -> p k c", p=128)
            prot = Rot(psf_t[0:3], "psf")
            active = []

            def advance():
                for g_ in list(active):
                    try:
                        next(g_)
                    except StopIteration:
                        active.remove(g_)

            for cb in range(0, ncols, 512):
                nb = min(512, ncols - cb)
                w, wk = wrot.next()
                wcast_load(w[:, :, :nb], wv[:, :, col0 + cb:col0 + cb + nb], wk)
                for j in range(nb // 128):
                    for ci, (s, n) in enumerate(CH):
                        ps, pk = prot.next()
                        for k in range(KC):
                            P.mm(ps[:, :n], w[:, k, j * 128:(j + 1) * 128], hT[:, k, s:s + n], start=(k == 0),
                                 stop=(k == KC - 1), reads=[wk, ("hT", ci)], writes=[pk])
                        advance()
                        g_ = consumer((cb // 128) + j, ci, s, n, ps, pk)
                        if g_ is not None:
                            active.append(g_)
            while active:
                advance()

        def stage_gates(l, hT):
            m = ar.mark()
            wrot = Rot([ar.alloc("wg", [128, KC, 512], BF16) for _ in range(2)], "wg")
            stg = Rot([ar.alloc("gst", [128, 512], BF16) for _ in range(4)], "gst")

            def cons(jg, ci, s, n, ps, pk):
                b, bk = stg.next()
                P.act(b[:, :n], ps[:, :n], AF.Sigmoid, reads=[pk], writes=[bk])
                P.dma(gT[jg // 8, :, jg % 8, s:s + n], b[:, :n], reads=[bk], writes=["gT"])
                yield

            proj_fm(w_in[l], OFF_GATE, 3 * D, hT, cons, wrot)
            P.barrier()
            ar.reset(m)

        def branch_proj(l, br, src_fn, first):
            m = ar.mark()
            wb = ar.alloc("wbr", [128, KC, D], BF16)
            wv = w_branch[l, br].rearrange("(k p) c -> p k c", p=128)
            for hh in range(2):
                wcast_load(wb[:, :, hh * 512:(hh + 1) * 512], wv[:, :, hh * 512:(hh + 1) * 512], ("wbr", hh))
            gch = Rot([ar.alloc("gch", [128, KC, 512], BF16) for _ in range(2)], "gch")
            mold = Rot([ar.alloc("mold", [128, KC, 512], BF16) for _ in range(2)], "mold")
            mnew = Rot([ar.alloc("mnew", [128, KC, 512], BF16) for _ in range(2)], "mnew")
            t32 = Rot([ar.alloc("bt32", [128, 512], F32) for _ in range(2)], "bt32")
            for ci, (s, n) in enumerate(CH):
                if l == DEPTH - 1 and ci == 0:
                    continue
                src, srck = src_fn(ci, s, n)
                g, gk = gch.next()
                P.dma(g[:, :, :n], gT[br, :, :, s:s + n], reads=["gT"], writes=[gk])
                mn, mnk = mnew.next()
                if not first:
                    mo, mok = mold.next()
                    P.dma(mo[:, :, :n], mT[:, :, s:s + n], reads=[("mT", ci)], writes=[mok])
                for j in range(KC):
                    ps, pk = psf.next()
                    for k in range(KC):
                        P.mm(ps[:, :n], wb[:, k, j * 128:(j + 1) * 128], src[:, k, :n], start=(k == 0),
                             stop=(k == KC - 1), reads=[("wbr", j // 4)] + list(srck), writes=[pk])
                    if first:
                        P.tt(mn[:, j, :n], ps[:, :n], g[:, j, :n], ALU.mult, reads=[pk, gk], writes=[mnk])
                    else:
                        t, tk = t32.next()
                        P.tt(t[:, :n], ps[:, :n], g[:, j, :n], ALU.mult, reads=[pk, gk], writes=[tk])
                        P.tt(mn[:, j, :n], t[:, :n], mo[:, j, :n], ALU.add, reads=[tk, mok], writes=[mnk], eng="gpsimd")
                P.dma(mT[:, :, s:s + n], mn[:, :, :n], reads=[mnk], writes=[("mT", ci)])
            P.barrier()
            ar.reset(m)

        def stage_attention(l, hT, attT):
            m = ar.mark()
            wrot = Rot([ar.alloc("wqk", [128, KC, 512], BF16) for _ in range(2)], "wqk")
            cosT = ar.alloc("cosT", [128, NLAT], F32)
            sinT = ar.alloc("sinT", [128, NLAT], F32)
            P.dma(cosT[:], c_cos, writes=["rope"])
            P.dma(sinT[:], c_sin, writes=["rope"])
            qT = ar.alloc("qT", [128, 10, TOK], BF16)
            vtm = ar.alloc("vtm", [128, NTILE, 256], BF16)
            sqb = Rot([ar.alloc("sqb", [128, 512], BF16) for _ in range(2)], "sqb")
            qnb = Rot([ar.alloc("qnb", [128, 512], BF16) for _ in range(2)], "qnb")
            rsb = Rot([ar.alloc("rsb", [128, 512], F32) for _ in range(2)], "rsb")
            t1b = Rot([ar.alloc("t1b", [128, 512], F32) for _ in range(2)], "t1b")
            t2b = Rot([ar.alloc("t2b", [128, 512], F32) for _ in range(2)], "t2b")

            def cons_qk(jg, ci, s, n, ps, pk):
                is_k = jg >= 8
                gcol = qkgs[:, l, 1:2] if is_k else qkgs[:, l, 0:1]
                sq, sqk = sqb.next()
                P.act(sq[:, :n], ps[:, :n], AF.Square, reads=[pk], writes=[sqk])
                ps2, pk2 = psf_t[3], "psf#3"
                P.mm(ps2[:, :n], onesb[:], sq[:, :n], reads=[sqk, "const"], writes=[pk2])
                yield
                rs, rsk = rsb.next()
                P.act(rs[:, :n], ps2[:, :n], AF.Ln, scale=1.0 / 128, bias=epsc[:, 0:1], reads=[pk2, "const"], writes=[rsk])
                P.act(rs[:, :n], rs[:, :n], AF.Exp, scale=-0.5, reads=[rsk], writes=[rsk])
                dstk = ("qT", jg, ci)
                if ci == 0:
                    P.stt(qT[:, jg, s:s + n], ps[:, :n], gcol, rs[:, :n], ALU.mult, ALU.mult,
                          reads=[pk, rsk, "const"], writes=[dstk])
                    return
                qn, qnk = qnb.next()
                P.stt(qn[:, :n], ps[:, :n], gcol, rs[:, :n], ALU.mult, ALU.mult, reads=[pk, rsk, "const"], writes=[qnk])
                ps3, pk3 = psf_t[4], "psf#4"
                P.mm(ps3[:, :n], rm[:], qn[:, :n], reads=[qnk, "const"], writes=[pk3])
                yield
                t1, t1k = t1b.next()
                t2, t2k = t2b.next()
                ls = s - NCTX
                P.tt(t1[:, :n], qn[:, :n], cosT[:, ls:ls + n], ALU.mult, reads=[qnk, "rope"], writes=[t1k], eng="gpsimd")
                P.tt(t2[:, :n], ps3[:, :n], sinT[:, ls:ls + n], ALU.mult, reads=[pk3, "rope"], writes=[t2k])
                P.tt(qT[:, jg, s:s + n], t1[:, :n], t2[:, :n], ALU.add, reads=[t1k, t2k], writes=[dstk])

            proj_fm(w_in[l], OFF_Q, 1024 + 256, hT, cons_qk, wrot)
            wv_t, wvk = wrot.next()
            wcast_load(wv_t[:, :, :256], w_in[l].rearrange("(k p) c -> p k c", p=128)[:, :, OFF_V:OFF_V + 256], wvk)
            for t in range(NTILE):
                ps, pk = psf.next()
                ci = 0 if t < 2 else 1 + (t - 2) // 4
                for k in range(KC):
                    P.mm(ps[:, :256], hT[:, k, t * 128:(t + 1) * 128], wv_t[:, k, :256], start=(k == 0),
                         stop=(k == KC - 1), reads=[wvk, ("hT", ci)], writes=[pk])
                P.copy(vtm[:, t, :], ps[:, :256], reads=[pk], writes=[("vtm", t)], eng="scalar")
            pTb = Rot([ar.alloc("pTb", [128, 512], BF16) for _ in range(3)], "pTb")
            psO_t, psS_t = psf_t[4], psf_t[5]
            psST = Rot(psf_t[0:4], "psf")
            rsum = ar.alloc("rsum", [128, 512], F32)
            scale = 128.0 ** -0.5
            for h in range(8):
                g = h // 4
                for ci, (s, n) in enumerate(CH):
                    if l == DEPTH - 1 and ci == 0:
                        continue
                    kts = [0, 1] if ci == 0 else list(range(NTILE))
                    kc_of = lambda kt: 0 if kt < 2 else 1 + (kt - 2) // 4
                    pend = None
                    for i, kt in enumerate(kts + [None]):
                        cur = None
                        if kt is not None:
                            ps, pk = psST.next()
                            P.mm(ps[:, :n], qT[:, 8 + g, kt * 128:(kt + 1) * 128], qT[:, h, s:s + n],
                                 reads=[("qT", 8 + g, kc_of(kt)), ("qT", h, ci)], writes=[pk])
                            pt, ptk = pTb.next()
                            P.act(pt[:, :n], ps[:, :n], AF.Exp, scale=scale, reads=[pk], writes=[ptk])
                            cur = (kt, pt, ptk)
                        if pend is not None:
                            pkt, ppt, pptk = pend
                            first = (pkt == kts[0])
                            last = (pkt == kts[-1])
                            P.mm(psO_t[:, :n], vtm[:, pkt, g * 128:(g + 1) * 128], ppt[:, :n], start=first, stop=last,
                                 reads=[("vtm", pkt), pptk], writes=["psf#4"])
                            P.mm(psS_t[:, :n], onesb[:], ppt[:, :n], start=first, stop=last,
                                 reads=["const", pptk], writes=["psf#5"])
                        pend = cur
                    P.recip(rsum[:, :n], psS_t[:, :n], reads=["psf#5"], writes=["rsum"])
                    P.tt(attT[:, h, s:s + n], psO_t[:, :n], rsum[:, :n], ALU.mult, reads=["psf#4", "rsum"],
                         writes=[("attT", ci)])
            if "brT" in dbg:
                P.dma(brT[:], attT[:], reads=[("attT", ci) for ci in range(5)], writes=["brT"])
            P.barrier()
            ar.reset(m)

        def stage_wout(l):
            m = ar.mark()
            wo = ar.alloc("wo", [128, KC, D], BF16)
            wv = w_out[l].rearrange("(k p) c -> p k c", p=128)
            for hh in range(2):
                wcast_load(wo[:, :, hh * 512:(hh + 1) * 512], wv[:, :, hh * 512:(hh + 1) * 512], ("wo", hh))
            mch = Rot([ar.alloc("mch", [128, KC, 512], BF16) for _ in range(2)], "mch")
            xcb = Rot([ar.alloc("xc", [128, KC, 512], F32) for _ in range(2)], "xc")
            for ci, (s, n) in enumerate(CH):
                if l == DEPTH - 1 and ci == 0:
                    continue
                w = which_of(s)
                mc, mck = mch.next()
                xc, xck = xcb.next()
                P.dma(mc[:, :, :n], mT[:, :, s:s + n], reads=[("mT", ci)], writes=[mck])
                P.dma(xc[:, :, :n], xT[:, :, s:s + n], reads=[xkey(ci)], writes=[xck])
                for j in range(KC):
                    ps, pk = psf.next()
                    for k in range(KC):
                        P.mm(ps[:, :n], wo[:, k, j * 128:(j + 1) * 128], mc[:, k, :n], start=(k == 0),
                             stop=(k == KC - 1), reads=[("wo", j // 4), mck], writes=[pk])
                    P.stt(xc[:, j, :n], ps[:, :n], modT[:, l, 16 + j, w:w + 1], xc[:, j, :n], ALU.mult, ALU.add,
                          reads=[pk, xck, "modT"], writes=[xck])
                P.dma(xT[:, :, s:s + n], xc[:, :, :n], reads=[xck], writes=[xkey(ci)])
            P.barrier()
            ar.reset(m)

        def stage_mlp(l):
            m = ar.mark()
            NM = 256
            w1 = ar.alloc("w1", [128, KC, DFF], BF16)
            w2 = ar.alloc("w2", [128, 32, D], BF16)
            w1v = w_mlp1[l].rearrange("(k p) c -> p k c", p=128)
            w2v = w_mlp2[l].rearrange("(f p) c -> p f c", p=128)
            for i in range(8):
                wcast_load(w1[:, :, i * 512:(i + 1) * 512], w1v[:, :, i * 512:(i + 1) * 512], ("w1", i))
            for i in range(8):
                wcast_load(w2[:, i * 4:(i + 1) * 4, :], w2v[:, i * 4:(i + 1) * 4, :], ("w2", i))
            xcb = Rot([ar.alloc("xc", [128, KC, NM], F32) for _ in range(2)], "xc")
            h2b = Rot([ar.alloc("h2", [128, KC, NM], BF16) for _ in range(2)], "h2")
            ub = Rot([ar.alloc("u", [128, 32, NM], BF16) for _ in range(1)], "u")
            rl = Rot([ar.alloc("rl", [128, NM], F32) for _ in range(3)], "rl")
            tmp = alloc_norm_tmp(NM)
            starts = [s_ for s_ in range(0, TOK, NM) if not (l == DEPTH - 1 and s_ < NCTX)]
            n = NM

            def prep(s):
                ci = 0 if s < NCTX else 1 + (s - NCTX) // 512
                xc, xck = xcb.next()
                h2, h2k = h2b.next()
                P.dma(xc[:], xT[:, :, s:s + n], reads=[xkey(ci)], writes=[xck])
                norm_chunk(l, 2, xc, xck, n, s, h2, h2k, tmp)
                return xc, xck, h2, h2k, ci

            nxt = prep(starts[0])
            for i_, s in enumerate(starts):
                xc, xck, h2, h2k, ci = nxt
                u, uk = ub.next()
                for f in range(32):
                    ps, pk = psf.next()
                    for k in range(KC):
                        P.mm(ps[:, :n], w1[:, k, f * 128:(f + 1) * 128], h2[:, k, :n], start=(k == 0),
                             stop=(k == KC - 1), reads=[("w1", f // 4), h2k], writes=[pk])
                    r, rk = rl.next()
                    P.act(r[:, :n], ps[:, :n], AF.Relu, reads=[pk], writes=[rk])
                    P.tt(u[:, f, :n], r[:, :n], r[:, :n], ALU.mult, reads=[rk], writes=[uk],
                         eng=("gpsimd" if f % 2 == 0 else "vector"))
                if i_ + 1 < len(starts):
                    nxt = prep(starts[i_ + 1])
                w = which_of(s)
                for j in range(KC):
                    ps, pk = psf.next()
                    for f in range(32):
                        P.mm(ps[:, :n], w2[:, f, j * 128:(j + 1) * 128], u[:, f, :n], start=(f == 0), stop=(f == 31),
                             reads=[("w2", f // 4), uk], writes=[pk])
                    P.stt(xc[:, j, :n], ps[:, :n], modT[:, l, 40 + j, w:w + 1], xc[:, j, :n], ALU.mult, ALU.add,
                          reads=[pk, xck, "modT"], writes=[xck])
                P.dma(xT[:, :, s:s + n], xc[:], reads=[xck], writes=[xkey(ci)])
            P.barrier()
            ar.reset(m)

        def stage_final():
            m = ar.mark()
            xin = Rot([ar.alloc("fxi", [128, KC, 128], F32) for _ in range(2)], "fxi")
            xo = Rot([ar.alloc("fxo", [128, D], F32) for _ in range(2)], "fxo")
            for t in range(2, NTILE):
                ci = 1 + (t - 2) // 4
                xi, xk = xin.next()
                o, ok = xo.next()
                P.dma(xi[:], xT[:, :, t * 128:(t + 1) * 128], reads=[xkey(ci)], writes=[xk])
                for half in range(2):
                    ps, pk = psf.next()
                    for kk in range(4):
                        k = half * 4 + kk
                        P.tr(ps[:, kk * 128:(kk + 1) * 128], xi[:, k, :], idf[:], reads=[xk, "const"], writes=[pk])
                    P.copy(o[:, half * 512:(half + 1) * 512], ps[:], reads=[pk], writes=[ok],
                           eng=("vector" if half == 0 else "scalar"))
                P.dma(out_d[(t - 2) * 128:(t - 1) * 128, :], o[:], reads=[ok], final=True)
            ar.reset(m)

        def stage_hgrn(l, hT):
            m = ar.mark()
            m1 = ar.mark()
            wrot = Rot([ar.alloc("wh", [128, KC, 512], BF16) for _ in range(2)], "wh")
            st32 = Rot([ar.alloc("hs32", [128, 512], F32) for _ in range(3)], "hs32")
            st16 = Rot([ar.alloc("hs16", [128, 512], BF16) for _ in range(3)], "hs16")
            dsts = [(zq, BF16), (zf, F32), (zb, F32), (zi, BF16), (zg, BF16)]
            wv = w_in[l].rearrange("(k p) c -> p k c", p=128)
            for cb in range(10):
                w, wk = wrot.next()
                wcast_load(w[:], wv[:, :, OFF_HQ + cb * 512:OFF_HQ + (cb + 1) * 512], wk)
                dst, dt = dsts[cb // 2]
                c0 = (cb % 2) * 512
                for t in range(NTILE):
                    ps, pk = psf.next()
                    ci = 0 if t < 2 else 1 + (t - 2) // 4
                    for k in range(KC):
                        P.mm(ps[:], hT[:, k, t * 128:(t + 1) * 128], w[:, k, :], start=(k == 0), stop=(k == KC - 1),
                             reads=[wk, ("hT", ci)], writes=[pk])
                    sb, sk = (st32 if dt == F32 else st16).next()
                    P.copy(sb[:], ps[:], reads=[pk], writes=[sk], eng=("vector" if t % 2 == 0 else "scalar"))
                    P.dma(dst[t * 128:(t + 1) * 128, c0:c0 + 512], sb[:], reads=[sk], writes=[("zh", cb // 2, t)])
            P.barrier()
            ar.reset(m1)
            Dm = ar.alloc("Dm", [128, 2, 128], F32)
            Dsel = ar.alloc("Dsel", [128, 2, 6], F32)
            msk = ar.alloc("msk", [128, 2, 512], F32)
            gn = ar.alloc("gn", [128, D], F32)
            lbt = ar.alloc("lbt", [128, 2, D], F32)
            oml = ar.alloc("oml", [128, 2, D], F32)
            P.dma(Dm[:], c_dm, writes=["hgc"])
            P.dma(Dsel[:], c_dsel, writes=["hgc"])
            P.dma(msk[:], c_msk, writes=["hgc"])
            P.dma(gn[:], hgn_row[l].partition_broadcast(128), writes=["hgc"])
            if l == 0:
                P.memset(lbt[:], 0.0, writes=["lbt"])
                P.memset(oml[:], 1.0, writes=["oml"])
            else:
                for dr in range(2):
                    P.dma(lbt[:, dr, :], hg_lb[1, dr].partition_broadcast(128), writes=["lbt"])
                    P.dma(oml[:, dr, :], hg_lb[0, dr].partition_broadcast(128), writes=["oml"])
                P.tt(lbt[:], lbt[:], oml[:], ALU.subtract, reads=["lbt", "oml"], writes=["lbt"])
                P.act(lbt[:], lbt[:], AF.Sigmoid, reads=["lbt"], writes=["lbt"])
                P.ts(oml[:], lbt[:], -1.0, 1.0, ALU.mult, ALU.add, reads=["lbt"], writes=["oml"])
            o_b = ar.alloc("o_b", [128, NTILE, D], BF16)
            S = ar.alloc("S", [128, 2, KC, 128], F32)
            P.memset(S[:], 0.0, writes=[("S%d" % d_, h_) for d_ in range(2) for h_ in range(KC)])
            qin = Rot([ar.alloc("hq", [128, D], BF16) for _ in range(2)], "hq")
            zin = Rot([ar.alloc("hz", [128, D], F32) for _ in range(2)], "hz")
            vin = Rot([ar.alloc("hv", [128, D], BF16) for _ in range(3)], "hv")
            gin = Rot([ar.alloc("hgi", [128, D], BF16) for _ in range(2)], "hgi")
            sig = ar.alloc("sig", [128, D], F32)
            logf = ar.alloc("logf", [128, D], F32)
            kk = ar.alloc("kk", [128, D], F32)
            Ep = ar.alloc("Ep", [128, D], F32)
            Em = ar.alloc("Em", [128, D], F32)
            qt = ar.alloc("qt", [128, D], BF16)
            ktR = Rot([ar.alloc("kt", [128, D], BF16) for _ in range(2)], "kt")
            kt2R = Rot([ar.alloc("kt2", [128, D], BF16) for _ in range(2)], "kt2")
            Dm2 = ar.alloc("Dm2", [128, 2, 128], F32)
            zbf = ar.alloc("zbf", [128, 512], BF16)
            P.memset(zbf[:], 0.0, writes=["hgc"])
            P.dma(Dm2[:], c_dm2, writes=["hgc"])
            qtTR = Rot([ar.alloc("qtT", [128, KC, 128], BF16) for _ in range(2)], "qtT")
            ktTR = Rot([ar.alloc("ktT", [128, KC, 128], BF16) for _ in range(2)], "ktT")
            ktBR = Rot([ar.alloc("ktB", [128, KC, 128], BF16) for _ in range(2)], "ktB")
            ktF = ar.alloc("ktF", [128, KC, 128], BF16)
            qtCR = Rot([ar.alloc("qtC", [128, KC, 128], BF16) for _ in range(2)], "qtC")
            cmk = ar.alloc("cmk", [128, 2, 3, 128], BF16)
            P.dma(cmk[:], c_cmk, writes=["hgc"])
            sclR = Rot([ar.alloc("scl", [128, KC, 6], F32) for _ in range(2)], "scl")
            sm = ar.alloc("sm", [128, KC, 128], BF16)
            Ss = ar.alloc("Ss", [128, KC, 128], BF16)
            o32 = ar.alloc("o32", [128, D], F32)
            sq32 = ar.alloc("sq32", [128, D], F32)
            silR = Rot([ar.alloc("sil", [128, D], BF16) for _ in range(2)], "sil")
            ssum = ar.alloc("ssum", [128, KC], F32)
            hgb = ar.alloc("hgb", [128, D], BF16)
            hgT = Rot([ar.alloc("hgT", [128, KC, 128], BF16) for _ in range(1)], "hgT")
            B0, B1, B2, B3, B4, B5 = ["psf#%d" % i for i in range(6)]
            p0, p1, p2, p3, p4, p5 = psf_t

            def tile_pass(t, dr):
                Sk = "S%d" % dr
                kt, kt_k = ktR.next()
                kt2, kt2_k = kt2R.next()
                qtT, qtT_k = qtTR.next()
                qtC, qtC_k = qtCR.next()
                ktT, ktT_k = ktTR.next()
                ktB, ktB_k = ktBR.next()
                scl, scl_k = sclR.next()
                q, qk = qin.next()
                z, zk = zin.next()
                v, vk = vin.next()
                r0 = t * 128
                P.dma(q[:], zq[r0:r0 + 128, :], reads=[("zh", 0, t)], writes=[qk])
                P.dma(z[:], (zf if dr == 0 else zb)[r0:r0 + 128, :], reads=[("zh", 1 + dr, t)], writes=[zk])
                P.dma(v[:], zi[r0:r0 + 128, :], reads=[("zh", 3, t)], writes=[vk])
                if dr == 0:
                    g, gk = gin.next()
                    P.dma(g[:], zg[r0:r0 + 128, :], reads=[("zh", 4, t)], writes=[gk])
                yield "P1"
                P.act(sig[:], z[:], AF.Sigmoid, reads=[zk], writes=["sig"])
                if dr == 0:
                    sil, silk = silR.next()
                    P.act(sil[:], g[:], AF.Silu, reads=[gk], writes=[silk])
                    P.tt(sil[:], sil[:], gn[:], ALU.mult, reads=[silk, "hgc"], writes=[silk], eng="gpsimd")
                yield
                P.tt(sig[:], sig[:], oml[:, dr, :], ALU.mult, reads=["sig", "oml"], writes=["sig"])
                yield
                P.tt(sig[:], sig[:], lbt[:, dr, :], ALU.add, reads=["sig", "lbt"], writes=["sig"])
                yield
                P.ts(kk[:], sig[:], -1.0, 1.0, ALU.mult, ALU.add, reads=["sig"], writes=["kk"], eng="gpsimd")
                P.ts(logf[:], sig[:], 1e-6, None, ALU.max, reads=["sig"], writes=["logf"])
                yield
                P.act(logf[:], logf[:], AF.Ln, reads=["logf"], writes=["logf"])
                yield
                for hf_ in range(2):
                    cs_ = slice(hf_ * 512, (hf_ + 1) * 512)
                    P.mm(p0[:], Dm[:, dr, :], logf[:, cs_], reads=["logf", "hgc"], writes=[B0])
                    yield
                    P.act(Ep[:, cs_], p0[:], AF.Exp, reads=[B0], writes=["Ep"])
                    P.act(Em[:, cs_], p0[:], AF.Exp, scale=-1.0, reads=[B0], writes=["Em"])
                    yield
                P.tt(qt[:], q[:], Ep[:], ALU.mult, reads=[qk, "Ep"], writes=["qt"])
                P.tt(kt[:], kk[:], Em[:], ALU.mult, reads=["kk", "Em"], writes=[kt_k], eng="gpsimd")
                yield
                for hf_ in range(2):
                    cs_ = slice(hf_ * 512, (hf_ + 1) * 512)
                    P.mm(p0[:], Dm2[:, dr, :], logf[:, cs_], reads=["logf", "hgc"], writes=[B0])
                    yield
                    P.act(Em[:, cs_], p0[:], AF.Exp, reads=[B0, kt_k], writes=["Em"])
                    yield
                P.tt(kt2[:], kk[:], Em[:], ALU.mult, reads=["kk", "Em"], writes=[kt2_k])
                for h in range(KC):
                    P.mm(p0[:, h * 6:(h + 1) * 6], logf[:, h * 128:(h + 1) * 128], Dsel[:, dr, :],
                         reads=["logf", "hgc"], writes=[B0])
                yield
                P.act(scl[:].rearrange("p h c -> p (h c)"), p0[:, 0:48], AF.Exp, reads=[B0], writes=[scl_k])
                fl = lambda a_: a_[:].rearrange("p h c -> p (h c)")
                for h in range(KC):
                    P.tr(psb_t[0][:, h * 128:(h + 1) * 128], qt[:, h * 128:(h + 1) * 128], idb[:],
                         reads=["qt", "const"], writes=["psb#0"])
                yield
                P.copy(fl(qtT), psb_t[0][:], reads=["psb#0"], writes=[qtT_k], eng="scalar")
                yield
                for h in range(KC):
                    P.tr(psb_t[0][:, h * 128:(h + 1) * 128], kt[:, h * 128:(h + 1) * 128], idb[:],
                         reads=[kt_k, "const"], writes=["psb#0"])
                P.tt(qtC[:], qtT[:], cmk[:, dr, 2, :].unsqueeze(1).to_broadcast([128, KC, 128]), ALU.mult,
                     reads=[qtT_k, "hgc"], writes=[qtC_k], eng="gpsimd")
                yield
                P.copy(fl(ktF), psb_t[0][:], reads=["psb#0"], writes=["ktF"], eng="scalar")
                yield
                P.tt(ktT[:], ktF[:], cmk[:, dr, 0, :].unsqueeze(1).to_broadcast([128, KC, 128]), ALU.mult,
                     reads=["ktF", "hgc"], writes=[ktT_k])
                P.tt(ktB[:], ktF[:], cmk[:, dr, 1, :].unsqueeze(1).to_broadcast([128, KC, 128]), ALU.mult,
                     reads=["ktF", "hgc"], writes=[ktB_k], eng="gpsimd")
                yield "P2"
                for hg in range(2):
                    for hh in range(4):
                        h = hg * 4 + hh
                        P.mm(p1[:, hh * 128:(hh + 1) * 128], ktT[:, h, :], qtT[:, h, :], start=True, stop=False,
                             reads=[ktT_k, qtT_k], writes=[B1])
                        P.mm(p1[:, hh * 128:(hh + 1) * 128], ktB[:, h, :], qtC[:, h, :], start=False, stop=True,
                             reads=[ktB_k, qtC_k], writes=[B1])
                    yield
                    P.tt(sm[:, hg * 4:(hg + 1) * 4, :].rearrange("p h c -> p (h c)"), p1[:], msk[:, dr, :], ALU.mult,
                         reads=[B1, "hgc"], writes=[("sm", hg)])
                    yield
                for hf_ in range(2):
                    pp, bk = (p2, B2) if hf_ == 0 else (p3, B3)
                    if dr == 0:
                        P.mm(pp[:], idb[:], o_b[:, t, hf_ * 512:(hf_ + 1) * 512], start=True, stop=False,
                             reads=["const", ("o_b", t)], writes=[bk])
                    else:
                        P.mm(pp[:], idb[:], zbf[:], start=True, stop=False, reads=["const", "hgc"], writes=[bk])
                for h in range(KC):
                    pp, bk = (p2, B2) if h < 4 else (p3, B3)
                    P.mm(pp[:, (h % 4) * 128:(h % 4 + 1) * 128], sm[:, h, :], v[:, h * 128:(h + 1) * 128],
                         start=False, stop=False, reads=[("sm", h // 4), vk], writes=[bk])
                yield
                order = (0, 1) if dr == 0 else (1, 0)
                for oi, cc in enumerate(order):
                    lo, hi = cc * 64, cc * 64 + 64
                    P.tt(Ss[:], S[:, dr, :, :], scl[:, :, cc * 3:cc * 3 + 1].to_broadcast([128, KC, 128]), ALU.mult,
                         reads=[(Sk, h_) for h_ in range(KC)] + [scl_k], writes=["Ss"])
                    for h in range(KC):
                        pp, bk = (p4, B4) if h < 4 else (p5, B5)
                        P.mm(pp[:, (h % 4) * 128:(h % 4 + 1) * 128], kt2[lo:hi, h * 128:(h + 1) * 128],
                             v[lo:hi, h * 128:(h + 1) * 128], reads=[kt2_k, vk], writes=[bk])
                    yield
                    for h in range(KC):
                        pp, bk = (p2, B2) if h < 4 else (p3, B3)
                        P.mm(pp[lo:hi, (h % 4) * 128:(h % 4 + 1) * 128], qtT[:, h, lo:hi], Ss[:, h, :],
                             start=False, stop=False, reads=[qtT_k, "Ss"], writes=[bk])
                    yield
                    for h in range(KC):
                        pp, bk = (p4, B4) if h < 4 else (p5, B5)
                        P.stt(S[:, dr, h, :], S[:, dr, h, :], scl[:, h, cc * 3 + 1:cc * 3 + 2],
                              pp[:, (h % 4) * 128:(h % 4 + 1) * 128], ALU.mult, ALU.add,
                              reads=[(Sk, h), scl_k, bk], writes=[(Sk, h)])
                        if h == 3:
                            yield
                    yield
                for hf_ in range(2):
                    pp, bk = (p2, B2) if hf_ == 0 else (p3, B3)
                    P.mm(pp[:], idb[:], zbf[:], start=False, stop=True, reads=["const", "hgc"], writes=[bk])
                if dr == 1:
                    P.copy(o_b[:, t, 0:512], p2[:], reads=[B2], writes=[("o_b", t)], eng="scalar")
                    P.copy(o_b[:, t, 512:1024], p3[:], reads=[B3], writes=[("o_b", t)])
                    return
                P.act(sq32[:, 0:512], p2[:], AF.Square, reads=[B2], writes=["sq32"])
                P.act(sq32[:, 512:1024], p3[:], AF.Square, reads=[B3], writes=["sq32"])
                yield
                P.op("vector", lambda e: e.reduce_sum(ssum[:], sq32[:].rearrange("p (h c) -> p h c", h=KC), AX.X),
                     reads=["sq32"], writes=["ssum"])
                yield
                P.act(ssum[:], ssum[:], AF.Ln, scale=1.0 / 128, bias=epsc[:, 0:1], reads=["ssum", "const"], writes=["ssum"])
                P.act(ssum[:], ssum[:], AF.Exp, scale=-0.5, reads=["ssum"], writes=["ssum"])
                yield
                for hf_ in range(2):
                    pp, bk = (p2, B2) if hf_ == 0 else (p3, B3)
                    P.tt(o32[:, hf_ * 512:(hf_ + 1) * 512].rearrange("p (h c) -> p h c", h=4),
                         pp[:].rearrange("p (h c) -> p h c", h=4),
                         ssum[:, hf_ * 4:(hf_ + 1) * 4].unsqueeze(2).to_broadcast([128, 4, 128]), ALU.mult,
                         reads=[bk, "ssum"], writes=["o32"])
                    yield
                P.tt(hgb[:], o32[:], sil[:], ALU.mult, reads=["o32", silk], writes=["hgb"])
                yield
                pb, pbk = psb_t[1], "psb#1"
                for h in range(KC):
                    P.tr(pb[:, h * 128:(h + 1) * 128], hgb[:, h * 128:(h + 1) * 128], idb[:], reads=["hgb", "const"], writes=[pbk])
                yield
                ht, htk = hgT.next()
                P.copy(ht[:].rearrange("p h c -> p (h c)"), pb[:], reads=[pbk], writes=[htk], eng="scalar")
                ci = 0 if t < 2 else 1 + (t - 2) // 4
                P.dma(brT[:, :, r0:r0 + 128], ht[:], reads=[htk], writes=[("brT", ci)])

            seq = [(t, 1) for t in [1, 0] + list(range(NTILE - 1, 1, -1))] + [(t, 0) for t in range(NTILE)]
            gens = [tile_pass(t, dr) for t, dr in seq]
            NG = len(gens)

            def run_to(g_, marker):
                for v_ in g_:
                    if v_ == marker:
                        return

            def step(g_):
                try:
                    return next(g_)
                except StopIteration:
                    return "END"

            run_to(gens[0], "P1")
            run_to(gens[0], "P2")
            run_to(gens[1], "P1")
            for i in range(NG):
                if i + 2 < NG:
                    run_to(gens[i + 2], "P1")
                d1 = i + 1 >= NG
                d2 = False
                while not (d1 and d2):
                    if not d2:
                        d2 = step(gens[i]) == "END"
                    if not d1:
                        d1 = step(gens[i + 1]) in ("P2", "END")
            P.barrier()
            ar.reset(m)

        def br_src_loader():
            bufs = Rot([ar.alloc("brc", [128, KC, 512], BF16) for _ in range(2)], "brc")

            def fn(ci, s, n):
                b, bk = bufs.next()
                P.dma(b[:, :, :n], brT[:, :, s:s + n], reads=[("brT", ci)], writes=[bk])
                return b, [bk]
            return fn

        def stage_hy_proj(l, hT):
            m = ar.mark()
            wrot = Rot([ar.alloc("wy", [128, KC, 512], BF16) for _ in range(2)], "wy")
            padl = Rot([ar.alloc("padl", [128, NLAT + 2], F32) for _ in range(2)], "padl")
            padc = Rot([ar.alloc("padc", [128, NCTX + 2], F32) for _ in range(2)], "padc")
            ubuf = Rot([ar.alloc("hyu", [128, NLAT], BF16) for _ in range(2)], "hyu")
            tmpu = ar.alloc("hytmp", [128, NLAT], F32)
            for r_ in (padl, padc):
                for t_, k_ in zip(r_.tiles, r_.keys):
                    P.memset(t_[:], 0.0, writes=[k_])
            cur = {}

            def conv(pad, padk, n, jg, off):
                w = lambda i: hysw[:, l, jg, i:i + 1]
                P.ts(tmpu[:, :n], pad[:, 0:n], w(0), hysb[:, l, jg:jg + 1], ALU.mult, ALU.add,
                     reads=[padk, "const"], writes=["hytmp"])
                P.stt(tmpu[:, :n], pad[:, 1:n + 1], w(1), tmpu[:, :n], ALU.mult, ALU.add,
                      reads=[padk, "const", "hytmp"], writes=["hytmp"])
                u, uk = ubuf.next()
                P.stt(u[:, :n], pad[:, 2:n + 2], w(2), tmpu[:, :n], ALU.mult, ALU.add,
                      reads=[padk, "const", "hytmp"], writes=[uk])
                dst = (hx1T, hx2T, hvT)[jg // 8]
                P.dma(dst[:, jg % 8, off:off + n], u[:, :n], reads=[uk], writes=["hyT"])

            def cons(jg, ci, s, n, ps, pk):
                if ci == 0:
                    cur["c"] = padc.next()
                    cur["l"] = padl.next()
                    pc, pck = cur["c"]
                    P.copy(pc[:, 1:1 + n], ps[:, :n], reads=[pk], writes=[pck], eng="scalar")
                    if l == 0:
                        conv(pc, pck, NCTX, jg, 0)
                else:
                    pl_, plk = cur["l"]
                    a = 1 + s - NCTX
                    P.copy(pl_[:, a:a + n], ps[:, :n], reads=[pk], writes=[plk], eng=("scalar" if ci % 2 else "vector"))
                    if ci == 4:
                        conv(pl_, plk, NLAT, jg, NCTX)
                yield

            proj_fm(w_in[l], OFF_HY, 3 * D, hT, cons, wrot)
            P.barrier()
            ar.reset(m)

        def stage_hyena_seq(l, n, off, tabs, Hs_d):
            NLT = n // 128
            NRT = 2 * NLT
            TCW = min(n, 512)
            NTC = n // TCW
            Ft_d, FTt_d, zfT_d, dec_d, arow_d = tabs
            m = ar.mark()
            arow = ar.alloc("arow", [128, NRT], F32)
            P.dma(arow[:], arow_d, writes=["hyk"])
            mf = ar.mark()
            w1s = ar.alloc("w1s", [64, 64], F32)
            w2s = ar.alloc("w2s", [64, 64], F32)
            fc = ar.alloc("fc", [64, 4], F32)
            fb = ar.alloc("fb", [64, 2], F32)
            zfT = ar.alloc("zfT", [64, n], F32)
            h1T = ar.alloc("h1T", [64, n], F32)
            h2Tb = ar.alloc("h2Tb", [64, n], BF16)
            w3b = ar.alloc("w3b", [64, 4 * D], BF16)
            v32 = ar.alloc("v32", [64, 512], F32)
            kint = ar.alloc("kint", [64, 512], mybir.dt.int32)
            kf = ar.alloc("kf", [64, 512], F32)
            mg = ar.alloc("mg", [64, 512], F32)
            P.memset(w1s[:], 0.0, writes=["hyk"])
            P.memset(zfT[:], 0.0, writes=["hyk"])
            P.dma(w1s[0:33, :], hy_w1[l], writes=["hyk"])
            P.dma(w2s[:], hy_w2[l], writes=["hyk"])
            P.dma(fc[:], hy_fc[:, l, :], writes=["hyk"])
            P.dma(zfT[0:33, :], zfT_d, writes=["hyk"])
            for i_ in range(8):
                P.dma(w3b[:, i_ * 512:(i_ + 1) * 512], hy_w3[l][:, i_ * 512:(i_ + 1) * 512], writes=["hyk"], eng="gpsimd")
            P.tt(fb[:], fc[:, 2:4], fc[:, 0:2], ALU.mult, reads=["hyk"], writes=["fb"])

            def mlp_layer(lhsT, rhs, K, li, out):
                for pc in range(0, n, 512):
                    np_ = min(512, n - pc)
                    ps, pk = psf.next()
                    P.mm(ps[0:64, :np_], lhsT[0:K, :], rhs[0:K, pc:pc + np_], reads=["hyk", "h1T"], writes=[pk])
                    P.ts(v32[:, :np_], ps[0:64, :np_], fc[:, li:li + 1], fb[:, li:li + 1], ALU.mult, ALU.add,
                         reads=[pk, "hyk", "fb"], writes=["v32"])
                    P.ts(v32[:, :np_], v32[:, :np_], 1.0 / (2 * math.pi), 16.0, ALU.mult, ALU.add, reads=["v32"], writes=["v32"])
                    P.copy(kint[:, :np_], v32[:, :np_], reads=["v32"], writes=["kint"])
                    P.copy(kf[:, :np_], kint[:, :np_], reads=["kint"], writes=["kf"])
                    P.tt(v32[:, :np_], v32[:, :np_], kf[:, :np_], ALU.subtract, reads=["v32", "kf"], writes=["v32"])
                    P.ts(mg[:, :np_], v32[:, :np_], 0.5, None, ALU.is_gt, reads=["v32"], writes=["mg"])
                    P.tt(v32[:, :np_], v32[:, :np_], mg[:, :np_], ALU.subtract, reads=["v32", "mg"], writes=["v32"])
                    P.act(out[:, pc:pc + np_], v32[:, :np_], AF.Sin, scale=float(2 * math.pi), reads=["v32"],
                          writes=["h1T" if out is h1T else "h2Tb"])

            mlp_layer(w1s, zfT, 64, 0, h1T)
            mlp_layer(w2s, h1T, 64, 1, h2Tb)
            fsum = ar.alloc("fsum", [128, 4, NLT, 512], BF16)
            fdiff = ar.alloc("fdiff", [128, 4, NLT, 512], BF16)
            rn = ar.alloc("rn", [128, 4, 512], F32)
            decb = Rot([ar.alloc("decb", [128, 512], F32) for _ in range(2)], "decb")
            fwb = Rot([ar.alloc("fwb", [128, 512], F32) for _ in range(2)], "fwb")
            bwb = Rot([ar.alloc("bwb", [128, 512], F32) for _ in range(2)], "bwb")
            a1b = Rot([ar.alloc("a1b", [128, 512], BF16) for _ in range(2)], "a1b")
            a2b = Rot([ar.alloc("a2b", [128, 512], BF16) for _ in range(2)], "a2b")
            ps5 = Rot(psf_t[0:5], "psf")
            psN = psf_t[5]
            for cmb in range(4):
                o, cb = cmb // 2, cmb % 2
                for lt in range(NLT):
                    dc, dck = decb.next()
                    P.dma(dc[:], dec_d[lt * 128:(lt + 1) * 128, cb * 512:(cb + 1) * 512], writes=[dck])
                    psF, pkF = ps5.next()
                    psB, pkB = ps5.next()
                    c0 = o * 2 * D + cb * 512
                    P.mm(psF[:], h2Tb[:, lt * 128:(lt + 1) * 128], w3b[:, c0:c0 + 512], reads=["h2Tb", "hyk"], writes=[pkF])
                    P.mm(psB[:], h2Tb[:, lt * 128:(lt + 1) * 128], w3b[:, c0 + D:c0 + D + 512], reads=["h2Tb", "hyk"], writes=[pkB])
                    fw, fwk = fwb.next()
                    bw, bwk = bwb.next()
                    P.tt(fw[:], psF[:], dc[:], ALU.mult, reads=[pkF, dck], writes=[fwk])
                    P.tt(bw[:], psB[:], dc[:], ALU.mult, reads=[pkB, dck], writes=[bwk])
                    if lt == 0:
                        P.memset(bw[0:1, :], 0.0, writes=[bwk])
                    P.tt(fsum[:, cmb, lt, :], fw[:], bw[:], ALU.add, reads=[fwk, bwk], writes=["fsum"], eng="gpsimd")
                    P.tt(fdiff[:, cmb, lt, :], fw[:], bw[:], ALU.subtract, reads=[fwk, bwk], writes=["fdiff"], eng="gpsimd")
                    a1, a1k = a1b.next()
                    a2, a2k = a2b.next()
                    P.act(a1[:], fw[:], AF.Abs, reads=[fwk], writes=[a1k])
                    P.act(a2[:], bw[:], AF.Abs, reads=[bwk], writes=[a2k])
                    P.mm(psN[:], onesb[:], a1[:], start=(lt == 0), stop=False, reads=[a1k, "const"], writes=["psf#5"])
                    P.mm(psN[:], onesb[:], a2[:], start=False, stop=(lt == NLT - 1), reads=[a2k, "const"], writes=["psf#5"])
                P.ts(rn[:, cmb, :], psN[:], EPS, None, ALU.add, reads=["psf#5"], writes=["rn"])
                P.recip(rn[:, cmb, :], rn[:, cmb, :], reads=["rn"], writes=["rn"])
            ftb = Rot([ar.alloc("ftb", [128, NLT, 128], BF16) for _ in range(2)], "ftb")
            hst = Rot([ar.alloc("hst", [128, 512], BF16) for _ in range(3)], "hst")
            for rt in range(NRT):
                ft, ftk = ftb.next()
                P.dma(ft[:], Ft_d[rt], writes=[ftk])
                for cmb in range(4):
                    o, cb = cmb // 2, cmb % 2
                    src, srck = (fsum, "fsum") if rt < NLT else (fdiff, "fdiff")
                    ps, pk = psf.next()
                    for lt in range(NLT):
                        P.mm(ps[:], ft[:, lt, :], src[:, cmb, lt, :], start=(lt == 0), stop=(lt == NLT - 1),
                             reads=[ftk, srck], writes=[pk])
                    hs, hsk = hst.next()
                    P.stt(hs[:], ps[:], arow[:, rt:rt + 1], rn[:, cmb, :], ALU.mult, ALU.mult, reads=[pk, "hyk", "rn"], writes=[hsk])
                    if rt == NLT:
                        ps2, pk2 = psf.next()
                        for lt in range(NLT):
                            P.mm(ps2[0:1, :], ft[:, lt, 0:1], fsum[:, cmb, lt, :], start=(lt == 0), stop=(lt == NLT - 1),
                                 reads=[ftk, "fsum"], writes=[pk2])
                        P.stt(hs[0:1, :], ps2[0:1, :], arow[0:1, rt:rt + 1], rn[0:1, cmb, :], ALU.mult, ALU.mult,
                              reads=[pk2, "hyk", "rn"], writes=[hsk])
                    P.dma(Hs_d[rt * 128:(rt + 1) * 128, o, cb * 512:(cb + 1) * 512], hs[:], reads=[hsk], writes=["Hs"])
            P.barrier()
            ar.reset(mf)
            Yc = ar.alloc("Yc", [128, NRT, D], BF16)
            for o in range(2):
                srcT = hvT if o == 0 else hz1T
                gateT = hx1T if o == 0 else hx2T
                dstT = hz1T if o == 0 else brT
                m2 = ar.mark()
                z = ar.alloc("z", [128, NLT, D], BF16)
                scb = Rot([ar.alloc("scb", [128, KC, TCW], BF16) for _ in range(2)], "scb")
                for tc in range(NTC):
                    sc_, sck = scb.next()
                    P.dma(sc_[:], srcT[:, :, off + tc * TCW:off + (tc + 1) * TCW], writes=[sck])
                    for tt_ in range(TCW // 128):
                        lt = tc * (TCW // 128) + tt_
                        pb, pbk = psb.next()
                        for k in range(KC):
                            P.tr(pb[:, k * 128:(k + 1) * 128], sc_[:, k, tt_ * 128:(tt_ + 1) * 128], idb[:],
                                 reads=[sck, "const"], writes=[pbk])
                        P.copy(z[:, lt, :], pb[:], reads=[pbk], writes=["z"], eng=("vector" if lt % 2 else "scalar"))
                fab = Rot([ar.alloc("fab", [128, NLT, 128], BF16) for _ in range(4)], "fab")
                hab = Rot([ar.alloc("hab", [128, D], BF16) for _ in range(4)], "hab")
                tb = [Rot([ar.alloc("tb%d" % i, [128, 512], F32) for _ in range(2)], "tb%d" % i) for i in range(4)]
                for i in range(NLT):
                    FA, FAk = fab.next()
                    FB, FBk = fab.next()
                    HA, HAk = hab.next()
                    HB, HBk = hab.next()
                    P.dma(FA[:], Ft_d[i], writes=[FAk])
                    P.dma(FB[:], Ft_d[NLT + i], writes=[FBk])
                    P.dma(HA[:], Hs_d[i * 128:(i + 1) * 128, o, :], writes=[HAk])
                    P.dma(HB[:], Hs_d[(NLT + i) * 128:(NLT + i + 1) * 128, o, :], writes=[HBk])
                    for hf_ in range(2):
                        cs_ = slice(hf_ * 512, (hf_ + 1) * 512)
                        psA, pkA = psf.next()
                        psB, pkB = psf.next()
                        for lt in range(NLT):
                            P.mm(psA[:], FA[:, lt, :], z[:, lt, cs_], start=(lt == 0), stop=(lt == NLT - 1),
                                 reads=[FAk, "z"], writes=[pkA])
                        for lt in range(NLT):
                            P.mm(psB[:], FB[:, lt, :], z[:, lt, cs_], start=(lt == 0), stop=(lt == NLT - 1),
                                 reads=[FBk, "z"], writes=[pkB])
                        (t1, t1k), (t2, t2k), (t3, t3k), (t4, t4k) = [r_.next() for r_ in tb]
                        P.tt(t1[:], psA[:], HA[:, cs_], ALU.mult, reads=[pkA, HAk], writes=[t1k])
                        P.tt(t2[:], psB[:], HB[:, cs_], ALU.mult, reads=[pkB, HBk], writes=[t2k])
                        P.tt(t3[:], psA[:], HB[:, cs_], ALU.mult, reads=[pkA, HBk], writes=[t3k])
                        P.tt(t4[:], psB[:], HA[:, cs_], ALU.mult, reads=[pkB, HAk], writes=[t4k])
                        P.tt(Yc[:, i, cs_], t1[:], t2[:], ALU.subtract, reads=[t1k, t2k], writes=["Yc"], eng="gpsimd")
                        P.tt(Yc[:, NLT + i, cs_], t3[:], t4[:], ALU.add, reads=[t3k, t4k], writes=["Yc"], eng="gpsimd")
                        if i == 0:
                            P.tt(Yc[0:1, 0, cs_], psA[0:1, :], HA[0:1, cs_], ALU.mult, reads=[pkA, HAk, "Yc"], writes=["Yc"])
                            P.tt(Yc[0:1, NLT, cs_], psB[0:1, :], HB[0:1, cs_], ALU.mult, reads=[pkB, HBk, "Yc"], writes=["Yc"])
                P.barrier()
                ar.reset(m2)
                ftt = Rot([ar.alloc("ftt", [128, NRT, TCW], BF16) for _ in range(2)], "ftt")
                zib = Rot([ar.alloc("zib", [128, KC, TCW], BF16) for _ in range(2)], "zib")
                xgb = Rot([ar.alloc("xgb", [128, KC, TCW], BF16) for _ in range(2)], "xgb")
                ocb = Rot([ar.alloc("ocb", [128, KC, TCW], BF16) for _ in range(2)], "ocb")
                t5b = Rot([ar.alloc("t5b", [128, TCW], F32) for _ in range(2)], "t5b")
                for tc in range(NTC):
                    ft, ftk = ftt.next()
                    zi_, zik = zib.next()
                    xg, xgk = xgb.next()
                    oc, ock = ocb.next()
                    sl = slice(off + tc * TCW, off + (tc + 1) * TCW)
                    for hh in range(2):
                        P.dma(ft[:, hh * NLT:(hh + 1) * NLT, :], FTt_d[tc, :, hh * NLT:(hh + 1) * NLT, :], writes=[(ftk, hh)])
                    P.dma(zi_[:], srcT[:, :, sl], writes=[zik])
                    P.dma(xg[:], gateT[:, :, sl], writes=[xgk])
                    for j in range(KC):
                        ps, pk = psf.next()
                        for rt in range(NRT):
                            P.mm(ps[:, :TCW], Yc[:, rt, j * 128:(j + 1) * 128], ft[:, rt, :], start=(rt == 0),
                                 stop=(rt == NRT - 1), reads=["Yc", (ftk, rt // NLT)], writes=[pk])
                        t5, t5k = t5b.next()
                        P.stt(t5[:], zi_[:, j, :], hyb[:, l, o, j:j + 1], ps[:, :TCW], ALU.mult, ALU.add,
                              reads=[zik, "const", pk], writes=[t5k])
                        P.tt(oc[:, j, :], t5[:], xg[:, j, :], ALU.mult, reads=[t5k, xgk], writes=[ock], eng="gpsimd")
                    ci = 0 if off == 0 else 1 + tc
                    P.dma(dstT[:, :, sl], oc[:], reads=[ock], writes=[("brT", ci) if o == 1 else "hz1T"])
                P.barrier()
                ar.reset(m2)
            P.barrier()
            ar.reset(m)

        stages = stages or ("adaln", "x0", "n1", "gates", "att", "hgrn", "hy", "wout", "mlp", "final")
        if "adaln" in stages:
            stage_adaln()
        if "x0" in stages:
            stage_x0()
        for l in range(DEPTH):
            lm = ar.mark()
            hT = ar.alloc("hT", [128, KC, TOK], BF16)
            if "n1" in stages:
                stage_norm1(l, hT)
            if "gates" in stages:
                stage_gates(l, hT)
            if "att" in stages:
                am = ar.mark()
                attT = ar.alloc("attT", [128, KC, TOK], BF16)
                stage_attention(l, hT, attT)
                branch_proj(l, 0, lambda ci, s, n: (attT[:, :, s:s + n], [("attT", ci)]), True)
                ar.reset(am)
            if "hy" in stages:
                stage_hy_proj(l, hT)
            if "hgrn" in stages:
                stage_hgrn(l, hT)
                bm = ar.mark()
                branch_proj(l, 1, br_src_loader(), "att" not in stages)
                ar.reset(bm)
            P.barrier()
            ar.reset(lm)
            if "hy" in stages:
                if l == 0:
                    stage_hyena_seq(l, NCTX, 0, tabs["c"], Hs_c)
                stage_hyena_seq(l, NLAT, NCTX, tabs["l"], Hs_l)
                bm = ar.mark()
                branch_proj(l, 2, br_src_loader(), not ("att" in stages or "hgrn" in stages))
                ar.reset(bm)
            if "wout" in stages:
                stage_wout(l)
            if "mlp" in stages:
                stage_mlp(l)
            if stages and "stop_l0" in stages:
                break
        if "final" in stages:
            stage_final()
        if dbg:
            P.barrier()
            for name in dbg:
                if name in scr:
                    a, shp, dt = scr[name]
                    tw = nc.dram_tensor("dbg_" + name, shp, dt, kind="ExternalOutput").ap()
                    P.dma(tw, a, final=True)
        P.emit(sems_eng, sems_dma)
    return nc


def _bf(a):
    return np.asarray(a, dtype=np.float32).astype(ml_dtypes.bfloat16)


def _col(v, k=KC):
    return np.ascontiguousarray(np.asarray(v, np.float32).reshape(k, 128).T)


def make_consts():
    c = {}
    c["c_idf"] = np.eye(128, dtype=np.float32)
    c["c_idb"] = _bf(np.eye(128))
    c["c_onesb"] = _bf(np.ones((128, 128)))
    rmm = np.zeros((128, 128), np.float32)
    for base in (0, 64):
        for i in range(32):
            rmm[base + i + 32, base + i] = -1.0
            rmm[base + i, base + i + 32] = 1.0
    c["c_rm"] = _bf(rmm)
    half = 64
    inv = (10000.0 ** (-np.arange(0, half, 2, dtype=np.float32) / half)).astype(np.float32)
    t = np.arange(NLAT)
    row = (t // 64).astype(np.float32)[:, None] * inv
    colv = (t % 64).astype(np.float32)[:, None] * inv
    ang = np.concatenate([row, row, colv, colv], axis=-1).astype(np.float32)
    mid = 31
    dm = np.zeros((128, 2, 128), np.float32)
    dsel = np.zeros((128, 2, 6), np.float32)
    msk = np.zeros((128, 2, 128), np.float32)
    for cc in range(2):
        for sp in range(64):
            for s_ in range(64):
                dm[cc * 64 + sp, 0, cc * 64 + s_] = float(sp <= s_) - float(sp <= mid)
                dm[cc * 64 + sp, 1, cc * 64 + s_] = float(sp >= s_) - float(sp >= mid)
                msk[cc * 64 + sp, 0, cc * 64 + s_] = float(sp <= s_)
                msk[cc * 64 + sp, 1, cc * 64 + s_] = float(sp >= s_)
            dsel[cc * 64 + sp, 0, cc * 3 + 0] = float(sp <= mid)
            dsel[cc * 64 + sp, 0, cc * 3 + 1] = 1.0
            dsel[cc * 64 + sp, 0, cc * 3 + 2] = float(sp > mid)
            dsel[cc * 64 + sp, 1, cc * 3 + 0] = float(sp >= mid)
            dsel[cc * 64 + sp, 1, cc * 3 + 1] = 1.0
            dsel[cc * 64 + sp, 1, cc * 3 + 2] = float(sp < mid)
    pos = np.arange(128) % 64
    cm = np.zeros((128, 2, 3, 128), np.float32)
    cm[:, 0, 0] = (pos <= mid); cm[:, 0, 1] = (pos > mid); cm[:, 0, 2] = (pos >= mid)
    cm[:, 1, 0] = (pos >= mid); cm[:, 1, 1] = (pos < mid); cm[:, 1, 2] = (pos <= mid)
    c["c_cmk"] = _bf(cm)
    dm2 = np.zeros((128, 2, 128), np.float32)
    for cc in range(2):
        for sp in range(64):
            for s_ in range(64):
                dm2[cc * 64 + sp, 0, cc * 64 + s_] = float(sp > mid) - dm[cc * 64 + sp, 0, cc * 64 + s_]
                dm2[cc * 64 + sp, 1, cc * 64 + s_] = float(sp < mid) - dm[cc * 64 + sp, 1, cc * 64 + s_]
    c["c_dm2"] = dm2
    c["c_dm"] = dm
    c["c_dsel"] = dsel
    c["c_msk"] = np.ascontiguousarray(np.tile(msk, (1, 1, 4)))
    c["c_cos"] = np.ascontiguousarray(np.cos(ang).T.astype(np.float32))
    c["c_sin"] = np.ascontiguousarray(np.sin(ang).T.astype(np.float32))
    return c


def make_hy_tables(n):
    nlt = n // 128
    nrt = 2 * nlt
    tcw = min(n, 512)
    N2 = 2 * n
    t = np.arange(n, dtype=np.int64)[:, None]
    r = np.arange(N2, dtype=np.int64)[None, :]
    f = np.where(r <= n, r, r - n)
    ang = 2.0 * np.pi * ((t * f) % N2).astype(np.float64) / N2
    F = np.where(r <= n, np.cos(ang), np.sin(ang)).astype(np.float32)
    Ft = F.reshape(nlt, 128, nrt, 128).transpose(2, 1, 0, 3)
    FTt = F.reshape(n // tcw, tcw, nrt, 128).transpose(0, 3, 2, 1)
    tt = np.linspace(0.0, 1.0, n, dtype=np.float32)[:, None]
    w = (np.float32(2.0 * math.pi / n) * np.arange(n, dtype=np.float32))[:, None]
    fr = np.linspace(1e-4, 15, 16, dtype=np.float32)[None, :]
    zf = np.concatenate([tt, np.cos(fr * w), -np.sin(fr * w)], axis=-1).astype(np.float32)
    mind, maxd = math.log(1e-2) / 1.5, math.log(1e-2) / 0.3
    deltas = np.abs(np.linspace(mind, maxd, D, dtype=np.float32))
    dec = np.exp(-tt * deltas).astype(np.float32)
    a = np.full(N2, 2.0 / N2, np.float32)
    a[0] = 1.0 / N2
    a[n] = 1.0 / N2
    return {"Ft": _bf(np.ascontiguousarray(Ft)), "FTt": _bf(np.ascontiguousarray(FTt)),
            "zfT": np.ascontiguousarray(zf.T), "dec": dec, "arow": np.ascontiguousarray(a.reshape(nrt, 128).T)}


def make_in_maps(inputs):
    f = lambda k: np.ascontiguousarray(np.asarray(inputs[k], dtype=np.float32))
    shared = dict(make_consts())
    shared["ada_w"] = f("ada_w")
    ab = f("ada_b")
    shared["ada_bc"] = np.ascontiguousarray(np.stack([_col(ab[l], 48) for l in range(DEPTH)], axis=1))
    shared["g1c"] = np.ascontiguousarray(np.stack([_col(f("norm1_g")[l]) for l in range(DEPTH)], axis=1))
    shared["g2c"] = np.ascontiguousarray(np.stack([_col(f("norm2_g")[l]) for l in range(DEPTH)], axis=1))
    shared["w_in"] = f("w_in")
    shared["qkg"] = np.ascontiguousarray(np.stack([f("q_norm_g"), f("k_norm_g")], axis=-1).transpose(1, 0, 2))
    shared["hgn_row"] = np.ascontiguousarray(np.tile(f("hg_norm_g"), (1, 8)))
    shared["hg_lb"] = f("hg_lb_raw")
    for nm, n_ in (("l", NLAT), ("c", NCTX)):
        for k_, v_ in make_hy_tables(n_).items():
            shared[k_ + "_" + nm] = v_
    sw = f("hy_short_w")
    shared["hysw"] = np.ascontiguousarray(sw.reshape(DEPTH, 3, 24, 128).transpose(3, 0, 2, 1))
    shared["hysb"] = np.ascontiguousarray(f("hy_short_b").reshape(DEPTH, 24, 128).transpose(2, 0, 1))
    shared["hyb"] = np.ascontiguousarray(f("hy_bias").reshape(DEPTH, 2, KC, 128).transpose(3, 0, 1, 2))
    shared["hy_w1"] = f("hy_filt_w1")
    shared["hy_w2"] = f("hy_filt_w2")
    shared["hy_w3"] = f("hy_filt_w3")
    fq = f("hy_freq")
    shared["hy_fc"] = np.ascontiguousarray(np.stack([fq[:, 0], fq[:, 1], f("hy_filt_b1"), f("hy_filt_b2")], axis=-1).transpose(1, 0, 2))
    shared["w_branch"] = f("w_branch")
    shared["w_out"] = f("w_out")
    shared["w_mlp1"] = f("w_mlp1")
    shared["w_mlp2"] = f("w_mlp2")
    x, c, ctx, c_ctx = f("x"), f("c"), f("ctx"), f("c_ctx")
    maps = []
    for b in range(8):
        m = dict(shared)
        m["x_b"] = x[b]
        m["ctx_b"] = ctx[b]
        m["ccol"] = np.ascontiguousarray(np.stack([_col(c[b]), _col(c_ctx)], axis=-1))
        maps.append(m)
    return maps


_NC_CACHE = {}


def kernel(**inputs):
    if "nc" not in _NC_CACHE:
        _NC_CACHE["nc"] = build_program()
    nc = _NC_CACHE["nc"]
    maps = make_in_maps(inputs)
    res = run_bass_kernel_spmd(nc, maps, core_ids=list(range(8)))
    return np.stack([np.asarray(r["out"], dtype=np.float32) for r in res.results], axis=0)
```
